# Optimizing a Trainium2 kernel written in Bass

```python
import jax, jax.numpy as jnp
from jax import lax
import numpy as np

D_MODEL = 1024
BATCH = 8
SEQ = 2048
DEPTH = 4
DEC_BATCH = 128
DEC_SEQ = 8
PAST_LEN = 16384
PAGE_SIZE = 128

N_EVEN = (DEPTH + 1) // 2
N_ODD = DEPTH // 2
BRANCH = D_MODEL // 2
W_A = BRANCH
N_HEADS_A = 8
CONV_A = 3
W_B = BRANCH
POOL_WINDOWS = (2, 4, 8, 16)
N_POOL_GROUPS = len(POOL_WINDOWS)
POOL_GROUP = W_B // N_POOL_GROUPS
POOL_HIST = max(POOL_WINDOWS) - 1
W_C = BRANCH
CHUNK = 128
N_SGU_GROUPS = 4
SGU_GROUP = W_C // N_SGU_GROUPS
W_D = BRANCH
N_HEADS_D = 8
CONV_D = 31
IN_EVEN = 4 * W_A + 2 * W_B
IN_ODD = 3 * W_C + 3 * W_D
OUT_EVEN = W_A + W_B
OUT_ODD = W_C + W_D
EPS = 1e-6

kernel_name = "hybrid_conv_pool_sgu_conformer_decode_step"


def rms_norm(x, g):
    xf = x.astype(jnp.float32)
    y = xf * lax.rsqrt(jnp.mean(xf * xf, axis=-1, keepdims=True) + EPS)
    return (y * g.astype(jnp.float32)).astype(x.dtype)


def layer_norm(x, g, b):
    xf = x.astype(jnp.float32)
    mu = jnp.mean(xf, axis=-1, keepdims=True)
    xc = xf - mu
    y = xc * lax.rsqrt(jnp.mean(xc * xc, axis=-1, keepdims=True) + EPS)
    return (y * g.astype(jnp.float32) + b.astype(jnp.float32)).astype(x.dtype)


def causal_depthwise_conv(x, hist, w):
    k = w.shape[0]
    xe = jnp.concatenate([hist.astype(x.dtype), x], axis=1)
    y = lax.conv_general_dilated(
        xe, w[:, None, :].astype(x.dtype), window_strides=(1,), padding="VALID",
        dimension_numbers=("NWC", "WIO", "NWC"), feature_group_count=x.shape[-1])
    return y, xe[:, xe.shape[1] - (k - 1):]


def short_conv_mixer(zb, zc, zh, zg, hist, conv_w):
    s = zc * zh
    y, new_hist = causal_depthwise_conv(s, hist, conv_w)
    return zb * y * jax.nn.silu(zg), new_hist


def multiscale_pool_mixer(p, zg, hist, pool_w, pool_scale, start_pos):
    bsz, t = p.shape[0], p.shape[1]
    pe = jnp.concatenate([hist.astype(p.dtype), p], axis=1)
    pf = pe.astype(jnp.float32)
    cs = jnp.concatenate([jnp.zeros((bsz, 1, W_B), jnp.float32), jnp.cumsum(pf, axis=1)], axis=1)
    pos = start_pos + jnp.arange(t, dtype=jnp.int32)
    end = POOL_HIST + 1
    means = []
    for gi, w in enumerate(POOL_WINDOWS):
        sl = slice(gi * POOL_GROUP, (gi + 1) * POOL_GROUP)
        win_sum = cs[:, end:end + t, sl] - cs[:, end - w:end - w + t, sl]
        cnt = jnp.minimum(pos + 1, w).astype(jnp.float32)[None, :, None]
        means.append(win_sum / cnt)
    d = jnp.concatenate(means, axis=-1) - pf[:, POOL_HIST:]
    d = d.reshape(bsz, t, N_POOL_GROUPS, POOL_GROUP)
    mixed = jnp.einsum("btgc,gcd->btgd", d, pool_w.astype(jnp.float32)).reshape(bsz, t, W_B)
    mixed = mixed * pool_scale.astype(jnp.float32)
    return mixed.astype(p.dtype) * jax.nn.silu(zg), pe[:, pe.shape[1] - POOL_HIST:]


def chunk_sgu_mixer(u, v, zg, ln_g, ln_b, w_s, b_s):
    bsz, t = u.shape[0], u.shape[1]
    lch = min(t, CHUNK)
    n = t // lch
    vn = layer_norm(v, ln_g, ln_b)
    vr = vn.reshape(bsz, n, lch, N_SGU_GROUPS, SGU_GROUP)
    mask = jnp.tril(jnp.ones((lch, lch), dtype=bool))
    wm = jnp.where(mask[None], w_s[:, :lch, :lch], 0.0).astype(v.dtype)
    mixed = jnp.einsum("gij,bnjgc->bnigc", wm, vr) + b_s[:, :lch].T.astype(v.dtype)[None, None, :, :, None]
    return u * mixed.reshape(bsz, t, W_C) * jax.nn.silu(zg), vn


def conformer_conv_mixer(za, zb, zg, hist, dw_w, dw_b, ln_g, ln_b):
    glu = za * jax.nn.sigmoid(zb)
    y, new_hist = causal_depthwise_conv(glu, hist, dw_w)
    y = layer_norm(y + dw_b.astype(y.dtype), ln_g, ln_b)
    return jax.nn.silu(y) * jax.nn.silu(zg), new_hist


def even_layer(x, hist_a, hist_b, start_pos, pre_g, post_g, w_in, w_out, conv_w, pool_w, pool_scale):
    h = rms_norm(x, pre_g)
    z = jnp.einsum("btd,de->bte", h, w_in)
    zb, zc, zh, ga, p, gb = jnp.split(z, 6, axis=-1)
    ya, new_a = short_conv_mixer(zb, zc, zh, ga, hist_a, conv_w)
    yb, new_b = multiscale_pool_mixer(p, gb, hist_b, pool_w, pool_scale, start_pos)
    o = jnp.einsum("bte,ed->btd", jnp.concatenate([ya, yb], axis=-1), w_out)
    return x + rms_norm(o, post_g), new_a, new_b


def odd_layer(x, hist_d, pre_g, post_g, w_in, w_out, ln_g, ln_b, w_s, b_s, dw_w, dw_b, cln_g, cln_b):
    h = rms_norm(x, pre_g)
    z = jnp.einsum("btd,de->bte", h, w_in)
    u, v, gc, za, zb, gd = jnp.split(z, 6, axis=-1)
    yc, vn = chunk_sgu_mixer(u, v, gc, ln_g, ln_b, w_s, b_s)
    yd, new_d = conformer_conv_mixer(za, zb, gd, hist_d, dw_w, dw_b, cln_g, cln_b)
    o = jnp.einsum("bte,ed->btd", jnp.concatenate([yc, yd], axis=-1), w_out)
    return x + rms_norm(o, post_g), new_d, vn


def setup_inputs(seed: int = 0) -> dict:
    key = jax.random.key(seed)
    ks = jax.random.split(key, 24)
    f32 = jnp.float32
    nrm = lambda k, shape, s: jax.random.normal(k, shape, f32) * s
    return {
        "x_prompt": nrm(ks[0], (BATCH, SEQ, D_MODEL), 1.0),
        "x_sample": nrm(ks[1], (DEC_BATCH, DEC_SEQ, D_MODEL), 1.0),
        "state_conv_a": nrm(ks[2], (N_EVEN, DEC_BATCH, CONV_A - 1, W_A), 1.0),
        "state_pool_b": nrm(ks[3], (N_EVEN, DEC_BATCH, POOL_HIST, W_B), 1.0),
        "state_conv_d": nrm(ks[4], (N_ODD, DEC_BATCH, CONV_D - 1, W_D), 1.0),
        "norm_pre": 1.0 + nrm(ks[5], (DEPTH, D_MODEL), 0.05),
        "norm_post": 1.0 + nrm(ks[6], (DEPTH, D_MODEL), 0.05),
        "w_in_even": nrm(ks[7], (N_EVEN, D_MODEL, IN_EVEN), D_MODEL ** -0.5),
        "w_out_even": nrm(ks[8], (N_EVEN, OUT_EVEN, D_MODEL), OUT_EVEN ** -0.5),
        "conv_a_w": nrm(ks[9], (N_EVEN, CONV_A, W_A), CONV_A ** -0.5),
        "pool_w": nrm(ks[10], (N_EVEN, N_POOL_GROUPS, POOL_GROUP, POOL_GROUP), POOL_GROUP ** -0.5),
        "pool_scale": 1.0 + nrm(ks[11], (N_EVEN, W_B), 0.05),
        "w_in_odd": nrm(ks[12], (N_ODD, D_MODEL, IN_ODD), D_MODEL ** -0.5),
        "w_out_odd": nrm(ks[13], (N_ODD, OUT_ODD, D_MODEL), OUT_ODD ** -0.5),
        "sgu_ln_g": 1.0 + nrm(ks[14], (N_ODD, W_C), 0.05),
        "sgu_ln_b": nrm(ks[15], (N_ODD, W_C), 0.02),
        "sgu_w": nrm(ks[16], (N_ODD, N_SGU_GROUPS, CHUNK, CHUNK), 0.02),
        "sgu_b": 1.0 + nrm(ks[17], (N_ODD, N_SGU_GROUPS, CHUNK), 0.02),
        "conf_dw_w": nrm(ks[18], (N_ODD, CONV_D, W_D), CONV_D ** -0.5),
        "conf_dw_b": nrm(ks[19], (N_ODD, W_D), 0.02),
        "conf_ln_g": 1.0 + nrm(ks[20], (N_ODD, W_D), 0.05),
        "conf_ln_b": nrm(ks[21], (N_ODD, W_D), 0.02),
    }


def reference(x_prompt, x_sample, state_conv_a, state_pool_b, state_conv_d,
              norm_pre, norm_post, w_in_even, w_out_even, conv_a_w, pool_w, pool_scale,
              w_in_odd, w_out_odd, sgu_ln_g, sgu_ln_b, sgu_w, sgu_b,
              conf_dw_w, conf_dw_b, conf_ln_g, conf_ln_b):
    xp, xs = x_prompt, x_sample
    bp = xp.shape[0]
    zero_a = jnp.zeros((bp, CONV_A - 1, W_A), xp.dtype)
    zero_b = jnp.zeros((bp, POOL_HIST, W_B), xp.dtype)
    zero_d = jnp.zeros((bp, CONV_D - 1, W_D), xp.dtype)
    ca_p, ca_s, pb_p, pb_s, cd_p, cd_s, v_s = [], [], [], [], [], [], []
    for l in range(DEPTH):
        i = l // 2
        if l % 2 == 0:
            args = (norm_pre[l], norm_post[l], w_in_even[i], w_out_even[i], conv_a_w[i], pool_w[i], pool_scale[i])
            xp, a_p, b_p = even_layer(xp, zero_a, zero_b, 0, *args)
            xs, a_s, b_s = even_layer(xs, state_conv_a[i], state_pool_b[i], PAST_LEN, *args)
            ca_p.append(a_p); ca_s.append(a_s); pb_p.append(b_p); pb_s.append(b_s)
        else:
            args = (norm_pre[l], norm_post[l], w_in_odd[i], w_out_odd[i], sgu_ln_g[i], sgu_ln_b[i],
                    sgu_w[i], sgu_b[i], conf_dw_w[i], conf_dw_b[i], conf_ln_g[i], conf_ln_b[i])
            xp, d_p, _ = odd_layer(xp, zero_d, *args)
            xs, d_s, vn_s = odd_layer(xs, state_conv_d[i], *args)
            cd_p.append(d_p); cd_s.append(d_s); v_s.append(vn_s)
    new_conv_a_prompt = jnp.stack(ca_p)
    new_conv_a_sample = jnp.stack(ca_s)
    new_pool_b_prompt = jnp.stack(pb_p)
    new_pool_b_sample = jnp.stack(pb_s)
    new_conv_d_prompt = jnp.stack(cd_p)
    new_conv_d_sample = jnp.stack(cd_s)
    new_chunk_v_sample = jnp.stack(v_s)
    return (xp, xs, new_conv_a_prompt, new_conv_a_sample, new_pool_b_prompt, new_pool_b_sample,
            new_conv_d_prompt, new_conv_d_sample, new_chunk_v_sample)
```

```python
import contextlib
import numpy as np
import concourse.bass as bass
import concourse.mybir as mybir
from concourse.bass_utils import run_bass_kernel_spmd

F32 = mybir.dt.float32
BF16 = mybir.dt.bfloat16
AF = mybir.ActivationFunctionType
ALU = mybir.AluOpType

NCORES = 8
EPS = 1e-6
NSLOT = 9
TBSZ = (1024, 1040, 1024, 1216, 1088, 1088, 1024, 1024)
NTB = 8
POOL_W = (2, 4, 8, 16)
PIPELINE = 4


class Tk:
    __slots__ = ("name", "w", "rd")

    def __init__(self, name):
        self.name = name
        self.w = None
        self.rd = {}


class Eng:
    def __init__(self, h, sem, name):
        self.h = h
        self.sem = sem
        self.name = name
        self.n = 0
        self.seen = {}


class Sched:
    def __init__(self, nc, es):
        self.nc = nc
        self.es = es
        mk = lambda nm: es.enter_context(nc.semaphore(nm))
        self.pe = Eng(nc.tensor, mk("s_pe"), "pe")
        self.act = Eng(nc.scalar, mk("s_act"), "act")
        self.dve = Eng(nc.vector, mk("s_dve"), "dve")
        self.pool = Eng(nc.gpsimd, mk("s_pool"), "pool")
        self.sp = Eng(nc.sync, mk("s_sp"), "sp")
        self.dsem = {}
        for q, cnt in (("sp", 16), ("pool", 12)):
            self.dsem[q] = [[mk("d_%s%d" % (q, j)), 0] for j in range(cnt)]
        self.drr = {"sp": 0, "pool": 0}
        self.gsem = {}

    def _waits(self, eng, reads, writes):
        deps = {}

        def add(d, same_ok):
            if d is None:
                return
            s, v = d
            if (not same_ok) and s is eng.sem:
                return
            if deps.get(s, (None, 0))[1] < v:
                deps[s] = (s, v)

        for t in reads:
            add(t.w, True)
        for t in writes:
            add(t.w, True)
            for s, v in t.rd.items():
                add((s, v), True)
        for s, v in deps.values():
            if eng.seen.get(s, 0) >= v:
                continue
            eng.h.wait_ge(s, v)
            eng.seen[s] = v

    def _commit(self, me, reads, writes):
        s, v = me
        for t in reads:
            t.rd[s] = v
        for t in writes:
            t.w = me
            t.rd = {}

    def op(self, eng, fn, reads=(), writes=()):
        self._waits(eng, reads, writes)
        ins = fn()
        eng.n += 1
        ins.then_inc(eng.sem, 1)
        self._commit((eng.sem, eng.n), reads, writes)

    def dma(self, q, out_ap, in_ap, reads=(), writes=(), **kw):
        eng = self.sp if q == "sp" else self.pool
        self._waits(eng, reads, writes)
        lst = self.dsem[q]
        j = self.drr[q]
        self.drr[q] = (j + 1) % len(lst)
        ent = lst[j]
        if ent[1] > 0 and eng.seen.get(ent[0], 0) < ent[1]:
            eng.h.wait_ge(ent[0], ent[1])
            eng.seen[ent[0]] = ent[1]
        ins = eng.h.dma_start(out=out_ap, in_=in_ap, **kw)
        ent[1] += 16
        ins.then_inc(ent[0], 16)
        self._commit((ent[0], ent[1]), reads, writes)

    def dma_group(self, q, pairs, reads=(), writes=(), key="g0"):
        eng = self.sp if q == "sp" else self.pool
        self._waits(eng, reads, writes)
        if key not in self.gsem:
            self.gsem[key] = [self.es.enter_context(self.nc.semaphore("dg_" + key)), 0]
        ent = self.gsem[key]
        for (o, i_) in pairs:
            eng.h.dma_start(out=o, in_=i_).then_inc(ent[0], 16)
            ent[1] += 16
        self._commit((ent[0], ent[1]), reads, writes)

    def finish(self):
        for q in ("sp", "pool"):
            for s, v in self.dsem[q]:
                if v > 0:
                    self.nc.sync.wait_ge(s, v)


class Blk:
    def __init__(self, kind, xc, n, nseq, T, tok0=0, sh=0):
        self.kind = kind
        self.xc = xc
        self.n = n
        self.nseq = nseq
        self.T = T
        self.tok0 = tok0
        self.sh = sh
        self.first = (kind == "p" and tok0 == 0)
        self.last = (kind == "p" and tok0 + n == 2048)


def build_program():
    nc = bass.Bass("TRN2", target_bir_lowering=False)
    es = contextlib.ExitStack()

    def din(name, shape):
        return nc.dram_tensor(name, shape, F32, kind="ExternalInput").ap()

    def dout(name, shape):
        return nc.dram_tensor(name, shape, F32, kind="ExternalOutput").ap()

    xp = din("xp", [2048, 1024])
    xs = din("xs", [128, 1024])
    st_a = din("st_a", [2, 16, 2, 512])
    st_b = din("st_b", [2, 16, 15, 512])
    st_d = din("st_d", [2, 16, 30, 512])
    p_gpre = din("p_gpre", [128, 4, 8])
    p_gpost = din("p_gpost", [128, 4, 8])
    p_caw = din("p_caw", [128, 2, 3, 4])
    p_psc = din("p_psc", [128, 2, 4])
    p_cdb = din("p_cdb", [128, 2, 4])
    p_clg = din("p_clg", [128, 2, 4])
    p_clb = din("p_clb", [128, 2, 4])
    p_cdw4 = din("p_cdw4", [128, 2, 8, 16])
    p_wsf = din("p_wsf", [2, 64, 4, 64])
    p_bsrows = din("p_bsrows", [2, 4, 64])
    w_in = [din("w_in_even", [2, 1024, 3072]), din("w_in_odd", [2, 1024, 3072])]
    w_out = [din("w_out_even", [2, 1024, 1024]), din("w_out_odd", [2, 1024, 1024])]
    pool_w = din("pool_w", [2, 4, 128, 128])
    sgu_ln_g = din("sgu_ln_g", [2, 512])
    sgu_ln_b = din("sgu_ln_b", [2, 512])
    sgu_w = din("sgu_w", [2, 4, 128, 128])
    sgu_b = din("sgu_b", [2, 4, 128])
    c_ident = din("c_ident", [128, 128])
    c_tril = din("c_tril", [128, 128])
    c_bmask = din("c_bmask", [64, 64])
    c_invc = din("c_invc", [128, 16])
    c_idrep = din("c_idrep", [128, 32])
    c_sel = din("c_sel", [4, 4, 128])

    y_p = dout("y_p", [2048, 1024])
    y_s = dout("y_s", [128, 1024])
    ca_p = dout("ca_p", [2, 2, 512])
    ca_s = dout("ca_s", [2, 16, 2, 512])
    pb_p = dout("pb_p", [2, 15, 512])
    pb_s = dout("pb_s", [2, 16, 15, 512])
    cd_p = dout("cd_p", [2, 30, 512])
    cd_s = dout("cd_s", [2, 16, 30, 512])
    v_s = dout("v_s", [2, 16, 8, 512])

    S = Sched(nc, es)
    PE, ACT, DVE, POOL = S.pe, S.act, S.dve, S.pool

    def sb(name, shape, dt=F32):
        return es.enter_context(nc.sbuf_tensor(name, shape, dt))

    XW = 1152
    X = sb("X", [128, 8, XW])
    RING = sb("RING", [128, NSLOT, 8, 512], BF16)
    H = sb("H", [128, 2, 8, 256], BF16)
    SQ = sb("SQ", [128, 8, 256], BF16)
    YB = sb("YB", [128, 1, 8, 256], BF16)
    OG = sb("OG", [128, 8, 256])
    MS = sb("MS", [128, 2])
    RSM = sb("RSM", [128, 2])
    DG = sb("DG", [128, 2, 2, 128])
    ONESF2 = sb("ONESF2", [128, 128])
    LST = sb("LST", [128, 4])
    LT = sb("LT", [128, 2])
    LRS = sb("LRS", [128, 2])
    LNMR = sb("LNMR", [128, 2])
    BSHI = sb("BSHI", [4, 128], BF16)
    BSLO = sb("BSLO", [4, 128], BF16)
    BSHIS = sb("BSHIS", [4, 64], BF16)
    BSLOS = sb("BSLOS", [4, 64], BF16)
    BSTMP = sb("BSTMP", [4, 128])
    TB = [sb("TB%d" % j, [128, TBSZ[j]]) if j not in (3, 4) else None for j in range(NTB)]
    TB34 = sb("TB34", [128, TBSZ[3] + TBSZ[4]])
    TB[3] = TB34[:, 0:TBSZ[3]]
    TB[4] = TB34[:, TBSZ[3]:TBSZ[3] + TBSZ[4]]
    BFB = [sb("BFB%d" % j, [128, 1024], BF16) for j in range(3)]
    XEB = sb("XEB", [128, 1248], BF16)
    WP4 = sb("WP4", [128, 8, 16, 32], BF16)
    CDW4 = sb("CDW4", [128, 2, 8, 16])
    IDREP = sb("IDREP", [128, 32])
    IDB = sb("IDB", [128, 128], BF16)
    SEL = sb("SEL", [4, 4, 128], BF16)
    STG = sb("STG", [128, 2, 512])
    PP = es.enter_context(nc.psum_tensor("PP", [128, 2, 1024], F32))
    PMX = es.enter_context(nc.psum_tensor("PMX", [128, 1024], F32))
    PS = es.enter_context(nc.psum_tensor("PS", [128, 2, 512], F32))
    IDENT = sb("IDENT", [128, 128])
    TRIL = sb("TRIL", [128, 128])
    BMASK = sb("BMASK", [64, 64])
    INVC = sb("INVC", [128, 16])
    ONESB = sb("ONESB", [128, 128], BF16)
    MHALF = sb("MHALF", [128, 2])
    GPRE = sb("GPRE", [128, 4, 8])
    GPOST = sb("GPOST", [128, 4, 8])
    CAW = sb("CAW", [128, 2, 3, 4])
    PSC = sb("PSC", [128, 2, 4])
    CDB = sb("CDB", [128, 2, 4])
    CLG = sb("CLG", [128, 2, 4])
    CLB = sb("CLB", [128, 2, 4])
    PW = sb("PW", [128, 4, 128], BF16)
    GB = sb("GB", [128, 512])
    BB = sb("BB", [128, 512])
    WMT = sb("WMT", [128, 2, 4, 128], BF16)
    WMTS = sb("WMTS", [64, 2, 4, 64], BF16)
    WSF = sb("WSF", [64, 4, 64])
    BSROW = sb("BSROW", [4, 128])
    BSROWS = sb("BSROWS", [4, 64])
    SH = sb("SH", [128, 2, 4, 2])
    PH = sb("PH", [128, 2, 4, 15])
    GH = sb("GH", [128, 2, 4, 30])
    BST = sb("BST", [128, 2, 6])
    MV = sb("MV", [128, 2, 2])
    VE = sb("VE", [128, 2])
    RS = sb("RS", [128, 2])
    NMR = sb("NMR", [128, 2])
    FIX = sb("FIX", [128, 16])
    CMP = sb("CMP", [128, 4, 64])
    GST = CMP
    DGA = sb("DGA", [128, 2, 128])

    tk = {}

    def T(name):
        if name not in tk:
            tk[name] = Tk(name)
        return tk[name]

    XT = lambda xc: T("X%d" % xc)
    TBT = [T("TB%d" % j) for j in range(NTB)]
    BFT = [T("BFB%d" % j) for j in range(3)]
    PPT = [T("PP0"), T("PP1")]
    PST = [T("PS0"), T("PS1")]
    HT = [T("H0"), T("H1")]
    YT = [T("Y0"), T("Y1")]
    YTA, YTB = T("YA"), T("YB")
    STGT = [T("STG0"), T("STG1")]
    RINGT = [T("RING%d" % j) for j in range(NSLOT)]
    rr = {"pp": 0, "ps": 0, "h": 0, "y": 0, "stg": 0, "ms": 0}

    def nxt(k, m=2):
        v = rr[k]
        rr[k] = (v + 1) % m
        return v

    def act_copy(out, in_, reads, writes, scale=None, func=None, bias=None):
        def fn():
            kw = {}
            if scale is not None:
                kw["scale"] = scale
            if bias is not None:
                kw["bias"] = bias
            f = func if func is not None else (AF.Copy if (scale is None and bias is None) else AF.Identity)
            return nc.scalar.activation(out=out, in_=in_, func=f, **kw)
        S.op(ACT, fn, reads=reads, writes=writes)

    def dve_tt(out, a, b, op, reads, writes):
        S.op(DVE, lambda: nc.vector.tensor_tensor(out=out, in0=a, in1=b, op=op), reads=reads, writes=writes)

    def dve_stt(out, a, sc, b, op0, op1, reads, writes):
        S.op(DVE, lambda: nc.vector.scalar_tensor_tensor(out=out, in0=a, scalar=sc, in1=b, op0=op0, op1=op1),
             reads=reads, writes=writes)

    def dve_ts(out, a, s1, s2, op0, op1, reads, writes):
        if s2 is None:
            S.op(DVE, lambda: nc.vector.tensor_scalar(out=out, in0=a, scalar1=s1, scalar2=None, op0=op0),
                 reads=reads, writes=writes)
        else:
            S.op(DVE, lambda: nc.vector.tensor_scalar(out=out, in0=a, scalar1=s1, scalar2=s2, op0=op0, op1=op1),
                 reads=reads, writes=writes)

    def dve_copy(out, in_, reads, writes):
        S.op(DVE, lambda: nc.vector.tensor_copy(out=out, in_=in_), reads=reads, writes=writes)

    def pool_pow(out, in_, n, reads, writes):
        S.op(POOL, lambda: nc.gpsimd.tensor_tensor(out=out, in0=in_, in1=MHALF[0:in_.shape[0], 0:n], op=ALU.pow),
             reads=list(reads) + [T("MHALF")], writes=writes)

    def cload(dst_ap, src_ap, name, q="sp", **kw):
        S.dma(q, dst_ap, src_ap, writes=[T(name)], **kw)

    groups = {}
    gcount = [0]

    def group_dma(items, base, q="sp"):
        tiles = [T(base)]
        for k, (dst, src, kw) in enumerate(items):
            gcount[0] += 1
            u = T("%s_u%d" % (base, gcount[0]))
            if k == 0:
                S.dma(q, dst, src, writes=[T(base), u], **kw)
            else:
                S.dma(q, dst, src, reads=[T(base)], writes=[u], **kw)
            tiles.append(u)
        groups[base] = tiles
        return tiles

    NCD = dict(allow_slow_non_contiguous=True)
    cload(IDENT[:, :], c_ident, "IDENT")
    cload(GPRE[:, :, :], p_gpre, "GPRE")
    dve_copy(IDB[:, :], IDENT[:, :], [T("IDENT")], [T("IDB")])
    S.op(DVE, lambda: nc.vector.memset(XEB[:, :], 0.0), writes=[T("XEB")])

    def late_consts():
        cload(GPOST[:, :, :], p_gpost, "GPOST")
        cload(CAW[:, :, :, :], p_caw, "CAW")
        cload(PSC[:, :, :], p_psc, "PSC")
        cload(INVC[:, :], c_invc, "INVC")
        cload(TRIL[:, :], c_tril, "TRIL")
        cload(BMASK[:, :], c_bmask, "BMASK")
        cload(IDREP[:, :], c_idrep, "IDREP")
        cload(SEL[:, :, :], c_sel, "SEL", q="pool")
        cload(CDB[:, :, :], p_cdb, "CDB")
        cload(CLG[:, :, :], p_clg, "CLG")
        cload(CLB[:, :, :], p_clb, "CLB")
        cload(CDW4[:, :, :, :], p_cdw4, "CDW4")
        groups["CDW4"] = [T("CDW4")]

    def load_params(l):
        par, i = l % 2, l // 2
        if par == 0:
            cload(PW[:, :, :], pool_w[i].rearrange("g c d -> c g d"), "PW", q="pool")
        else:
            cload(GB[:, :], sgu_ln_g[i:i + 1, :].partition_broadcast(128)[:, 0, :], "GB")
            cload(BB[:, :], sgu_ln_b[i:i + 1, :].partition_broadcast(128)[:, 0, :], "BB")
            cload(BSROW[:, :], sgu_b[i], "BSROW")
            cload(BSROWS[:, :], p_bsrows[i], "BSROWS")
            bst = [T("BSROWS")]
            for (src, srct, hi, lo, w) in ((BSROW, [T("BSROW")], BSHI, BSLO, 128), (BSROWS, bst, BSHIS, BSLOS, 64)):
                dve_copy(hi[:, :], src[:, :], srct, [T("BSHL")])
                dve_tt(BSTMP[:, 0:w], src[:, :], hi[:, :], ALU.subtract, srct + [T("BSHL")], [T("BSTMP")])
                dve_copy(lo[:, :], BSTMP[:, 0:w], [T("BSTMP")], [T("BSHL")])
            for m_ in range(8):
                S.op(POOL, lambda m_=m_: nc.gpsimd.tensor_tensor(
                    out=WP4[:, m_, :, :], in0=IDREP[:, :].unsqueeze(1).broadcast_to([128, 16, 32]),
                    in1=CDW4[:, i, m_, :].unsqueeze(2).broadcast_to([128, 16, 32]), op=ALU.mult),
                    reads=[T("IDREP")] + groups["CDW4"], writes=[T("WP4")])

    S.op(DVE, lambda: nc.vector.memset(ONESB[:, :], 1.0), writes=[T("ONESB")])
    S.op(DVE, lambda: nc.vector.memset(ONESF2[:, :], 1.0), writes=[T("ONESF2")])
    S.op(DVE, lambda: nc.vector.memset(MHALF[:, :], -0.5), writes=[T("MHALF")])
    S.op(DVE, lambda: nc.vector.memset(SH[:, :, :, :], 0.0), writes=[T("SH0"), T("SH1")])
    S.op(DVE, lambda: nc.vector.memset(PH[:, :, :, :], 0.0), writes=[T("PH0"), T("PH1")])
    S.op(DVE, lambda: nc.vector.memset(GH[:, :, :, :], 0.0), writes=[T("GH0"), T("GH1")])

    chunks = []
    ORDER = {0: [4, 1, 2, 3, 0, 5], 1: [3, 4, 1, 2, 0, 5]}
    for ps in range(2):
        for l in range(4):
            par, i = l % 2, l // 2
            wi = w_in[par][i].rearrange("(k p) f -> p k f", p=128)
            wo = w_out[par][i].rearrange("(k p) f -> p k f", p=128)
            for part in ORDER[par]:
                chunks.append(wi[:, :, part * 512:(part + 1) * 512])
            for hf in range(2):
                chunks.append(wo[:, :, hf * 512:(hf + 1) * 512])
    nchunk_loaded = [0]
    rep_no = [0]

    def load_chunk():
        n = nchunk_loaded[0]
        if n >= len(chunks):
            return
        nchunk_loaded[0] += 1
        slot = n % NSLOT
        S.dma("pool", RING[:, slot, :, :], chunks[n], writes=[RINGT[slot]])

    for _ in range(NSLOT):
        load_chunk()

    def sgu_setup():
        for i in range(2):
            for g in range(4):
                tb = TB[g % 2]
                tbt = TBT[g % 2]
                S.dma("sp", tb[:, 0:128], sgu_w[i, g], writes=[tbt])
                dve_tt(tb[:, 0:128], tb[:, 0:128], TRIL[:, :], ALU.mult, [tbt, T("TRIL")], [tbt])
                pm = nxt("ps")
                S.op(PE, lambda tb=tb, pm=pm: nc.tensor.transpose(out=PS[:, pm, 0:128], in_=tb[:, 0:128], identity=IDENT[:, :]),
                     reads=[tbt, T("IDENT")], writes=[PST[pm]])
                act_copy(WMT[:, i, g, :], PS[:, pm, 0:128], [PST[pm]], [T("WMT")])
            cload(WSF[:, :, :], p_wsf[i], "WSF")
            wt = [T("WSF")]
            for g in range(4):
                dve_tt(WMTS[:, i, g, :], WSF[:, g, :], BMASK[:, :], ALU.mult, wt + [T("BMASK")], [T("WMTS")])

    def load_x_tile(src_rows, xc, j, xts):
        tb, tbt = TB[j % 8], TBT[j % 8]
        S.dma("sp", tb[:, 0:1024], src_rows, writes=[tbt])
        pp = nxt("pp")

        def fn():
            ins = None
            for k in range(8):
                ins = nc.tensor.transpose(out=PP[:, pp, k * 128:(k + 1) * 128], in_=tb[:, k * 128:(k + 1) * 128],
                                          identity=IDENT[:, :])
            return ins
        S.op(PE, fn, reads=[tbt, T("IDENT")], writes=[PPT[pp]])
        src = PP[:, pp, :].rearrange("p (k t) -> p k t", k=8)
        if j % 2 == 0:
            act_copy(X[:, :, xc:xc + 128], src, [PPT[pp]], list(xts))
        else:
            dve_copy(X[:, :, xc:xc + 128], src, [PPT[pp]], list(xts))

    def store_x_tile(dst_rows, xc, j, xts):
        pp = nxt("pp")

        def fn():
            ins = None
            for k in range(8):
                ins = nc.tensor.transpose(out=PP[:, pp, k * 128:(k + 1) * 128], in_=X[:, k, xc:xc + 128],
                                          identity=IDENT[:, :])
            return ins
        S.op(PE, fn, reads=list(xts) + [T("IDENT")], writes=[PPT[pp]])
        tb, tbt = TB[j % 8], TBT[j % 8]
        if j % 2 == 0:
            act_copy(tb[:, 0:1024], PP[:, pp, :], [PPT[pp]], [tbt])
        else:
            dve_copy(tb[:, 0:1024], PP[:, pp, :], [PPT[pp]], [tbt])
        S.dma("sp", dst_rows, tb[:, 0:1024], reads=[tbt])

    def rows_out(src_fn, nrows, dst_ap, reads, src4=None):
        if src4 is not None:
            act_copy(CMP[:, :, 0:nrows].rearrange("p g (s j) -> p g s j", s=src4.shape[2]), src4, list(reads), [T("CMP")])
            src_fn = lambda g: CMP[:, g, 0:nrows]
            reads = [T("CMP")]
        pm = nxt("ps")

        def fn():
            ins = None
            for g in range(4):
                ins = nc.tensor.transpose(out=PS[0:nrows, pm, g * 128:(g + 1) * 128], in_=src_fn(g), identity=IDENT[:, :])
            return ins
        S.op(PE, fn, reads=list(reads) + [T("IDENT")], writes=[PST[pm]])
        sg = nxt("stg")
        act_copy(STG[0:nrows, sg, :], PS[0:nrows, pm, 0:512], [PST[pm]], [STGT[sg]])
        S.dma("sp", dst_ap, STG[0:nrows, sg, :], reads=[STGT[sg]])

    def hist_in(src_ap, nrows, dst_view, nsq, J, wtile):
        sg = nxt("stg")
        S.dma("sp", STG[0:nrows, sg, :], src_ap, writes=[STGT[sg]])
        pm = nxt("ps")

        def fn():
            ins = None
            for g in range(4):
                ins = nc.tensor.transpose(out=PS[:, pm, g * nrows:(g + 1) * nrows], in_=STG[0:nrows, sg, g * 128:(g + 1) * 128],
                                          identity=IDENT[0:nrows, 0:nrows])
            return ins
        S.op(PE, fn, reads=[STGT[sg], T("IDENT")], writes=[PST[pm]])
        act_copy(dst_view, PS[:, pm, 0:4 * nrows].rearrange("p (g s j) -> p g s j", g=4, s=nsq),
                 [PST[pm]], [wtile])

    def bcast_dg(cols, m, ntile, reads):
        for j, col in enumerate(cols):
            for ti in range(ntile):
                S.op(DVE, lambda j=j, ti=ti, col=col: nc.vector.tensor_scalar(
                    out=DG[0:m, j, ti, 0:m], in0=IDENT[0:m, 0:m], scalar1=col[:, ti:ti + 1], scalar2=None, op0=ALU.mult),
                    reads=list(reads) + [T("IDENT")], writes=[T("DG")])

    def bcast_pe(ps, ncols, m, ntile):
        def fn():
            ins = None
            for j in range(ncols):
                for ti in range(ntile):
                    ins = nc.tensor.matmul(PS[:, ps, j * 256 + ti * m:j * 256 + (ti + 1) * m], lhsT=ONESF2[0:m, :],
                                           rhs=DG[0:m, j, ti, 0:m], start=True, stop=True)
            return ins
        S.op(PE, fn, reads=[T("DG"), T("ONESF2")], writes=[PST[ps]])

    def stats_rstd_a(sq_n, reads_sq, n, own=False):
        m = min(n, 128)
        ntile = n // m
        ps = nxt("ps")

        def fn():
            ins = None
            for ti in range(ntile):
                for k in range(8):
                    ins = nc.tensor.matmul(PS[0:m, ps, 508 + ti:509 + ti], lhsT=SQ[:, k, ti * m:(ti + 1) * m], rhs=ONESB[:, 0:1],
                                           start=(k == 0), stop=(k == 7))
            return ins
        S.op(PE, fn, reads=list(reads_sq) + [T("ONESB")], writes=[PST[ps]])
        act_copy(MS[0:m, 0:ntile], PS[0:m, ps, 508:508 + ntile], [PST[ps]], [T("MS")], scale=1.0 / sq_n, bias=EPS)
        S.op(POOL, lambda: nc.gpsimd.tensor_tensor(out=RSM[0:m, 0:ntile], in0=MS[0:m, 0:ntile], in1=MHALF[0:m, 0:ntile], op=ALU.pow),
             reads=[T("MS"), T("MHALF")], writes=[T("RSM")])
        dgv = DGA if own else DG[:, 0, :, :]
        dgt = T("DGA") if own else T("DG")
        for ti in range(ntile):
            S.op(DVE, lambda ti=ti: nc.vector.tensor_scalar(
                out=dgv[0:m, ti, 0:m], in0=IDENT[0:m, 0:m], scalar1=RSM[0:m, ti:ti + 1], scalar2=None, op0=ALU.mult),
                reads=[T("RSM"), T("IDENT")], writes=[dgt])
        return (ps, m, ntile, dgv, dgt)

    def stats_rstd_b(ctx):
        ps, m, ntile, dgv, dgt = ctx
        if dgt is T("DGA"):
            ps = nxt("ps")

        def fn():
            ins = None
            for ti in range(ntile):
                ins = nc.tensor.matmul(PS[:, ps, ti * m:(ti + 1) * m], lhsT=ONESF2[0:m, :], rhs=dgv[0:m, ti, 0:m], start=True, stop=True)
            return ins
        S.op(PE, fn, reads=[dgt, T("ONESF2")], writes=[PST[ps]])
        return ps

    class Step:
        pass

    hookq = []

    def run_hook():
        if hookq:
            hookq.pop(0)()

    def make_step(l, blk, wslots, last_in_layer):
        st = Step()
        par, i = l % 2, l // 2
        n, nseq, Tt = blk.n, blk.nseq, blk.T
        xc = blk.xc
        xt = XT(xc)
        hs_box = [None]
        Yv = YB[:, 0, :, 0:n]
        ys = 0

        abox = {}

        def phase_Aa():
            S.op(ACT, lambda: nc.scalar.activation(out=SQ[:, :, 0:n], in_=X[:, :, xc:xc + n], func=AF.Square),
                 reads=[xt], writes=[T("SQ")])
            abox["ctx"] = stats_rstd_a(1024.0, [T("SQ")], n, own=True)

        def phase_Ab():
            psr = stats_rstd_b(abox["ctx"])
            hs = nxt("h")
            hs_box[0] = hs
            for k in range(8):
                dve_stt(H[:, hs, k, 0:n], X[:, k, xc:xc + n], GPRE[:, l, k:k + 1], PS[:, psr, 0:n], ALU.mult, ALU.mult,
                        [xt, T("GPRE"), PST[psr]], [HT[hs]])

        def phase_A():
            phase_Aa()
            phase_Ab()

        def after_use():
            if last_in_layer:
                load_chunk()

        def inproj_fm(c):
            hs = hs_box[0]
            slot = wslots[c]
            pp = nxt("pp")
            view = PP[:, pp, 0:4 * n].rearrange("p (g t) -> p g t", g=4)

            def fn():
                ins = None
                for g in range(4):
                    for k in range(8):
                        ins = nc.tensor.matmul(view[:, g, :], lhsT=RING[:, slot, k, g * 128:(g + 1) * 128], rhs=H[:, hs, k, 0:n],
                                               start=(k == 0), stop=(k == 7))
                return ins
            S.op(PE, fn, reads=[RINGT[slot], HT[hs]], writes=[PPT[pp]])
            after_use()
            return pp, view

        def v4(ap):
            return ap.rearrange("p g (s t) -> p g s t", s=nseq)

        pmx4 = PMX[:, 0:4 * n].rearrange("p (g t) -> p g t", g=4)
        sl8 = slice(8 * blk.sh, 8 * blk.sh + 8)

        if par == 0:
            E = 15 + Tt
            E2 = 2 + Tt
            PX = TB[3][:, 0:4 * nseq * E].rearrange("p (g s e) -> p g s e", g=4, s=nseq)
            SX = TB[1][:, 0:4 * nseq * E2].rearrange("p (g s e) -> p g s e", g=4, s=nseq)
            Df = BFB[0][:, 0:4 * n].rearrange("p (g t) -> p g t", g=4)
            G2 = TB[6][:, 0:4 * n].rearrange("p (g t) -> p g t", g=4)

            def phase_B1():
                pp, view = inproj_fm(0)
                if blk.kind == "p":
                    dve_copy(PX[:, :, 0, 0:15], PH[:, i, :, :], [T("PH%d" % i)], [TBT[3]])
                else:
                    hist_in(st_b[i, sl8].rearrange("s j c -> (s j) c"), 120, PX[:, :, :, 0:15], 8, 15, TBT[3])
                    S.dma("sp", pb_s[i, sl8, 0:7, :], st_b[i, sl8, 8:15, :])
                act_copy(PX[:, :, :, 15:E], v4(view), [PPT[pp]], [TBT[3]])
                if blk.kind == "p" and not blk.last:
                    dve_copy(PH[:, i, :, :], PX[:, :, 0, Tt:Tt + 15], [TBT[3]], [T("PH%d" % i)])
                if blk.last:
                    rows_out(lambda g: PX[:, g, 0, Tt:Tt + 15], 15, pb_p[i], [TBT[3]])
                if blk.kind == "s":
                    rows_out(None, 64, pb_s[i, sl8, 7:15, :], [TBT[3]], src4=PX[:, :, :, 15:E])
                A1 = TB[4][:, 0:4 * nseq * E].rearrange("p (g s e) -> p g s e", g=4, s=nseq)
                A2 = TB[5][:, 0:4 * nseq * E].rearrange("p (g s e) -> p g s e", g=4, s=nseq)
                dve_tt(A1[:, :, :, 1:E], PX[:, :, :, 1:E], PX[:, :, :, 0:E - 1], ALU.add, [TBT[3]], [TBT[4]])
                dve_tt(A2[:, 1:4, :, 3:E], A1[:, 1:4, :, 3:E], A1[:, 1:4, :, 1:E - 2], ALU.add, [TBT[4]], [TBT[5]])
                dve_tt(A1[:, 2:4, :, 7:E], A2[:, 2:4, :, 7:E], A2[:, 2:4, :, 3:E - 4], ALU.add, [TBT[5]], [TBT[4]])
                dve_tt(A2[:, 3, :, 15:E], A1[:, 3, :, 15:E], A1[:, 3, :, 7:E - 8], ALU.add, [TBT[4]], [TBT[5]])
                WIN = [A1[:, 0], A2[:, 1], A1[:, 2], A2[:, 3]]
                wint = [TBT[4], TBT[5], TBT[4], TBT[5]]
                Dv = BFB[0][:, 0:4 * n].rearrange("p (g s t) -> p g s t", g=4, s=nseq)
                for g in range(4):
                    dve_stt(Dv[:, g], WIN[g][:, :, 15:E], 1.0 / POOL_W[g], PX[:, g, :, 15:E], ALU.mult, ALU.subtract,
                            [wint[g], TBT[3]], [BFT[0]])
                if blk.first:
                    for g in range(4):
                        w = POOL_W[g]
                        dve_tt(FIX[:, 0:w - 1], WIN[g][:, 0, 15:15 + w - 1], INVC[:, 0:w - 1], ALU.mult,
                               [wint[g], T("INVC")], [T("FIX")])
                        dve_tt(Dv[:, g, 0, 0:w - 1], FIX[:, 0:w - 1], PX[:, g, 0, 15:15 + w - 1], ALU.subtract,
                               [T("FIX"), TBT[3]], [BFT[0]])
                run_hook()
                pp, view = inproj_fm(1)
                ZC = TB[0][:, 0:4 * n].rearrange("p (g t) -> p g t", g=4)
                act_copy(ZC, view, [PPT[pp]], [TBT[0]])
                if blk.kind == "p":
                    dve_copy(SX[:, :, 0, 0:2], SH[:, i, :, :], [T("SH%d" % i)], [TBT[1]])
                else:
                    hist_in(st_a[i, sl8].rearrange("s j c -> (s j) c"), 16, SX[:, :, :, 0:2], 8, 2, TBT[1])
                pp, view = inproj_fm(2)
                dve_tt(SX[:, :, :, 2:E2], v4(ZC), v4(view), ALU.mult, [TBT[0], PPT[pp]], [TBT[1]])
                if blk.kind == "p" and not blk.last:
                    dve_copy(SH[:, i, :, :], SX[:, :, 0, Tt:Tt + 2], [TBT[1]], [T("SH%d" % i)])
                if blk.last:
                    rows_out(lambda g: SX[:, g, 0, Tt:Tt + 2], 2, ca_p[i], [TBT[1]])
                if blk.kind == "s":
                    rows_out(None, 16, ca_s[i, sl8].rearrange("s j c -> (s j) c"), [TBT[1]], src4=SX[:, :, :, Tt:Tt + 2])
                run_hook()

            def phase_B2():
                Y3 = TB[0][:, 0:4 * n].rearrange("p (g s t) -> p g s t", g=4, s=nseq)
                y3t = [T("Y3_%d" % g) for g in range(4)]
                for k in range(3):
                    for g in range(4):
                        if k == 0:
                            dve_ts(Y3[:, g], SX[:, g, :, 0:Tt], CAW[:, i, 0, g:g + 1], None, ALU.mult, None,
                                   [TBT[1], T("CAW")], [TBT[0], y3t[g]] if g == 0 else [y3t[g]])
                        else:
                            dve_stt(Y3[:, g], SX[:, g, :, k:k + Tt], CAW[:, i, k, g:g + 1], Y3[:, g], ALU.mult, ALU.add,
                                    [TBT[1], T("CAW"), y3t[g]], [y3t[g]])
                pp, view = inproj_fm(3)
                G = TB[2][:, 0:4 * n].rearrange("p (g t) -> p g t", g=4)
                act_copy(G, view, [PPT[pp]], [TBT[2]], func=AF.Silu)
                pp, view = inproj_fm(4)
                dve_tt(G, view, G, ALU.mult, [PPT[pp], TBT[2]], [TBT[2]])
                dve_tt(v4(Yv[:, 0:4, :]), Y3, v4(G), ALU.mult, [TBT[2]] + y3t, [YTA, TBT[0]])
                pp, view = inproj_fm(5)
                act_copy(G2, view, [PPT[pp]], [TBT[6]], func=AF.Silu)

                def fnm():
                    ins = None
                    for g in range(4):
                        ins = nc.tensor.matmul(pmx4[:, g, :], lhsT=PW[:, g, :], rhs=Df[:, g, :], start=True, stop=True)
                    return ins
                S.op(PE, fnm, reads=[BFT[0], T("PW")], writes=[T("PMX")])
                for g in range(4):
                    dve_stt(Yv[:, 4 + g, :], pmx4[:, g, :], PSC[:, i, g:g + 1], G2[:, g, :], ALU.mult, ALU.mult,
                            [T("PMX"), T("PSC"), TBT[6]], [YTB])
                run_hook()
        else:
            m = 128 if blk.kind == "p" else 64
            ntile = n // m
            EL = 31 + Tt
            RL = 28 + Tt
            XE = XEB[:, 0:4 * nseq * EL].rearrange("p (g s e) -> p g s e", g=4, s=nseq)
            xet = T("XEB")
            RV = TB34[:, :].bitcast(BF16).rearrange("p (q s e) -> p q s e", q=16, s=nseq)
            YN = TB[5][:, 0:4 * n].rearrange("p (g t) -> p g t", g=4)
            YBF = BFB[1][:, 0:4 * n].rearrange("p (g t) -> p g t", g=4)
            YSQ = BFB[2][:, 0:4 * n].rearrange("p (g t) -> p g t", g=4)
            VN = BFB[0][:, 0:1024].rearrange("p (t c) -> p t c", t=2)
            G = TB[2][:, 0:4 * n].rearrange("p (g t) -> p g t", g=4)
            mL = min(n, 128)
            ntL = n // mL
            box = {}

            def phase_B1():
                ppa, viewa = inproj_fm(0)
                ZA = TB[0][:, 0:4 * n].rearrange("p (g t) -> p g t", g=4)
                act_copy(ZA, viewa, [PPT[ppa]], [TBT[0]], scale=0.5)
                run_hook()
                ppb, viewb = inproj_fm(1)
                TH = TB[6][:, 0:4 * n].rearrange("p (g t) -> p g t", g=4)
                act_copy(TH, viewb, [PPT[ppb]], [TBT[6]], scale=0.5, func=AF.Tanh)
                if blk.kind == "p":
                    dve_copy(XE[:, :, 0, 0:30], GH[:, i, :, :], [T("GH%d" % i)], [xet])
                else:
                    for q in range(2):
                        sl = slice(8 * blk.sh + 4 * q, 8 * blk.sh + 4 * q + 4)
                        hist_in(st_d[i, sl].rearrange("s j c -> (s j) c"), 120, XE[:, :, 4 * q:4 * q + 4, 0:30], 4, 30, xet)
                    S.dma("sp", cd_s[i, sl8, 0:22, :], st_d[i, sl8, 8:30, :])
                dve_stt(XE[:, :, :, 30:30 + Tt], v4(TH), 1.0, v4(ZA), ALU.add, ALU.mult, [TBT[6], TBT[0]], [xet])
                if blk.kind == "p":
                    dve_stt(GH[:, i, :, :], TH[:, :, Tt - 30:Tt], 1.0, ZA[:, :, Tt - 30:Tt], ALU.add, ALU.mult,
                            [TBT[6], TBT[0]], [T("GH%d" % i)])
                    if blk.last:
                        rows_out(lambda g: GH[:, i, g, :], 30, cd_p[i], [T("GH%d" % i)])
                else:
                    dve_stt(GST[:, :, :], TH, 1.0, ZA, ALU.add, ALU.mult, [TBT[6], TBT[0]], [T("CMP")])
                    rows_out(lambda g: GST[:, g, :], 64, cd_s[i, sl8, 22:30, :], [T("CMP")])
                rts = [T("R%d" % z) for z in range(8)]
                for z in range(8):
                    if z % 3 == 2:
                        ptile, pview = T("PMX"), PMX[:, :]
                    else:
                        pp = nxt("pp")
                        ptile, pview = PPT[pp], PP[:, pp, :]

                    def fnrep(z=z, pview=pview):
                        ins = None
                        for dq in range(2):
                            q = 2 * z + dq
                            J, j = q // 4, q % 4
                            for s_ in range(4):
                                rhs = XE[:, J, 0, s_:s_ + RL] if nseq == 1 else XE[:, J, :, s_:s_ + RL]
                                ins = nc.tensor.matmul(pview[32 * s_:32 * s_ + 32, dq * 512:dq * 512 + nseq * RL],
                                                       lhsT=IDB[:, 32 * j:32 * j + 32], rhs=rhs, start=True, stop=True,
                                                       tile_position=(0, 32 * s_))
                        return ins
                    S.op(PE, fnrep, reads=[xet, T("IDB")], writes=[ptile])
                    src = pview.rearrange("p (a c) -> p a c", a=2)[:, :, 0:nseq * RL].rearrange("p a (s e) -> p a s e", s=nseq)
                    dst = RV[:, 2 * z:2 * z + 2, :, 0:RL]
                    wr = [TBT[3], TBT[4], rts[z]] if z == 0 else [rts[z]]
                    rdx = [ptile] if z == 0 else [ptile, TBT[3]]
                    if z % 3 != 1:
                        act_copy(dst, src, rdx, wr)
                    else:
                        dve_copy(dst, src, rdx, wr)

                def fnconv():
                    ins = None
                    for J in range(4):
                        for m_ in range(8):
                            for j in range(4):
                                if nseq == 1:
                                    rhs = RV[:, 4 * J + j, 0, 4 * m_:4 * m_ + Tt]
                                else:
                                    rhs = RV[:, 4 * J + j, :, 4 * m_:4 * m_ + Tt]
                                ins = nc.tensor.matmul(pmx4[32 * j:32 * j + 32, J, :], lhsT=WP4[:, m_, 4 * J + j, :], rhs=rhs,
                                                       start=(m_ == 0), stop=(m_ == 7), tile_position=(0, 32 * j))
                    return ins
                S.op(PE, fnconv, reads=rts + [T("WP4")], writes=[T("PMX"), TBT[3], TBT[4]])
                run_hook()
                for g in range(4):
                    act_copy(YN[:, g, :], pmx4[:, g, :], [T("PMX"), T("CDB")], [TBT[5]], bias=CDB[:, i, g:g + 1])
                act_copy(YBF, YN, [TBT[5]], [BFT[1]])
                act_copy(YSQ, YN, [TBT[5]], [BFT[2]], func=AF.Square)
                slot = wslots[2]
                hs = hs_box[0]
                pp = nxt("pp")

                def fnv():
                    ins = None
                    for ti in range(ntile):
                        for k in range(8):
                            ins = nc.tensor.matmul(PP[0:m, pp, ti * 512:(ti + 1) * 512], lhsT=H[:, hs, k, ti * m:(ti + 1) * m],
                                                   rhs=RING[:, slot, k, :], start=(k == 0), stop=(k == 7))
                    return ins
                S.op(PE, fnv, reads=[RINGT[slot], HT[hs]], writes=[PPT[pp]])
                after_use()
                for ti in range(ntile):
                    S.op(DVE, lambda ti=ti: nc.vector.bn_stats(out=BST[0:m, ti, :], in_=PP[0:m, pp, ti * 512:(ti + 1) * 512]),
                         reads=[PPT[pp]], writes=[T("BST%d" % ti)])
                    S.op(DVE, lambda ti=ti: nc.vector.bn_aggr(out=MV[0:m, ti, :], in_=BST[0:m, ti, :]),
                         reads=[T("BST%d" % ti)], writes=[T("MV%d" % ti)])
                mvt = [T("MV%d" % ti) for ti in range(ntile)]
                S.op(POOL, lambda: nc.gpsimd.tensor_scalar(out=VE[0:m, 0:ntile], in0=MV[0:m, 0:ntile, 1], scalar1=EPS, scalar2=None,
                                                            op0=ALU.add), reads=mvt, writes=[T("VE")])
                S.op(POOL, lambda: nc.gpsimd.tensor_tensor(out=RS[0:m, 0:ntile], in0=VE[0:m, 0:ntile], in1=MHALF[0:m, 0:ntile],
                                                            op=ALU.pow), reads=[T("VE"), T("MHALF")], writes=[T("RS")])
                dve_stt(NMR[0:m, 0:ntile], MV[0:m, 0:ntile, 0], -1.0, RS[0:m, 0:ntile], ALU.mult, ALU.mult,
                        mvt + [T("RS")], [T("NMR")])
                NT = TB[1][:, 0:1024].rearrange("p (t c) -> p t c", t=2)
                for ti in range(ntile):
                    act_copy(NT[0:m, ti, :], PP[0:m, pp, ti * 512:(ti + 1) * 512], [PPT[pp], T("RS"), T("NMR")], [TBT[1]],
                             scale=RS[0:m, ti:ti + 1], bias=NMR[0:m, ti:ti + 1])
                gbb = GB[0:m, :].unsqueeze(1).broadcast_to([m, ntile, 512])
                bbb = BB[0:m, :].unsqueeze(1).broadcast_to([m, ntile, 512])
                dve_tt(NT[0:m, 0:ntile, :], NT[0:m, 0:ntile, :], gbb, ALU.mult, [TBT[1], T("GB")], [TBT[1]])
                if blk.kind == "p":
                    dve_tt(VN[0:m, 0:ntile, :], NT[0:m, 0:ntile, :], bbb, ALU.add, [TBT[1], T("BB")], [BFT[0]])
                else:
                    dve_tt(NT[0:m, 0:1, :], NT[0:m, 0:1, :], bbb, ALU.add, [TBT[1], T("BB")], [TBT[1]])
                    S.dma("sp", v_s[i, sl8].rearrange("s t c -> (s t) c"), NT[0:m, 0, :], reads=[TBT[1]])
                    act_copy(VN[0:m, 0:1, :], NT[0:m, 0:1, :], [TBT[1]], [BFT[0]])
                psa = nxt("ps")
                box["psa"] = psa

                def fnln():
                    ins = None
                    for ti in range(ntL):
                        for j, src in enumerate((YBF, YSQ)):
                            for g in range(4):
                                ins = nc.tensor.matmul(PS[0:mL, psa, 2 * j + ti:2 * j + ti + 1], lhsT=src[:, g, ti * mL:(ti + 1) * mL],
                                                       rhs=ONESB[:, 0:1], start=(g == 0), stop=(g == 3))
                    return ins
                S.op(PE, fnln, reads=[BFT[1], BFT[2], T("ONESB")], writes=[PST[psa]])
                act_copy(LST[0:mL, 0:4], PS[0:mL, psa, 0:4], [PST[psa]], [T("LST")], scale=1.0 / 512)
                dve_tt(LT[0:mL, 0:ntL], LST[0:mL, 0:ntL], LST[0:mL, 0:ntL], ALU.mult, [T("LST")], [T("LT")])
                dve_tt(LT[0:mL, 0:ntL], LST[0:mL, 2:2 + ntL], LT[0:mL, 0:ntL], ALU.subtract, [T("LST"), T("LT")], [T("LT")])
                S.op(POOL, lambda: nc.gpsimd.tensor_scalar(out=LT[0:mL, 0:ntL], in0=LT[0:mL, 0:ntL], scalar1=EPS, scalar2=None, op0=ALU.add),
                     reads=[T("LT")], writes=[T("LT")])
                S.op(POOL, lambda: nc.gpsimd.tensor_tensor(out=LRS[0:mL, 0:ntL], in0=LT[0:mL, 0:ntL], in1=MHALF[0:mL, 0:ntL], op=ALU.pow),
                     reads=[T("LT"), T("MHALF")], writes=[T("LRS")])
                dve_stt(LNMR[0:mL, 0:ntL], LST[0:mL, 0:ntL], -1.0, LRS[0:mL, 0:ntL], ALU.mult, ALU.mult,
                        [T("LST"), T("LRS")], [T("LNMR")])
                bcast_dg([LRS[0:mL, 0:ntL], LNMR[0:mL, 0:ntL]], mL, ntL, [T("LRS"), T("LNMR")])

            def phase_B2():
                ppg, viewg = inproj_fm(3)
                act_copy(G, viewg, [PPT[ppg]], [TBT[2]], func=AF.Silu)
                ppu, viewu = inproj_fm(4)
                dve_tt(G, viewu, G, ALU.mult, [PPT[ppu], TBT[2]], [TBT[2]])
                psb = nxt("ps")
                bcast_pe(psb, 2, mL, ntL)
                def fnmix():
                    ins = None
                    for g in range(4):
                        for ti in range(ntile):
                            o = pmx4[:, g, ti * m:(ti + 1) * m]
                            if blk.kind == "p":
                                nc.tensor.matmul(o, lhsT=VN[0:m, ti, g * 128:(g + 1) * 128], rhs=WMT[:, i, g, :], start=True, stop=False)
                                nc.tensor.matmul(o, lhsT=SEL[:, g, :], rhs=BSHI[:, :], start=False, stop=False)
                                ins = nc.tensor.matmul(o, lhsT=SEL[:, g, :], rhs=BSLO[:, :], start=False, stop=True)
                            else:
                                nc.tensor.matmul(o, lhsT=VN[0:m, ti, g * 128:(g + 1) * 128], rhs=WMTS[:, i, g, :], start=True, stop=False)
                                nc.tensor.matmul(o, lhsT=SEL[:, g, :], rhs=BSHIS[:, :], start=False, stop=False)
                                ins = nc.tensor.matmul(o, lhsT=SEL[:, g, :], rhs=BSLOS[:, :], start=False, stop=True)
                    return ins
                S.op(PE, fnmix, reads=[BFT[0], T("WMT"), T("WMTS"), T("SEL"), T("BSHL")], writes=[T("PMX")])
                dve_tt(Yv[:, 0:4, :], pmx4, G, ALU.mult, [T("PMX"), TBT[2]], [YTA])
                rb = PS[:, psb, 0:n].unsqueeze(1).broadcast_to([128, 4, n])
                mb = PS[:, psb, 256:256 + n].unsqueeze(1).broadcast_to([128, 4, n])
                t5h = [T("T5a"), T("T5b")]
                rbh = PS[:, psb, 0:n].unsqueeze(1).broadcast_to([128, 2, n])
                mbh = PS[:, psb, 256:256 + n].unsqueeze(1).broadcast_to([128, 2, n])
                for hh in range(2):
                    gs = slice(2 * hh, 2 * hh + 2)
                    dve_tt(YN[:, gs, :], YN[:, gs, :], rbh, ALU.mult, [TBT[5], PST[psb]], [t5h[hh]])
                    dve_tt(YN[:, gs, :], YN[:, gs, :], mbh, ALU.add, [t5h[hh], PST[psb]], [t5h[hh]])
                    for g in (2 * hh, 2 * hh + 1):
                        act_copy(YN[:, g, :], YN[:, g, :], [t5h[hh], T("CLG"), T("CLB")], [t5h[hh]], func=AF.Silu,
                                 scale=CLG[:, i, g:g + 1], bias=CLB[:, i, g:g + 1])
                ppd, viewd = inproj_fm(5)
                SGD = TB[7][:, 0:4 * n].rearrange("p (g t) -> p g t", g=4)
                act_copy(SGD, viewd, [PPT[ppd]], [TBT[7]], func=AF.Silu)
                dve_tt(Yv[:, 4:6, :], YN[:, 0:2, :], SGD[:, 0:2, :], ALU.mult, [t5h[0], TBT[7]], [T("YBh0")])
                dve_tt(Yv[:, 6:8, :], YN[:, 2:4, :], SGD[:, 2:4, :], ALU.mult, [t5h[1], TBT[7]], [YTB, TBT[5]])
                run_hook()

        def phase_C1():
            slots, views = [], []
            for hf in range(2):
                pp = nxt("pp")
                slots.append(pp)
                views.append(PP[:, pp, 0:4 * n].rearrange("p (g t) -> p g t", g=4))
            for hf in range(2):
                slot, view, pp = wslots[6 + hf], views[hf], slots[hf]

                def fno1(slot=slot, view=view):
                    ins = None
                    for g in range(4):
                        for k in range(4):
                            first = (k == 0) and ((g * n) % 512 == 0)
                            ins = nc.tensor.matmul(view[:, g, :], lhsT=RING[:, slot, k, g * 128:(g + 1) * 128], rhs=YB[:, ys, k, 0:n],
                                                   start=first, stop=False, skip_group_check=True)
                    return ins
                S.op(PE, fno1, reads=[RINGT[slot], YTA], writes=[PPT[pp]])
            for hf in range(2):
                slot, view, pp = wslots[6 + hf], views[hf], slots[hf]

                def fno2(slot=slot, view=view):
                    ins = None
                    for g in range(4):
                        for k in range(4, 8):
                            ins = nc.tensor.matmul(view[:, g, :], lhsT=RING[:, slot, k, g * 128:(g + 1) * 128], rhs=YB[:, ys, k, 0:n],
                                                   start=False, stop=(k == 7), skip_group_check=True)
                    return ins
                S.op(PE, fno2, reads=[RINGT[slot], YTB, T("YBh0"), PPT[pp]], writes=[PPT[pp]])
                after_use()
                act_copy(SQ[:, hf * 4:hf * 4 + 4, 0:n], view, [PPT[pp]], [T("SQ")], func=AF.Square)
                for g in range(4):
                    kk = hf * 4 + g
                    act_copy(OG[:, kk, 0:n], view[:, g, :], [PPT[pp], T("GPOST")], [T("OG")], scale=GPOST[:, l, kk:kk + 1])

        cbox = {}

        def phase_C2a():
            cbox["ctx"] = stats_rstd_a(1024.0, [T("SQ")], n)

        def phase_C2b():
            psr2 = stats_rstd_b(cbox["ctx"])
            rb2 = PS[:, psr2, 0:n].unsqueeze(1).broadcast_to([128, 8, n])
            dve_tt(OG[:, :, 0:n], OG[:, :, 0:n], rb2, ALU.mult, [T("OG"), PST[psr2]], [T("OG")])
            dve_tt(X[:, :, xc:xc + n], X[:, :, xc:xc + n], OG[:, :, 0:n], ALU.add, [xt, T("OG")], [xt])

        def phase_C2():
            phase_C2a()
            phase_C2b()

        st.Aa, st.Ab, st.C2a, st.C2b = phase_Aa, phase_Ab, phase_C2a, phase_C2b
        st.A, st.B1, st.B2, st.C1, st.C2 = phase_A, phase_B1, phase_B2, phase_C1, phase_C2
        return st

    passes = [
        [Blk("s", 1024, 64, 8, 8, sh=0), Blk("s", 1088, 64, 8, 8, sh=1)] +
        [Blk("p", 256 * b, 256, 1, 256, tok0=256 * b) for b in range(4)],
        [Blk("p", 256 * b, 256, 1, 256, tok0=1024 + 256 * b) for b in range(4)],
    ]
    chunk_no = 0
    for ps_, blocks in enumerate(passes):
        j = 0
        if ps_ == 0:
            for tt in range(8):
                load_x_tile(xp[tt * 128:(tt + 1) * 128, :], tt * 128, j, [XT((tt // 2) * 256)])
                j += 1
            load_x_tile(xs[:, :], 1024, j, [XT(1024), XT(1088)])
            j += 1
            late_consts()
            load_params(0)
        else:
            for tt in range(8):
                load_x_tile(xp[1024 + tt * 128:1024 + (tt + 1) * 128, :], tt * 128, j, [XT((tt // 2) * 256)])
                j += 1
        steps = []
        for l in range(4):
            wslots = [(chunk_no + c) % NSLOT for c in range(8)]
            for bi, blk in enumerate(blocks):
                stp = make_step(l, blk, wslots, bi == len(blocks) - 1)
                stp.pre = (lambda l=l: load_params((l + 1) % 4)) if (bi == 0 and not (ps_ == 1 and l == 3)) else None
                if ps_ == 0 and l == 0:
                    stp.pre = (lambda: (sgu_setup(), load_params(1))) if bi == 2 else None
                steps.append(stp)
            chunk_no += 8
        if PIPELINE == 4:
            steps[0].A()
            for si, stp in enumerate(steps):
                if stp.pre is not None:
                    stp.pre()
                prev = steps[si - 1] if si > 0 else None
                nx = steps[si + 1] if si + 1 < len(steps) else None
                if prev is not None:
                    hookq.append(prev.C2a)
                    if nx is not None:
                        hookq.append(lambda prev=prev, nx=nx: (prev.C2b(), nx.Aa()))
                        hookq.append(nx.Ab)
                    else:
                        hookq.append(prev.C2b)
                elif nx is not None:
                    hookq.append(nx.Aa)
                    hookq.append(nx.Ab)
                stp.B1()
                stp.B2()
                while hookq:
                    hookq.pop(0)()
                stp.C1()
            steps[-1].C2()
        elif PIPELINE == 2:
            steps[0].A()
            for si, stp in enumerate(steps):
                if stp.pre is not None:
                    stp.pre()
                stp.B1()
                if si > 0:
                    steps[si - 1].C2()
                if si + 1 < len(steps):
                    steps[si + 1].A()
                stp.B2()
                stp.C1()
            steps[-1].C2()
        elif PIPELINE == 3:
            for si, stp in enumerate(steps):
                if stp.pre is not None:
                    stp.pre()
                stp.A()
                stp.B1()
                if si > 0:
                    steps[si - 1].C2()
                stp.B2()
                stp.C1()
            steps[-1].C2()
        elif PIPELINE == 1:
            steps[0].A()
            for si, stp in enumerate(steps):
                if stp.pre is not None:
                    stp.pre()
                stp.B1()
                if si + 1 < len(steps):
                    steps[si + 1].A()
                stp.B2()
                stp.C1()
                stp.C2()
        else:
            for si, stp in enumerate(steps):
                if stp.pre is not None:
                    stp.pre()
                stp.A()
                stp.B1()
                stp.B2()
                stp.C1()
                stp.C2()
        j = 0
        if ps_ == 0:
            for tt in range(8):
                store_x_tile(y_p[tt * 128:(tt + 1) * 128, :], tt * 128, j, [XT((tt // 2) * 256)])
                j += 1
            store_x_tile(y_s[:, :], 1024, j, [XT(1024), XT(1088)])
        else:
            for tt in range(8):
                store_x_tile(y_p[1024 + tt * 128:1024 + (tt + 1) * 128, :], tt * 128, j, [XT((tt // 2) * 256)])
                j += 1
    print('SBUF bytes remaining per partition:', nc.sbuf_bytes_remaining)
    S.finish()
    es.close()
    return nc


_CACHE = {}


def kernel(x_prompt, x_sample, state_conv_a, state_pool_b, state_conv_d,
           norm_pre, norm_post, w_in_even, w_out_even, conv_a_w, pool_w, pool_scale,
           w_in_odd, w_out_odd, sgu_ln_g, sgu_ln_b, sgu_w, sgu_b,
           conf_dw_w, conf_dw_b, conf_ln_g, conf_ln_b):
    f = lambda a: np.ascontiguousarray(np.asarray(a, dtype=np.float32))
    x_prompt, x_sample = f(x_prompt), f(x_sample)
    state_conv_a, state_pool_b, state_conv_d = f(state_conv_a), f(state_pool_b), f(state_conv_d)
    shared = dict(
        norm_pre=f(norm_pre), norm_post=f(norm_post), w_in_even=f(w_in_even), w_out_even=f(w_out_even),
        conv_a_w=f(conv_a_w), pool_w=f(pool_w), pool_scale=f(pool_scale), w_in_odd=f(w_in_odd),
        w_out_odd=f(w_out_odd), sgu_ln_g=f(sgu_ln_g), sgu_ln_b=f(sgu_ln_b), sgu_w=f(sgu_w), sgu_b=f(sgu_b),
        conf_dw_w=f(conf_dw_w), conf_dw_b=f(conf_dw_b), conf_ln_g=f(conf_ln_g), conf_ln_b=f(conf_ln_b),
    )
    shared["c_ident"] = np.eye(128, dtype=np.float32)
    shared["c_tril"] = np.tril(np.ones((128, 128), np.float32))
    jj = np.arange(64)
    shared["c_bmask"] = ((jj[:, None] // 8 == jj[None, :] // 8) & (jj[:, None] % 8 <= jj[None, :] % 8)).astype(np.float32)
    npre, npost = shared.pop("norm_pre"), shared.pop("norm_post")
    caw, psc = shared.pop("conv_a_w"), shared.pop("pool_scale")
    cdw, cdb = shared.pop("conf_dw_w"), shared.pop("conf_dw_b")
    clg, clb = shared.pop("conf_ln_g"), shared.pop("conf_ln_b")
    c_ = np.ascontiguousarray
    shared["p_gpre"] = c_(npre.reshape(4, 8, 128).transpose(2, 0, 1))
    shared["p_gpost"] = c_(npost.reshape(4, 8, 128).transpose(2, 0, 1))
    shared["p_caw"] = c_(caw.reshape(2, 3, 4, 128).transpose(3, 0, 1, 2))
    shared["p_psc"] = c_(psc.reshape(2, 4, 128).transpose(2, 0, 1))
    shared["p_cdb"] = c_(cdb.reshape(2, 4, 128).transpose(2, 0, 1))
    shared["p_clg"] = c_(clg.reshape(2, 4, 128).transpose(2, 0, 1))
    shared["p_clb"] = c_(clb.reshape(2, 4, 128).transpose(2, 0, 1))
    wpad = np.zeros((2, 32, 512), np.float32)
    wpad[:, 0:31, :] = cdw
    shared["p_cdw4"] = c_(wpad.reshape(2, 8, 4, 16, 32).transpose(2, 4, 0, 1, 3).reshape(128, 2, 8, 16))
    sw = shared["sgu_w"]
    wsf = np.zeros((2, 64, 4, 64), np.float32)
    for s_ in range(8):
        wsf[:, 8 * s_:8 * s_ + 8, :, 8 * s_:8 * s_ + 8] = sw[:, :, 0:8, 0:8].transpose(0, 3, 1, 2)
    shared["p_wsf"] = wsf
    shared["p_bsrows"] = c_(np.tile(shared["sgu_b"][:, :, 0:8], (1, 1, 8)))
    shared["c_idrep"] = np.tile(np.eye(32, dtype=np.float32), (4, 1))
    sel = np.zeros((4, 4, 128), np.float32)
    for g in range(4):
        sel[g, g, :] = 1.0
    shared["c_sel"] = sel
    shared["c_invc"] = np.tile((1.0 / np.arange(1, 17, dtype=np.float32))[None, :], (128, 1)).astype(np.float32)
    in_maps = []
    for c in range(NCORES):
        m = dict(shared)
        m["xp"] = x_prompt[c]
        m["xs"] = np.ascontiguousarray(x_sample[16 * c:16 * c + 16].reshape(128, 1024))
        m["st_a"] = np.ascontiguousarray(state_conv_a[:, 16 * c:16 * c + 16])
        m["st_b"] = np.ascontiguousarray(state_pool_b[:, 16 * c:16 * c + 16])
        m["st_d"] = np.ascontiguousarray(state_conv_d[:, 16 * c:16 * c + 16])
        in_maps.append(m)
    if "nc" not in _CACHE:
        _CACHE["nc"] = build_program()
    res = run_bass_kernel_spmd(_CACHE["nc"], in_maps, core_ids=list(range(NCORES)))
    R = res.results
    y_prompt = np.stack([R[c]["y_p"] for c in range(NCORES)], 0)
    y_sample = np.concatenate([R[c]["y_s"].reshape(16, 8, 1024) for c in range(NCORES)], 0)
    ca_p = np.stack([R[c]["ca_p"] for c in range(NCORES)], 1)
    ca_s = np.concatenate([R[c]["ca_s"] for c in range(NCORES)], 1)
    pb_p = np.stack([R[c]["pb_p"] for c in range(NCORES)], 1)
    pb_s = np.concatenate([R[c]["pb_s"] for c in range(NCORES)], 1)
    cd_p = np.stack([R[c]["cd_p"] for c in range(NCORES)], 1)
    cd_s = np.concatenate([R[c]["cd_s"] for c in range(NCORES)], 1)
    v_s = np.concatenate([R[c]["v_s"] for c in range(NCORES)], 1)
    return (y_prompt, y_sample, ca_p, ca_s, pb_p, pb_s, cd_p, cd_s, v_s)
```

```python
import contextlib
import numpy as np
import concourse.bass as bass
import concourse.mybir as mybir
from concourse.bass_utils import run_bass_kernel_spmd

F32 = mybir.dt.float32
BF16 = mybir.dt.bfloat16
AF = mybir.ActivationFunctionType
ALU = mybir.AluOpType

NCORES = 8
EPS = 1e-6
NSLOT = 9
TBSZ = (1024, 1040, 1024, 1216, 1088, 1088, 1024, 1024)
NTB = 8
POOL_W = (2, 4, 8, 16)
PIPELINE = 4


class Tk:
    __slots__ = ("name", "w", "rd")

    def __init__(self, name):
        self.name = name
        self.w = None
        self.rd = {}


class Eng:
    def __init__(self, h, sem, name):
        self.h = h
        self.sem = sem
        self.name = name
        self.n = 0
        self.seen = {}


class Sched:
    def __init__(self, nc, es):
        self.nc = nc
        self.es = es
        mk = lambda nm: es.enter_context(nc.semaphore(nm))
        self.pe = Eng(nc.tensor, mk("s_pe"), "pe")
        self.act = Eng(nc.scalar, mk("s_act"), "act")
        self.dve = Eng(nc.vector, mk("s_dve"), "dve")
        self.pool = Eng(nc.gpsimd, mk("s_pool"), "pool")
        self.sp = Eng(nc.sync, mk("s_sp"), "sp")
        self.dsem = {}
        for q, cnt in (("sp", 16), ("pool", 12)):
            self.dsem[q] = [[mk("d_%s%d" % (q, j)), 0] for j in range(cnt)]
        self.drr = {"sp": 0, "pool": 0}
        self.gsem = {}

    def _waits(self, eng, reads, writes):
        deps = {}

        def add(d, same_ok):
            if d is None:
                return
            s, v = d
            if (not same_ok) and s is eng.sem:
                return
            if deps.get(s, (None, 0))[1] < v:
                deps[s] = (s, v)

        for t in reads:
            add(t.w, True)
        for t in writes:
            add(t.w, True)
            for s, v in t.rd.items():
                add((s, v), True)
        for s, v in deps.values():
            if eng.seen.get(s, 0) >= v:
                continue
            eng.h.wait_ge(s, v)
            eng.seen[s] = v

    def _commit(self, me, reads, writes):
        s, v = me
        for t in reads:
            t.rd[s] = v
        for t in writes:
            t.w = me
            t.rd = {}

    def op(self, eng, fn, reads=(), writes=()):
        self._waits(eng, reads, writes)
        ins = fn()
        eng.n += 1
        ins.then_inc(eng.sem, 1)
        self._commit((eng.sem, eng.n), reads, writes)

    def dma(self, q, out_ap, in_ap, reads=(), writes=(), **kw):
        eng = self.sp if q == "sp" else self.pool
        self._waits(eng, reads, writes)
        lst = self.dsem[q]
        j = self.drr[q]
        self.drr[q] = (j + 1) % len(lst)
        ent = lst[j]
        if ent[1] > 0 and eng.seen.get(ent[0], 0) < ent[1]:
            eng.h.wait_ge(ent[0], ent[1])
            eng.seen[ent[0]] = ent[1]
        ins = eng.h.dma_start(out=out_ap, in_=in_ap, **kw)
        ent[1] += 16
        ins.then_inc(ent[0], 16)
        self._commit((ent[0], ent[1]), reads, writes)

    def dma_group(self, q, pairs, reads=(), writes=(), key="g0"):
        eng = self.sp if q == "sp" else self.pool
        self._waits(eng, reads, writes)
        if key not in self.gsem:
            self.gsem[key] = [self.es.enter_context(self.nc.semaphore("dg_" + key)), 0]
        ent = self.gsem[key]
        for (o, i_) in pairs:
            eng.h.dma_start(out=o, in_=i_).then_inc(ent[0], 16)
            ent[1] += 16
        self._commit((ent[0], ent[1]), reads, writes)

    def finish(self):
        for q in ("sp", "pool"):
            for s, v in self.dsem[q]:
                if v > 0:
                    self.nc.sync.wait_ge(s, v)


class Blk:
    def __init__(self, kind, xc, n, nseq, T, tok0=0, sh=0):
        self.kind = kind
        self.xc = xc
        self.n = n
        self.nseq = nseq
        self.T = T
        self.tok0 = tok0
        self.sh = sh
        self.first = (kind == "p" and tok0 == 0)
        self.last = (kind == "p" and tok0 + n == 2048)


def build_program():
    nc = bass.Bass("TRN2", target_bir_lowering=False)
    es = contextlib.ExitStack()

    def din(name, shape):
        return nc.dram_tensor(name, shape, F32, kind="ExternalInput").ap()

    def dout(name, shape):
        return nc.dram_tensor(name, shape, F32, kind="ExternalOutput").ap()

    xp = din("xp", [2048, 1024])
    xs = din("xs", [128, 1024])
    st_a = din("st_a", [2, 16, 2, 512])
    st_b = din("st_b", [2, 16, 15, 512])
    st_d = din("st_d", [2, 16, 30, 512])
    p_gpre = din("p_gpre", [128, 4, 8])
    p_gpost = din("p_gpost", [128, 4, 8])
    p_caw = din("p_caw", [128, 2, 3, 4])
    p_psc = din("p_psc", [128, 2, 4])
    p_cdb = din("p_cdb", [128, 2, 4])
    p_clg = din("p_clg", [128, 2, 4])
    p_clb = din("p_clb", [128, 2, 4])
    p_cdw4 = din("p_cdw4", [128, 2, 8, 16])
    p_wsf = din("p_wsf", [2, 64, 4, 64])
    p_bsrows = din("p_bsrows", [2, 4, 64])
    w_in = [din("w_in_even", [2, 1024, 3072]), din("w_in_odd", [2, 1024, 3072])]
    w_out = [din("w_out_even", [2, 1024, 1024]), din("w_out_odd", [2, 1024, 1024])]
    pool_w = din("pool_w", [2, 4, 128, 128])
    sgu_ln_g = din("sgu_ln_g", [2, 512])
    sgu_ln_b = din("sgu_ln_b", [2, 512])
    sgu_w = din("sgu_w", [2, 4, 128, 128])
    sgu_b = din("sgu_b", [2, 4, 128])
    c_ident = din("c_ident", [128, 128])
    c_tril = din("c_tril", [128, 128])
    c_bmask = din("c_bmask", [64, 64])
    c_invc = din("c_invc", [128, 16])
    c_idrep = din("c_idrep", [128, 32])
    c_sel = din("c_sel", [4, 4, 128])

    y_p = dout("y_p", [2048, 1024])
    y_s = dout("y_s", [128, 1024])
    ca_p = dout("ca_p", [2, 2, 512])
    ca_s = dout("ca_s", [2, 16, 2, 512])
    pb_p = dout("pb_p", [2, 15, 512])
    pb_s = dout("pb_s", [2, 16, 15, 512])
    cd_p = dout("cd_p", [2, 30, 512])
    cd_s = dout("cd_s", [2, 16, 30, 512])
    v_s = dout("v_s", [2, 16, 8, 512])

    S = Sched(nc, es)
    PE, ACT, DVE, POOL = S.pe, S.act, S.dve, S.pool

    def sb(name, shape, dt=F32):
        return es.enter_context(nc.sbuf_tensor(name, shape, dt))

    XW = 1152
    X = sb("X", [128, 8, XW])
    RING = sb("RING", [128, NSLOT, 8, 512], BF16)
    H = sb("H", [128, 2, 8, 256], BF16)
    SQ = sb("SQ", [128, 8, 256], BF16)
    YB = sb("YB", [128, 1, 8, 256], BF16)
    OG = sb("OG", [128, 8, 256])
    MS = sb("MS", [128, 2])
    RSM = sb("RSM", [128, 2])
    DG = sb("DG", [128, 2, 2, 128])
    ONESF2 = sb("ONESF2", [128, 128])
    LST = sb("LST", [128, 4])
    LT = sb("LT", [128, 2])
    LRS = sb("LRS", [128, 2])
    LNMR = sb("LNMR", [128, 2])
    BSHI = sb("BSHI", [4, 128], BF16)
    BSLO = sb("BSLO", [4, 128], BF16)
    BSHIS = sb("BSHIS", [4, 64], BF16)
    BSLOS = sb("BSLOS", [4, 64], BF16)
    BSTMP = sb("BSTMP", [4, 128])
    TB = [sb("TB%d" % j, [128, TBSZ[j]]) if j not in (3, 4) else None for j in range(NTB)]
    TB34 = sb("TB34", [128, TBSZ[3] + TBSZ[4]])
    TB[3] = TB34[:, 0:TBSZ[3]]
    TB[4] = TB34[:, TBSZ[3]:TBSZ[3] + TBSZ[4]]
    BFB = [sb("BFB%d" % j, [128, 1024], BF16) for j in range(3)]
    XEB = sb("XEB", [128, 1248], BF16)
    WP4 = sb("WP4", [128, 8, 16, 32], BF16)
    CDW4 = sb("CDW4", [128, 2, 8, 16])
    IDREP = sb("IDREP", [128, 32])
    IDB = sb("IDB", [128, 128], BF16)
    SEL = sb("SEL", [4, 4, 128], BF16)
    STG = sb("STG", [128, 2, 512])
    PP = es.enter_context(nc.psum_tensor("PP", [128, 2, 1024], F32))
    PMX = es.enter_context(nc.psum_tensor("PMX", [128, 1024], F32))
    PS = es.enter_context(nc.psum_tensor("PS", [128, 2, 512], F32))
    IDENT = sb("IDENT", [128, 128])
    TRIL = sb("TRIL", [128, 128])
    BMASK = sb("BMASK", [64, 64])
    INVC = sb("INVC", [128, 16])
    ONESB = sb("ONESB", [128, 128], BF16)
    MHALF = sb("MHALF", [128, 2])
    GPRE = sb("GPRE", [128, 4, 8])
    GPOST = sb("GPOST", [128, 4, 8])
    CAW = sb("CAW", [128, 2, 3, 4])
    PSC = sb("PSC", [128, 2, 4])
    CDB = sb("CDB", [128, 2, 4])
    CLG = sb("CLG", [128, 2, 4])
    CLB = sb("CLB", [128, 2, 4])
    PW = sb("PW", [128, 4, 128], BF16)
    GB = sb("GB", [128, 512])
    BB = sb("BB", [128, 512])
    WMT = sb("WMT", [128, 2, 4, 128], BF16)
    WMTS = sb("WMTS", [64, 2, 4, 64], BF16)
    WSF = sb("WSF", [64, 4, 64])
    BSROW = sb("BSROW", [4, 128])
    BSROWS = sb("BSROWS", [4, 64])
    SH = sb("SH", [128, 2, 4, 2])
    PH = sb("PH", [128, 2, 4, 15])
    GH = sb("GH", [128, 2, 4, 30])
    BST = sb("BST", [128, 2, 6])
    MV = sb("MV", [128, 2, 2])
    VE = sb("VE", [128, 2])
    RS = sb("RS", [128, 2])
    NMR = sb("NMR", [128, 2])
    FIX = sb("FIX", [128, 16])
    CMP = sb("CMP", [128, 4, 64])
    GST = CMP
    DGA = sb("DGA", [128, 2, 128])

    tk = {}

    def T(name):
        if name not in tk:
            tk[name] = Tk(name)
        return tk[name]

    XT = lambda xc: T("X%d" % xc)
    TBT = [T("TB%d" % j) for j in range(NTB)]
    BFT = [T("BFB%d" % j) for j in range(3)]
    PPT = [T("PP0"), T("PP1")]
    PST = [T("PS0"), T("PS1")]
    HT = [T("H0"), T("H1")]
    YT = [T("Y0"), T("Y1")]
    YTA, YTB = T("YA"), T("YB")
    STGT = [T("STG0"), T("STG1")]
    RINGT = [T("RING%d" % j) for j in range(NSLOT)]
    rr = {"pp": 0, "ps": 0, "h": 0, "y": 0, "stg": 0, "ms": 0}

    def nxt(k, m=2):
        v = rr[k]
        rr[k] = (v + 1) % m
        return v

    def act_copy(out, in_, reads, writes, scale=None, func=None, bias=None):
        def fn():
            kw = {}
            if scale is not None:
                kw["scale"] = scale
            if bias is not None:
                kw["bias"] = bias
            f = func if func is not None else (AF.Copy if (scale is None and bias is None) else AF.Identity)
            return nc.scalar.activation(out=out, in_=in_, func=f, **kw)
        S.op(ACT, fn, reads=reads, writes=writes)

    def dve_tt(out, a, b, op, reads, writes):
        S.op(DVE, lambda: nc.vector.tensor_tensor(out=out, in0=a, in1=b, op=op), reads=reads, writes=writes)

    def dve_stt(out, a, sc, b, op0, op1, reads, writes):
        S.op(DVE, lambda: nc.vector.scalar_tensor_tensor(out=out, in0=a, scalar=sc, in1=b, op0=op0, op1=op1),
             reads=reads, writes=writes)

    def dve_ts(out, a, s1, s2, op0, op1, reads, writes):
        if s2 is None:
            S.op(DVE, lambda: nc.vector.tensor_scalar(out=out, in0=a, scalar1=s1, scalar2=None, op0=op0),
                 reads=reads, writes=writes)
        else:
            S.op(DVE, lambda: nc.vector.tensor_scalar(out=out, in0=a, scalar1=s1, scalar2=s2, op0=op0, op1=op1),
                 reads=reads, writes=writes)

    def dve_copy(out, in_, reads, writes):
        S.op(DVE, lambda: nc.vector.tensor_copy(out=out, in_=in_), reads=reads, writes=writes)

    def pool_pow(out, in_, n, reads, writes):
        S.op(POOL, lambda: nc.gpsimd.tensor_tensor(out=out, in0=in_, in1=MHALF[0:in_.shape[0], 0:n], op=ALU.pow),
             reads=list(reads) + [T("MHALF")], writes=writes)

    def cload(dst_ap, src_ap, name, q="sp", **kw):
        S.dma(q, dst_ap, src_ap, writes=[T(name)], **kw)

    groups = {}
    gcount = [0]

    def group_dma(items, base, q="sp"):
        tiles = [T(base)]
        for k, (dst, src, kw) in enumerate(items):
            gcount[0] += 1
            u = T("%s_u%d" % (base, gcount[0]))
            if k == 0:
                S.dma(q, dst, src, writes=[T(base), u], **kw)
            else:
                S.dma(q, dst, src, reads=[T(base)], writes=[u], **kw)
            tiles.append(u)
        groups[base] = tiles
        return tiles

    NCD = dict(allow_slow_non_contiguous=True)
    cload(IDENT[:, :], c_ident, "IDENT")
    cload(GPRE[:, :, :], p_gpre, "GPRE")
    dve_copy(IDB[:, :], IDENT[:, :], [T("IDENT")], [T("IDB")])
    S.op(DVE, lambda: nc.vector.memset(XEB[:, :], 0.0), writes=[T("XEB")])

    def late_consts():
        cload(GPOST[:, :, :], p_gpost, "GPOST")
        cload(CAW[:, :, :, :], p_caw, "CAW")
        cload(PSC[:, :, :], p_psc, "PSC")
        cload(INVC[:, :], c_invc, "INVC")
        cload(TRIL[:, :], c_tril, "TRIL")
        cload(BMASK[:, :], c_bmask, "BMASK")
        cload(IDREP[:, :], c_idrep, "IDREP")
        cload(SEL[:, :, :], c_sel, "SEL", q="pool")
        cload(CDB[:, :, :], p_cdb, "CDB")
        cload(CLG[:, :, :], p_clg, "CLG")
        cload(CLB[:, :, :], p_clb, "CLB")
        cload(CDW4[:, :, :, :], p_cdw4, "CDW4")
        groups["CDW4"] = [T("CDW4")]

    def load_params(l):
        par, i = l % 2, l // 2
        if par == 0:
            cload(PW[:, :, :], pool_w[i].rearrange("g c d -> c g d"), "PW", q="pool")
        else:
            cload(GB[:, :], sgu_ln_g[i:i + 1, :].partition_broadcast(128)[:, 0, :], "GB")
            cload(BB[:, :], sgu_ln_b[i:i + 1, :].partition_broadcast(128)[:, 0, :], "BB")
            cload(BSROW[:, :], sgu_b[i], "BSROW")
            cload(BSROWS[:, :], p_bsrows[i], "BSROWS")
            bst = [T("BSROWS")]
            for (src, srct, hi, lo, w) in ((BSROW, [T("BSROW")], BSHI, BSLO, 128), (BSROWS, bst, BSHIS, BSLOS, 64)):
                dve_copy(hi[:, :], src[:, :], srct, [T("BSHL")])
                dve_tt(BSTMP[:, 0:w], src[:, :], hi[:, :], ALU.subtract, srct + [T("BSHL")], [T("BSTMP")])
                dve_copy(lo[:, :], BSTMP[:, 0:w], [T("BSTMP")], [T("BSHL")])
            for m_ in range(8):
                S.op(POOL, lambda m_=m_: nc.gpsimd.tensor_tensor(
                    out=WP4[:, m_, :, :], in0=IDREP[:, :].unsqueeze(1).broadcast_to([128, 16, 32]),
                    in1=CDW4[:, i, m_, :].unsqueeze(2).broadcast_to([128, 16, 32]), op=ALU.mult),
                    reads=[T("IDREP")] + groups["CDW4"], writes=[T("WP4")])

    S.op(DVE, lambda: nc.vector.memset(ONESB[:, :], 1.0), writes=[T("ONESB")])
    S.op(DVE, lambda: nc.vector.memset(ONESF2[:, :], 1.0), writes=[T("ONESF2")])
    S.op(DVE, lambda: nc.vector.memset(MHALF[:, :], -0.5), writes=[T("MHALF")])
    S.op(DVE, lambda: nc.vector.memset(SH[:, :, :, :], 0.0), writes=[T("SH0"), T("SH1")])
    S.op(DVE, lambda: nc.vector.memset(PH[:, :, :, :], 0.0), writes=[T("PH0"), T("PH1")])
    S.op(DVE, lambda: nc.vector.memset(GH[:, :, :, :], 0.0), writes=[T("GH0"), T("GH1")])

    chunks = []
    ORDER = {0: [4, 1, 2, 3, 0, 5], 1: [3, 4, 1, 2, 0, 5]}
    for ps in range(2):
        for l in range(4):
            par, i = l % 2, l // 2
            wi = w_in[par][i].rearrange("(k p) f -> p k f", p=128)
            wo = w_out[par][i].rearrange("(k p) f -> p k f", p=128)
            for part in ORDER[par]:
                chunks.append(wi[:, :, part * 512:(part + 1) * 512])
            for hf in range(2):
                chunks.append(wo[:, :, hf * 512:(hf + 1) * 512])
    nchunk_loaded = [0]
    rep_no = [0]

    def load_chunk():
        n = nchunk_loaded[0]
        if n >= len(chunks):
            return
        nchunk_loaded[0] += 1
        slot = n % NSLOT
        S.dma("pool", RING[:, slot, :, :], chunks[n], writes=[RINGT[slot]])

    for _ in range(NSLOT):
        load_chunk()

    def sgu_setup():
        for i in range(2):
            for g in range(4):
                tb = TB[g % 2]
                tbt = TBT[g % 2]
                S.dma("sp", tb[:, 0:128], sgu_w[i, g], writes=[tbt])
                dve_tt(tb[:, 0:128], tb[:, 0:128], TRIL[:, :], ALU.mult, [tbt, T("TRIL")], [tbt])
                pm = nxt("ps")
                S.op(PE, lambda tb=tb, pm=pm: nc.tensor.transpose(out=PS[:, pm, 0:128], in_=tb[:, 0:128], identity=IDENT[:, :]),
                     reads=[tbt, T("IDENT")], writes=[PST[pm]])
                act_copy(WMT[:, i, g, :], PS[:, pm, 0:128], [PST[pm]], [T("WMT")])
            cload(WSF[:, :, :], p_wsf[i], "WSF")
            wt = [T("WSF")]
            for g in range(4):
                dve_tt(WMTS[:, i, g, :], WSF[:, g, :], BMASK[:, :], ALU.mult, wt + [T("BMASK")], [T("WMTS")])

    def load_x_tile(src_rows, xc, j, xts):
        tb, tbt = TB[j % 8], TBT[j % 8]
        S.dma("sp", tb[:, 0:1024], src_rows, writes=[tbt])
        pp = nxt("pp")

        def fn():
            ins = None
            for k in range(8):
                ins = nc.tensor.transpose(out=PP[:, pp, k * 128:(k + 1) * 128], in_=tb[:, k * 128:(k + 1) * 128],
                                          identity=IDENT[:, :])
            return ins
        S.op(PE, fn, reads=[tbt, T("IDENT")], writes=[PPT[pp]])
        src = PP[:, pp, :].rearrange("p (k t) -> p k t", k=8)
        if j % 2 == 0:
            act_copy(X[:, :, xc:xc + 128], src, [PPT[pp]], list(xts))
        else:
            dve_copy(X[:, :, xc:xc + 128], src, [PPT[pp]], list(xts))

    def store_x_tile(dst_rows, xc, j, xts):
        pp = nxt("pp")

        def fn():
            ins = None
            for k in range(8):
                ins = nc.tensor.transpose(out=PP[:, pp, k * 128:(k + 1) * 128], in_=X[:, k, xc:xc + 128],
                                          identity=IDENT[:, :])
            return ins
        S.op(PE, fn, reads=list(xts) + [T("IDENT")], writes=[PPT[pp]])
        tb, tbt = TB[j % 8], TBT[j % 8]
        if j % 2 == 0:
            act_copy(tb[:, 0:1024], PP[:, pp, :], [PPT[pp]], [tbt])
        else:
            dve_copy(tb[:, 0:1024], PP[:, pp, :], [PPT[pp]], [tbt])
        S.dma("sp", dst_rows, tb[:, 0:1024], reads=[tbt])

    def rows_out(src_fn, nrows, dst_ap, reads, src4=None):
        if src4 is not None:
            act_copy(CMP[:, :, 0:nrows].rearrange("p g (s j) -> p g s j", s=src4.shape[2]), src4, list(reads), [T("CMP")])
            src_fn = lambda g: CMP[:, g, 0:nrows]
            reads = [T("CMP")]
        pm = nxt("ps")

        def fn():
            ins = None
            for g in range(4):
                ins = nc.tensor.transpose(out=PS[0:nrows, pm, g * 128:(g + 1) * 128], in_=src_fn(g), identity=IDENT[:, :])
            return ins
        S.op(PE, fn, reads=list(reads) + [T("IDENT")], writes=[PST[pm]])
        sg = nxt("stg")
        act_copy(STG[0:nrows, sg, :], PS[0:nrows, pm, 0:512], [PST[pm]], [STGT[sg]])
        S.dma("sp", dst_ap, STG[0:nrows, sg, :], reads=[STGT[sg]])

    def hist_in(src_ap, nrows, dst_view, nsq, J, wtile):
        sg = nxt("stg")
        S.dma("sp", STG[0:nrows, sg, :], src_ap, writes=[STGT[sg]])
        pm = nxt("ps")

        def fn():
            ins = None
            for g in range(4):
                ins = nc.tensor.transpose(out=PS[:, pm, g * nrows:(g + 1) * nrows], in_=STG[0:nrows, sg, g * 128:(g + 1) * 128],
                                          identity=IDENT[0:nrows, 0:nrows])
            return ins
        S.op(PE, fn, reads=[STGT[sg], T("IDENT")], writes=[PST[pm]])
        act_copy(dst_view, PS[:, pm, 0:4 * nrows].rearrange("p (g s j) -> p g s j", g=4, s=nsq),
                 [PST[pm]], [wtile])

    def bcast_dg(cols, m, ntile, reads):
        for j, col in enumerate(cols):
            for ti in range(ntile):
                S.op(DVE, lambda j=j, ti=ti, col=col: nc.vector.tensor_scalar(
                    out=DG[0:m, j, ti, 0:m], in0=IDENT[0:m, 0:m], scalar1=col[:, ti:ti + 1], scalar2=None, op0=ALU.mult),
                    reads=list(reads) + [T("IDENT")], writes=[T("DG")])

    def bcast_pe(ps, ncols, m, ntile):
        def fn():
            ins = None
            for j in range(ncols):
                for ti in range(ntile):
                    ins = nc.tensor.matmul(PS[:, ps, j * 256 + ti * m:j * 256 + (ti + 1) * m], lhsT=ONESF2[0:m, :],
                                           rhs=DG[0:m, j, ti, 0:m], start=True, stop=True)
            return ins
        S.op(PE, fn, reads=[T("DG"), T("ONESF2")], writes=[PST[ps]])

    def stats_rstd_a(sq_n, reads_sq, n, own=False):
        m = min(n, 128)
        ntile = n // m
        ps = nxt("ps")

        def fn():
            ins = None
            for ti in range(ntile):
                for k in range(8):
                    ins = nc.tensor.matmul(PS[0:m, ps, 508 + ti:509 + ti], lhsT=SQ[:, k, ti * m:(ti + 1) * m], rhs=ONESB[:, 0:1],
                                           start=(k == 0), stop=(k == 7))
            return ins
        S.op(PE, fn, reads=list(reads_sq) + [T("ONESB")], writes=[PST[ps]])
        act_copy(MS[0:m, 0:ntile], PS[0:m, ps, 508:508 + ntile], [PST[ps]], [T("MS")], scale=1.0 / sq_n, bias=EPS)
        S.op(POOL, lambda: nc.gpsimd.tensor_tensor(out=RSM[0:m, 0:ntile], in0=MS[0:m, 0:ntile], in1=MHALF[0:m, 0:ntile], op=ALU.pow),
             reads=[T("MS"), T("MHALF")], writes=[T("RSM")])
        dgv = DGA if own else DG[:, 0, :, :]
        dgt = T("DGA") if own else T("DG")
        for ti in range(ntile):
            S.op(DVE, lambda ti=ti: nc.vector.tensor_scalar(
                out=dgv[0:m, ti, 0:m], in0=IDENT[0:m, 0:m], scalar1=RSM[0:m, ti:ti + 1], scalar2=None, op0=ALU.mult),
                reads=[T("RSM"), T("IDENT")], writes=[dgt])
        return (ps, m, ntile, dgv, dgt)

    def stats_rstd_b(ctx):
        ps, m, ntile, dgv, dgt = ctx
        if dgt is T("DGA"):
            ps = nxt("ps")

        def fn():
            ins = None
            for ti in range(ntile):
                ins = nc.tensor.matmul(PS[:, ps, ti * m:(ti + 1) * m], lhsT=ONESF2[0:m, :], rhs=dgv[0:m, ti, 0:m], start=True, stop=True)
            return ins
        S.op(PE, fn, reads=[dgt, T("ONESF2")], writes=[PST[ps]])
        return ps

    class Step:
        pass

    hookq = []

    def run_hook():
        if hookq:
            hookq.pop(0)()

    def make_step(l, blk, wslots, last_in_layer):
        st = Step()
        par, i = l % 2, l // 2
        n, nseq, Tt = blk.n, blk.nseq, blk.T
        xc = blk.xc
        xt = XT(xc)
        hs_box = [None]
        Yv = YB[:, 0, :, 0:n]
        ys = 0

        abox = {}

        def phase_Aa():
            S.op(ACT, lambda: nc.scalar.activation(out=SQ[:, :, 0:n], in_=X[:, :, xc:xc + n], func=AF.Square),
                 reads=[xt], writes=[T("SQ")])
            abox["ctx"] = stats_rstd_a(1024.0, [T("SQ")], n, own=True)

        def phase_Ab():
            psr = stats_rstd_b(abox["ctx"])
            hs = nxt("h")
            hs_box[0] = hs
            for k in range(8):
                dve_stt(H[:, hs, k, 0:n], X[:, k, xc:xc + n], GPRE[:, l, k:k + 1], PS[:, psr, 0:n], ALU.mult, ALU.mult,
                        [xt, T("GPRE"), PST[psr]], [HT[hs]])

        def phase_A():
            phase_Aa()
            phase_Ab()

        def after_use():
            if last_in_layer:
                load_chunk()

        def inproj_fm(c):
            hs = hs_box[0]
            slot = wslots[c]
            pp = nxt("pp")
            view = PP[:, pp, 0:4 * n].rearrange("p (g t) -> p g t", g=4)

            def fn():
                ins = None
                for g in range(4):
                    for k in range(8):
                        ins = nc.tensor.matmul(view[:, g, :], lhsT=RING[:, slot, k, g * 128:(g + 1) * 128], rhs=H[:, hs, k, 0:n],
                                               start=(k == 0), stop=(k == 7))
                return ins
            S.op(PE, fn, reads=[RINGT[slot], HT[hs]], writes=[PPT[pp]])
            after_use()
            return pp, view

        def v4(ap):
            return ap.rearrange("p g (s t) -> p g s t", s=nseq)

        pmx4 = PMX[:, 0:4 * n].rearrange("p (g t) -> p g t", g=4)
        sl8 = slice(8 * blk.sh, 8 * blk.sh + 8)

        if par == 0:
            E = 15 + Tt
            E2 = 2 + Tt
            PX = TB[3][:, 0:4 * nseq * E].rearrange("p (g s e) -> p g s e", g=4, s=nseq)
            SX = TB[1][:, 0:4 * nseq * E2].rearrange("p (g s e) -> p g s e", g=4, s=nseq)
            Df = BFB[0][:, 0:4 * n].rearrange("p (g t) -> p g t", g=4)
            G2 = TB[6][:, 0:4 * n].rearrange("p (g t) -> p g t", g=4)

            def phase_B1():
                pp, view = inproj_fm(0)
                if blk.kind == "p":
                    dve_copy(PX[:, :, 0, 0:15], PH[:, i, :, :], [T("PH%d" % i)], [TBT[3]])
                else:
                    hist_in(st_b[i, sl8].rearrange("s j c -> (s j) c"), 120, PX[:, :, :, 0:15], 8, 15, TBT[3])
                    S.dma("sp", pb_s[i, sl8, 0:7, :], st_b[i, sl8, 8:15, :])
                act_copy(PX[:, :, :, 15:E], v4(view), [PPT[pp]], [TBT[3]])
                if blk.kind == "p" and not blk.last:
                    dve_copy(PH[:, i, :, :], PX[:, :, 0, Tt:Tt + 15], [TBT[3]], [T("PH%d" % i)])
                if blk.last:
                    rows_out(lambda g: PX[:, g, 0, Tt:Tt + 15], 15, pb_p[i], [TBT[3]])
                if blk.kind == "s":
                    rows_out(None, 64, pb_s[i, sl8, 7:15, :], [TBT[3]], src4=PX[:, :, :, 15:E])
                A1 = TB[4][:, 0:4 * nseq * E].rearrange("p (g s e) -> p g s e", g=4, s=nseq)
                A2 = TB[5][:, 0:4 * nseq * E].rearrange("p (g s e) -> p g s e", g=4, s=nseq)
                dve_tt(A1[:, :, :, 1:E], PX[:, :, :, 1:E], PX[:, :, :, 0:E - 1], ALU.add, [TBT[3]], [TBT[4]])
                dve_tt(A2[:, 1:4, :, 3:E], A1[:, 1:4, :, 3:E], A1[:, 1:4, :, 1:E - 2], ALU.add, [TBT[4]], [TBT[5]])
                dve_tt(A1[:, 2:4, :, 7:E], A2[:, 2:4, :, 7:E], A2[:, 2:4, :, 3:E - 4], ALU.add, [TBT[5]], [TBT[4]])
                dve_tt(A2[:, 3, :, 15:E], A1[:, 3, :, 15:E], A1[:, 3, :, 7:E - 8], ALU.add, [TBT[4]], [TBT[5]])
                WIN = [A1[:, 0], A2[:, 1], A1[:, 2], A2[:, 3]]
                wint = [TBT[4], TBT[5], TBT[4], TBT[5]]
                Dv = BFB[0][:, 0:4 * n].rearrange("p (g s t) -> p g s t", g=4, s=nseq)
                for g in range(4):
                    dve_stt(Dv[:, g], WIN[g][:, :, 15:E], 1.0 / POOL_W[g], PX[:, g, :, 15:E], ALU.mult, ALU.subtract,
                            [wint[g], TBT[3]], [BFT[0]])
                if blk.first:
                    for g in range(4):
                        w = POOL_W[g]
                        dve_tt(FIX[:, 0:w - 1], WIN[g][:, 0, 15:15 + w - 1], INVC[:, 0:w - 1], ALU.mult,
                               [wint[g], T("INVC")], [T("FIX")])
                        dve_tt(Dv[:, g, 0, 0:w - 1], FIX[:, 0:w - 1], PX[:, g, 0, 15:15 + w - 1], ALU.subtract,
                               [T("FIX"), TBT[3]], [BFT[0]])
                run_hook()
                pp, view = inproj_fm(1)
                ZC = TB[0][:, 0:4 * n].rearrange("p (g t) -> p g t", g=4)
                act_copy(ZC, view, [PPT[pp]], [TBT[0]])
                if blk.kind == "p":
                    dve_copy(SX[:, :, 0, 0:2], SH[:, i, :, :], [T("SH%d" % i)], [TBT[1]])
                else:
                    hist_in(st_a[i, sl8].rearrange("s j c -> (s j) c"), 16, SX[:, :, :, 0:2], 8, 2, TBT[1])
                pp, view = inproj_fm(2)
                dve_tt(SX[:, :, :, 2:E2], v4(ZC), v4(view), ALU.mult, [TBT[0], PPT[pp]], [TBT[1]])
                if blk.kind == "p" and not blk.last:
                    dve_copy(SH[:, i, :, :], SX[:, :, 0, Tt:Tt + 2], [TBT[1]], [T("SH%d" % i)])
                if blk.last:
                    rows_out(lambda g: SX[:, g, 0, Tt:Tt + 2], 2, ca_p[i], [TBT[1]])
                if blk.kind == "s":
                    rows_out(None, 16, ca_s[i, sl8].rearrange("s j c -> (s j) c"), [TBT[1]], src4=SX[:, :, :, Tt:Tt + 2])
                run_hook()

            def phase_B2():
                Y3 = TB[0][:, 0:4 * n].rearrange("p (g s t) -> p g s t", g=4, s=nseq)
                y3t = [T("Y3_%d" % g) for g in range(4)]
                for k in range(3):
                    for g in range(4):
                        if k == 0:
                            dve_ts(Y3[:, g], SX[:, g, :, 0:Tt], CAW[:, i, 0, g:g + 1], None, ALU.mult, None,
                                   [TBT[1], T("CAW")], [TBT[0], y3t[g]] if g == 0 else [y3t[g]])
                        else:
                            dve_stt(Y3[:, g], SX[:, g, :, k:k + Tt], CAW[:, i, k, g:g + 1], Y3[:, g], ALU.mult, ALU.add,
                                    [TBT[1], T("CAW"), y3t[g]], [y3t[g]])
                pp, view = inproj_fm(3)
                G = TB[2][:, 0:4 * n].rearrange("p (g t) -> p g t", g=4)
                act_copy(G, view, [PPT[pp]], [TBT[2]], func=AF.Silu)
                pp, view = inproj_fm(4)
                dve_tt(G, view, G, ALU.mult, [PPT[pp], TBT[2]], [TBT[2]])
                dve_tt(v4(Yv[:, 0:4, :]), Y3, v4(G), ALU.mult, [TBT[2]] + y3t, [YTA, TBT[0]])
                pp, view = inproj_fm(5)
                act_copy(G2, view, [PPT[pp]], [TBT[6]], func=AF.Silu)

                def fnm():
                    ins = None
                    for g in range(4):
                        ins = nc.tensor.matmul(pmx4[:, g, :], lhsT=PW[:, g, :], rhs=Df[:, g, :], start=True, stop=True)
                    return ins
                S.op(PE, fnm, reads=[BFT[0], T("PW")], writes=[T("PMX")])
                for g in range(4):
                    dve_stt(Yv[:, 4 + g, :], pmx4[:, g, :], PSC[:, i, g:g + 1], G2[:, g, :], ALU.mult, ALU.mult,
                            [T("PMX"), T("PSC"), TBT[6]], [YTB])
                run_hook()
        else:
            m = 128 if blk.kind == "p" else 64
            ntile = n // m
            EL = 31 + Tt
            RL = 28 + Tt
            XE = XEB[:, 0:4 * nseq * EL].rearrange("p (g s e) -> p g s e", g=4, s=nseq)
            xet = T("XEB")
            RV = TB34[:, :].bitcast(BF16).rearrange("p (q s e) -> p q s e", q=16, s=nseq)
            YN = TB[5][:, 0:4 * n].rearrange("p (g t) -> p g t", g=4)
            YBF = BFB[1][:, 0:4 * n].rearrange("p (g t) -> p g t", g=4)
            YSQ = BFB[2][:, 0:4 * n].rearrange("p (g t) -> p g t", g=4)
            VN = BFB[0][:, 0:1024].rearrange("p (t c) -> p t c", t=2)
            G = TB[2][:, 0:4 * n].rearrange("p (g t) -> p g t", g=4)
            mL = min(n, 128)
            ntL = n // mL
            box = {}

            def phase_B1():
                ppa, viewa = inproj_fm(0)
                ZA = TB[0][:, 0:4 * n].rearrange("p (g t) -> p g t", g=4)
                act_copy(ZA, viewa, [PPT[ppa]], [TBT[0]], scale=0.5)
                run_hook()
                ppb, viewb = inproj_fm(1)
                TH = TB[6][:, 0:4 * n].rearrange("p (g t) -> p g t", g=4)
                act_copy(TH, viewb, [PPT[ppb]], [TBT[6]], scale=0.5, func=AF.Tanh)
                if blk.kind == "p":
                    dve_copy(XE[:, :, 0, 0:30], GH[:, i, :, :], [T("GH%d" % i)], [xet])
                else:
                    for q in range(2):
                        sl = slice(8 * blk.sh + 4 * q, 8 * blk.sh + 4 * q + 4)
                        hist_in(st_d[i, sl].rearrange("s j c -> (s j) c"), 120, XE[:, :, 4 * q:4 * q + 4, 0:30], 4, 30, xet)
                    S.dma("sp", cd_s[i, sl8, 0:22, :], st_d[i, sl8, 8:30, :])
                dve_stt(XE[:, :, :, 30:30 + Tt], v4(TH), 1.0, v4(ZA), ALU.add, ALU.mult, [TBT[6], TBT[0]], [xet])
                if blk.kind == "p":
                    dve_stt(GH[:, i, :, :], TH[:, :, Tt - 30:Tt], 1.0, ZA[:, :, Tt - 30:Tt], ALU.add, ALU.mult,
                            [TBT[6], TBT[0]], [T("GH%d" % i)])
                    if blk.last:
                        rows_out(lambda g: GH[:, i, g, :], 30, cd_p[i], [T("GH%d" % i)])
                else:
                    dve_stt(GST[:, :, :], TH, 1.0, ZA, ALU.add, ALU.mult, [TBT[6], TBT[0]], [T("CMP")])
                    rows_out(lambda g: GST[:, g, :], 64, cd_s[i, sl8, 22:30, :], [T("CMP")])
                rts = [T("R%d" % z) for z in range(8)]
                for z in range(8):
                    if z % 3 == 2:
                        ptile, pview = T("PMX"), PMX[:, :]
                    else:
                        pp = nxt("pp")
                        ptile, pview = PPT[pp], PP[:, pp, :]

                    def fnrep(z=z, pview=pview):
                        ins = None
                        for dq in range(2):
                            q = 2 * z + dq
                            J, j = q // 4, q % 4
                            for s_ in range(4):
                                rhs = XE[:, J, 0, s_:s_ + RL] if nseq == 1 else XE[:, J, :, s_:s_ + RL]
                                ins = nc.tensor.matmul(pview[32 * s_:32 * s_ + 32, dq * 512:dq * 512 + nseq * RL],
                                                       lhsT=IDB[:, 32 * j:32 * j + 32], rhs=rhs, start=True, stop=True,
                                                       tile_position=(0, 32 * s_))
                        return ins
                    S.op(PE, fnrep, reads=[xet, T("IDB")], writes=[ptile])
                    src = pview.rearrange("p (a c) -> p a c", a=2)[:, :, 0:nseq * RL].rearrange("p a (s e) -> p a s e", s=nseq)
                    dst = RV[:, 2 * z:2 * z + 2, :, 0:RL]
                    wr = [TBT[3], TBT[4], rts[z]] if z == 0 else [rts[z]]
                    rdx = [ptile] if z == 0 else [ptile, TBT[3]]
                    if z % 3 != 1:
                        act_copy(dst, src, rdx, wr)
                    else:
                        dve_copy(dst, src, rdx, wr)

                def fnconv():
                    ins = None
                    for J in range(4):
                        for m_ in range(8):
                            for j in range(4):
                                if nseq == 1:
                                    rhs = RV[:, 4 * J + j, 0, 4 * m_:4 * m_ + Tt]
                                else:
                                    rhs = RV[:, 4 * J + j, :, 4 * m_:4 * m_ + Tt]
                                ins = nc.tensor.matmul(pmx4[32 * j:32 * j + 32, J, :], lhsT=WP4[:, m_, 4 * J + j, :], rhs=rhs,
                                                       start=(m_ == 0), stop=(m_ == 7), tile_position=(0, 32 * j))
                    return ins
                S.op(PE, fnconv, reads=rts + [T("WP4")], writes=[T("PMX"), TBT[3], TBT[4]])
                run_hook()
                for g in range(4):
                    act_copy(YN[:, g, :], pmx4[:, g, :], [T("PMX"), T("CDB")], [TBT[5]], bias=CDB[:, i, g:g + 1])
                act_copy(YBF, YN, [TBT[5]], [BFT[1]])
                act_copy(YSQ, YN, [TBT[5]], [BFT[2]], func=AF.Square)
                slot = wslots[2]
                hs = hs_box[0]
                pp = nxt("pp")

                def fnv():
                    ins = None
                    for ti in range(ntile):
                        for k in range(8):
                            ins = nc.tensor.matmul(PP[0:m, pp, ti * 512:(ti + 1) * 512], lhsT=H[:, hs, k, ti * m:(ti + 1) * m],
                                                   rhs=RING[:, slot, k, :], start=(k == 0), stop=(k == 7))
                    return ins
                S.op(PE, fnv, reads=[RINGT[slot], HT[hs]], writes=[PPT[pp]])
                after_use()
                for ti in range(ntile):
                    S.op(DVE, lambda ti=ti: nc.vector.bn_stats(out=BST[0:m, ti, :], in_=PP[0:m, pp, ti * 512:(ti + 1) * 512]),
                         reads=[PPT[pp]], writes=[T("BST%d" % ti)])
                    S.op(DVE, lambda ti=ti: nc.vector.bn_aggr(out=MV[0:m, ti, :], in_=BST[0:m, ti, :]),
                         reads=[T("BST%d" % ti)], writes=[T("MV%d" % ti)])
                mvt = [T("MV%d" % ti) for ti in range(ntile)]
                S.op(POOL, lambda: nc.gpsimd.tensor_scalar(out=VE[0:m, 0:ntile], in0=MV[0:m, 0:ntile, 1], scalar1=EPS, scalar2=None,
                                                            op0=ALU.add), reads=mvt, writes=[T("VE")])
                S.op(POOL, lambda: nc.gpsimd.tensor_tensor(out=RS[0:m, 0:ntile], in0=VE[0:m, 0:ntile], in1=MHALF[0:m, 0:ntile],
                                                            op=ALU.pow), reads=[T("VE"), T("MHALF")], writes=[T("RS")])
                dve_stt(NMR[0:m, 0:ntile], MV[0:m, 0:ntile, 0], -1.0, RS[0:m, 0:ntile], ALU.mult, ALU.mult,
                        mvt + [T("RS")], [T("NMR")])
                NT = TB[1][:, 0:1024].rearrange("p (t c) -> p t c", t=2)
                for ti in range(ntile):
                    act_copy(NT[0:m, ti, :], PP[0:m, pp, ti * 512:(ti + 1) * 512], [PPT[pp], T("RS"), T("NMR")], [TBT[1]],
                             scale=RS[0:m, ti:ti + 1], bias=NMR[0:m, ti:ti + 1])
                gbb = GB[0:m, :].unsqueeze(1).broadcast_to([m, ntile, 512])
                bbb = BB[0:m, :].unsqueeze(1).broadcast_to([m, ntile, 512])
                dve_tt(NT[0:m, 0:ntile, :], NT[0:m, 0:ntile, :], gbb, ALU.mult, [TBT[1], T("GB")], [TBT[1]])
                if blk.kind == "p":
                    dve_tt(VN[0:m, 0:ntile, :], NT[0:m, 0:ntile, :], bbb, ALU.add, [TBT[1], T("BB")], [BFT[0]])
                else:
                    dve_tt(NT[0:m, 0:1, :], NT[0:m, 0:1, :], bbb, ALU.add, [TBT[1], T("BB")], [TBT[1]])
                    S.dma("sp", v_s[i, sl8].rearrange("s t c -> (s t) c"), NT[0:m, 0, :], reads=[TBT[1]])
                    act_copy(VN[0:m, 0:1, :], NT[0:m, 0:1, :], [TBT[1]], [BFT[0]])
                psa = nxt("ps")
                box["psa"] = psa

                def fnln():
                    ins = None
                    for ti in range(ntL):
                        for j, src in enumerate((YBF, YSQ)):
                            for g in range(4):
                                ins = nc.tensor.matmul(PS[0:mL, psa, 2 * j + ti:2 * j + ti + 1], lhsT=src[:, g, ti * mL:(ti + 1) * mL],
                                                       rhs=ONESB[:, 0:1], start=(g == 0), stop=(g == 3))
                    return ins
                S.op(PE, fnln, reads=[BFT[1], BFT[2], T("ONESB")], writes=[PST[psa]])
                act_copy(LST[0:mL, 0:4], PS[0:mL, psa, 0:4], [PST[psa]], [T("LST")], scale=1.0 / 512)
                dve_tt(LT[0:mL, 0:ntL], LST[0:mL, 0:ntL], LST[0:mL, 0:ntL], ALU.mult, [T("LST")], [T("LT")])
                dve_tt(LT[0:mL, 0:ntL], LST[0:mL, 2:2 + ntL], LT[0:mL, 0:ntL], ALU.subtract, [T("LST"), T("LT")], [T("LT")])
                S.op(POOL, lambda: nc.gpsimd.tensor_scalar(out=LT[0:mL, 0:ntL], in0=LT[0:mL, 0:ntL], scalar1=EPS, scalar2=None, op0=ALU.add),
                     reads=[T("LT")], writes=[T("LT")])
                S.op(POOL, lambda: nc.gpsimd.tensor_tensor(out=LRS[0:mL, 0:ntL], in0=LT[0:mL, 0:ntL], in1=MHALF[0:mL, 0:ntL], op=ALU.pow),
                     reads=[T("LT"), T("MHALF")], writes=[T("LRS")])
                dve_stt(LNMR[0:mL, 0:ntL], LST[0:mL, 0:ntL], -1.0, LRS[0:mL, 0:ntL], ALU.mult, ALU.mult,
                        [T("LST"), T("LRS")], [T("LNMR")])
                bcast_dg([LRS[0:mL, 0:ntL], LNMR[0:mL, 0:ntL]], mL, ntL, [T("LRS"), T("LNMR")])

            def phase_B2():
                ppg, viewg = inproj_fm(3)
                act_copy(G, viewg, [PPT[ppg]], [TBT[2]], func=AF.Silu)
                ppu, viewu = inproj_fm(4)
                dve_tt(G, viewu, G, ALU.mult, [PPT[ppu], TBT[2]], [TBT[2]])
                ppd, viewd = inproj_fm(5)
                SGD = TB[7][:, 0:4 * n].rearrange("p (g t) -> p g t", g=4)
                act_copy(SGD, viewd, [PPT[ppd]], [TBT[7]], func=AF.Silu)
                psb = nxt("ps")
                bcast_pe(psb, 2, mL, ntL)
                def fnmix():
                    ins = None
                    for g in range(4):
                        for ti in range(ntile):
                            o = pmx4[:, g, ti * m:(ti + 1) * m]
                            if blk.kind == "p":
                                nc.tensor.matmul(o, lhsT=VN[0:m, ti, g * 128:(g + 1) * 128], rhs=WMT[:, i, g, :], start=True, stop=False)
                                nc.tensor.matmul(o, lhsT=SEL[:, g, :], rhs=BSHI[:, :], start=False, stop=False)
                                ins = nc.tensor.matmul(o, lhsT=SEL[:, g, :], rhs=BSLO[:, :], start=False, stop=True)
                            else:
                                nc.tensor.matmul(o, lhsT=VN[0:m, ti, g * 128:(g + 1) * 128], rhs=WMTS[:, i, g, :], start=True, stop=False)
                                nc.tensor.matmul(o, lhsT=SEL[:, g, :], rhs=BSHIS[:, :], start=False, stop=False)
                                ins = nc.tensor.matmul(o, lhsT=SEL[:, g, :], rhs=BSLOS[:, :], start=False, stop=True)
                    return ins
                S.op(PE, fnmix, reads=[BFT[0], T("WMT"), T("WMTS"), T("SEL"), T("BSHL")], writes=[T("PMX")])
                dve_tt(Yv[:, 0:4, :], pmx4, G, ALU.mult, [T("PMX"), TBT[2]], [YTA])
                rb = PS[:, psb, 0:n].unsqueeze(1).broadcast_to([128, 4, n])
                mb = PS[:, psb, 256:256 + n].unsqueeze(1).broadcast_to([128, 4, n])
                t5h = [T("T5a"), T("T5b")]
                rbh = PS[:, psb, 0:n].unsqueeze(1).broadcast_to([128, 2, n])
                mbh = PS[:, psb, 256:256 + n].unsqueeze(1).broadcast_to([128, 2, n])
                for hh in range(2):
                    gs = slice(2 * hh, 2 * hh + 2)
                    dve_tt(YN[:, gs, :], YN[:, gs, :], rbh, ALU.mult, [TBT[5], PST[psb]], [t5h[hh]])
                    dve_tt(YN[:, gs, :], YN[:, gs, :], mbh, ALU.add, [t5h[hh], PST[psb]], [t5h[hh]])
                    for g in (2 * hh, 2 * hh + 1):
                        act_copy(YN[:, g, :], YN[:, g, :], [t5h[hh], T("CLG"), T("CLB")], [t5h[hh]], func=AF.Silu,
                                 scale=CLG[:, i, g:g + 1], bias=CLB[:, i, g:g + 1])
                dve_tt(Yv[:, 4:6, :], YN[:, 0:2, :], SGD[:, 0:2, :], ALU.mult, [t5h[0], TBT[7]], [T("YBh0")])
                dve_tt(Yv[:, 6:8, :], YN[:, 2:4, :], SGD[:, 2:4, :], ALU.mult, [t5h[1], TBT[7]], [YTB, TBT[5]])
                run_hook()

        def phase_C1():
            slots, views = [], []
            for hf in range(2):
                pp = nxt("pp")
                slots.append(pp)
                views.append(PP[:, pp, 0:4 * n].rearrange("p (g t) -> p g t", g=4))
            for hf in range(2):
                slot, view, pp = wslots[6 + hf], views[hf], slots[hf]

                def fno1(slot=slot, view=view):
                    ins = None
                    for g in range(4):
                        for k in range(4):
                            first = (k == 0) and ((g * n) % 512 == 0)
                            ins = nc.tensor.matmul(view[:, g, :], lhsT=RING[:, slot, k, g * 128:(g + 1) * 128], rhs=YB[:, ys, k, 0:n],
                                                   start=first, stop=False, skip_group_check=True)
                    return ins
                S.op(PE, fno1, reads=[RINGT[slot], YTA], writes=[PPT[pp]])
            for hf in range(2):
                slot, view, pp = wslots[6 + hf], views[hf], slots[hf]

                def fno2(slot=slot, view=view):
                    ins = None
                    for g in range(4):
                        for k in range(4, 8):
                            ins = nc.tensor.matmul(view[:, g, :], lhsT=RING[:, slot, k, g * 128:(g + 1) * 128], rhs=YB[:, ys, k, 0:n],
                                                   start=False, stop=(k == 7), skip_group_check=True)
                    return ins
                S.op(PE, fno2, reads=[RINGT[slot], YTB, T("YBh0"), PPT[pp]], writes=[PPT[pp]])
                after_use()
                act_copy(SQ[:, hf * 4:hf * 4 + 4, 0:n], view, [PPT[pp]], [T("SQ")], func=AF.Square)
                for g in range(4):
                    kk = hf * 4 + g
                    act_copy(OG[:, kk, 0:n], view[:, g, :], [PPT[pp], T("GPOST")], [T("OG")], scale=GPOST[:, l, kk:kk + 1])

        cbox = {}

        def phase_C2a():
            cbox["ctx"] = stats_rstd_a(1024.0, [T("SQ")], n)

        def phase_C2b():
            psr2 = stats_rstd_b(cbox["ctx"])
            rb2 = PS[:, psr2, 0:n].unsqueeze(1).broadcast_to([128, 8, n])
            dve_tt(OG[:, :, 0:n], OG[:, :, 0:n], rb2, ALU.mult, [T("OG"), PST[psr2]], [T("OG")])
            dve_tt(X[:, :, xc:xc + n], X[:, :, xc:xc + n], OG[:, :, 0:n], ALU.add, [xt, T("OG")], [xt])

        def phase_C2():
            phase_C2a()
            phase_C2b()

        st.Aa, st.Ab, st.C2a, st.C2b = phase_Aa, phase_Ab, phase_C2a, phase_C2b
        st.A, st.B1, st.B2, st.C1, st.C2 = phase_A, phase_B1, phase_B2, phase_C1, phase_C2
        return st

    passes = [
        [Blk("s", 1024, 64, 8, 8, sh=0), Blk("s", 1088, 64, 8, 8, sh=1)] +
        [Blk("p", 256 * b, 256, 1, 256, tok0=256 * b) for b in range(4)],
        [Blk("p", 256 * b, 256, 1, 256, tok0=1024 + 256 * b) for b in range(4)],
    ]
    chunk_no = 0
    for ps_, blocks in enumerate(passes):
        j = 0
        if ps_ == 0:
            for tt in range(8):
                load_x_tile(xp[tt * 128:(tt + 1) * 128, :], tt * 128, j, [XT((tt // 2) * 256)])
                j += 1
            load_x_tile(xs[:, :], 1024, j, [XT(1024), XT(1088)])
            j += 1
            late_consts()
            load_params(0)
        else:
            for tt in range(8):
                load_x_tile(xp[1024 + tt * 128:1024 + (tt + 1) * 128, :], tt * 128, j, [XT((tt // 2) * 256)])
                j += 1
        steps = []
        for l in range(4):
            wslots = [(chunk_no + c) % NSLOT for c in range(8)]
            for bi, blk in enumerate(blocks):
                stp = make_step(l, blk, wslots, bi == len(blocks) - 1)
                stp.pre = (lambda l=l: load_params((l + 1) % 4)) if (bi == 0 and not (ps_ == 1 and l == 3)) else None
                if ps_ == 0 and l == 0:
                    stp.pre = (lambda: (sgu_setup(), load_params(1))) if bi == 2 else None
                steps.append(stp)
            chunk_no += 8
        if PIPELINE == 4:
            steps[0].A()
            for si, stp in enumerate(steps):
                if stp.pre is not None:
                    stp.pre()
                prev = steps[si - 1] if si > 0 else None
                nx = steps[si + 1] if si + 1 < len(steps) else None
                if prev is not None:
                    hookq.append(prev.C2a)
                    if nx is not None:
                        hookq.append(lambda prev=prev, nx=nx: (prev.C2b(), nx.Aa()))
                        hookq.append(nx.Ab)
                    else:
                        hookq.append(prev.C2b)
                elif nx is not None:
                    hookq.append(nx.Aa)
                    hookq.append(nx.Ab)
                stp.B1()
                stp.B2()
                while hookq:
                    hookq.pop(0)()
                stp.C1()
            steps[-1].C2()
        elif PIPELINE == 2:
            steps[0].A()
            for si, stp in enumerate(steps):
                if stp.pre is not None:
                    stp.pre()
                stp.B1()
                if si > 0:
                    steps[si - 1].C2()
                if si + 1 < len(steps):
                    steps[si + 1].A()
                stp.B2()
                stp.C1()
            steps[-1].C2()
        elif PIPELINE == 3:
            for si, stp in enumerate(steps):
                if stp.pre is not None:
                    stp.pre()
                stp.A()
                stp.B1()
                if si > 0:
                    steps[si - 1].C2()
                stp.B2()
                stp.C1()
            steps[-1].C2()
        elif PIPELINE == 1:
            steps[0].A()
            for si, stp in enumerate(steps):
                if stp.pre is not None:
                    stp.pre()
                stp.B1()
                if si + 1 < len(steps):
                    steps[si + 1].A()
                stp.B2()
                stp.C1()
                stp.C2()
        else:
            for si, stp in enumerate(steps):
                if stp.pre is not None:
                    stp.pre()
                stp.A()
                stp.B1()
                stp.B2()
                stp.C1()
                stp.C2()
        j = 0
        if ps_ == 0:
            for tt in range(8):
                store_x_tile(y_p[tt * 128:(tt + 1) * 128, :], tt * 128, j, [XT((tt // 2) * 256)])
                j += 1
            store_x_tile(y_s[:, :], 1024, j, [XT(1024), XT(1088)])
        else:
            for tt in range(8):
                store_x_tile(y_p[1024 + tt * 128:1024 + (tt + 1) * 128, :], tt * 128, j, [XT((tt // 2) * 256)])
                j += 1
    print('SBUF bytes remaining per partition:', nc.sbuf_bytes_remaining)
    S.finish()
    es.close()
    return nc


_CACHE = {}


def kernel(x_prompt, x_sample, state_conv_a, state_pool_b, state_conv_d,
           norm_pre, norm_post, w_in_even, w_out_even, conv_a_w, pool_w, pool_scale,
           w_in_odd, w_out_odd, sgu_ln_g, sgu_ln_b, sgu_w, sgu_b,
           conf_dw_w, conf_dw_b, conf_ln_g, conf_ln_b):
    f = lambda a: np.ascontiguousarray(np.asarray(a, dtype=np.float32))
    x_prompt, x_sample = f(x_prompt), f(x_sample)
    state_conv_a, state_pool_b, state_conv_d = f(state_conv_a), f(state_pool_b), f(state_conv_d)
    shared = dict(
        norm_pre=f(norm_pre), norm_post=f(norm_post), w_in_even=f(w_in_even), w_out_even=f(w_out_even),
        conv_a_w=f(conv_a_w), pool_w=f(pool_w), pool_scale=f(pool_scale), w_in_odd=f(w_in_odd),
        w_out_odd=f(w_out_odd), sgu_ln_g=f(sgu_ln_g), sgu_ln_b=f(sgu_ln_b), sgu_w=f(sgu_w), sgu_b=f(sgu_b),
        conf_dw_w=f(conf_dw_w), conf_dw_b=f(conf_dw_b), conf_ln_g=f(conf_ln_g), conf_ln_b=f(conf_ln_b),
    )
    shared["c_ident"] = np.eye(128, dtype=np.float32)
    shared["c_tril"] = np.tril(np.ones((128, 128), np.float32))
    jj = np.arange(64)
    shared["c_bmask"] = ((jj[:, None] // 8 == jj[None, :] // 8) & (jj[:, None] % 8 <= jj[None, :] % 8)).astype(np.float32)
    npre, npost = shared.pop("norm_pre"), shared.pop("norm_post")
    caw, psc = shared.pop("conv_a_w"), shared.pop("pool_scale")
    cdw, cdb = shared.pop("conf_dw_w"), shared.pop("conf_dw_b")
    clg, clb = shared.pop("conf_ln_g"), shared.pop("conf_ln_b")
    c_ = np.ascontiguousarray
    shared["p_gpre"] = c_(npre.reshape(4, 8, 128).transpose(2, 0, 1))
    shared["p_gpost"] = c_(npost.reshape(4, 8, 128).transpose(2, 0, 1))
    shared["p_caw"] = c_(caw.reshape(2, 3, 4, 128).transpose(3, 0, 1, 2))
    shared["p_psc"] = c_(psc.reshape(2, 4, 128).transpose(2, 0, 1))
    shared["p_cdb"] = c_(cdb.reshape(2, 4, 128).transpose(2, 0, 1))
    shared["p_clg"] = c_(clg.reshape(2, 4, 128).transpose(2, 0, 1))
    shared["p_clb"] = c_(clb.reshape(2, 4, 128).transpose(2, 0, 1))
    wpad = np.zeros((2, 32, 512), np.float32)
    wpad[:, 0:31, :] = cdw
    shared["p_cdw4"] = c_(wpad.reshape(2, 8, 4, 16, 32).transpose(2, 4, 0, 1, 3).reshape(128, 2, 8, 16))
    sw = shared["sgu_w"]
    wsf = np.zeros((2, 64, 4, 64), np.float32)
    for s_ in range(8):
        wsf[:, 8 * s_:8 * s_ + 8, :, 8 * s_:8 * s_ + 8] = sw[:, :, 0:8, 0:8].transpose(0, 3, 1, 2)
    shared["p_wsf"] = wsf
    shared["p_bsrows"] = c_(np.tile(shared["sgu_b"][:, :, 0:8], (1, 1, 8)))
    shared["c_idrep"] = np.tile(np.eye(32, dtype=np.float32), (4, 1))
    sel = np.zeros((4, 4, 128), np.float32)
    for g in range(4):
        sel[g, g, :] = 1.0
    shared["c_sel"] = sel
    shared["c_invc"] = np.tile((1.0 / np.arange(1, 17, dtype=np.float32))[None, :], (128, 1)).astype(np.float32)
    in_maps = []
    for c in range(NCORES):
        m = dict(shared)
        m["xp"] = x_prompt[c]
        m["xs"] = np.ascontiguousarray(x_sample[16 * c:16 * c + 16].reshape(128, 1024))
        m["st_a"] = np.ascontiguousarray(state_conv_a[:, 16 * c:16 * c + 16])
        m["st_b"] = np.ascontiguousarray(state_pool_b[:, 16 * c:16 * c + 16])
        m["st_d"] = np.ascontiguousarray(state_conv_d[:, 16 * c:16 * c + 16])
        in_maps.append(m)
    if "nc" not in _CACHE:
        _CACHE["nc"] = build_program()
    res = run_bass_kernel_spmd(_CACHE["nc"], in_maps, core_ids=list(range(NCORES)))
    R = res.results
    y_prompt = np.stack([R[c]["y_p"] for c in range(NCORES)], 0)
    y_sample = np.concatenate([R[c]["y_s"].reshape(16, 8, 1024) for c in range(NCORES)], 0)
    ca_p = np.stack([R[c]["ca_p"] for c in range(NCORES)], 1)
    ca_s = np.concatenate([R[c]["ca_s"] for c in range(NCORES)], 1)
    pb_p = np.stack([R[c]["pb_p"] for c in range(NCORES)], 1)
    pb_s = np.concatenate([R[c]["pb_s"] for c in range(NCORES)], 1)
    cd_p = np.stack([R[c]["cd_p"] for c in range(NCORES)], 1)
    cd_s = np.concatenate([R[c]["cd_s"] for c in range(NCORES)], 1)
    v_s = np.concatenate([R[c]["v_s"] for c in range(NCORES)], 1)
    return (y_prompt, y_sample, ca_p, ca_s, pb_p, pb_s, cd_p, cd_s, v_s)
```

```python
import contextlib
import numpy as np
import concourse.bass as bass
import concourse.mybir as mybir
from concourse.bass_utils import run_bass_kernel_spmd

F32 = mybir.dt.float32
BF16 = mybir.dt.bfloat16
AF = mybir.ActivationFunctionType
ALU = mybir.AluOpType

NCORES = 8
EPS = 1e-6
NSLOT = 9
TBSZ = (1024, 1040, 1024, 1216, 1088, 1088, 1024, 1024)
NTB = 8
POOL_W = (2, 4, 8, 16)
PIPELINE = 4


class Tk:
    __slots__ = ("name", "w", "rd")

    def __init__(self, name):
        self.name = name
        self.w = None
        self.rd = {}


class Eng:
    def __init__(self, h, sem, name):
        self.h = h
        self.sem = sem
        self.name = name
        self.n = 0
        self.seen = {}


class Sched:
    def __init__(self, nc, es):
        self.nc = nc
        self.es = es
        mk = lambda nm: es.enter_context(nc.semaphore(nm))
        self.pe = Eng(nc.tensor, mk("s_pe"), "pe")
        self.act = Eng(nc.scalar, mk("s_act"), "act")
        self.dve = Eng(nc.vector, mk("s_dve"), "dve")
        self.pool = Eng(nc.gpsimd, mk("s_pool"), "pool")
        self.sp = Eng(nc.sync, mk("s_sp"), "sp")
        self.dsem = {}
        for q, cnt in (("sp", 16), ("pool", 12)):
            self.dsem[q] = [[mk("d_%s%d" % (q, j)), 0] for j in range(cnt)]
        self.drr = {"sp": 0, "pool": 0}
        self.gsem = {}

    def _waits(self, eng, reads, writes):
        deps = {}

        def add(d, same_ok):
            if d is None:
                return
            s, v = d
            if (not same_ok) and s is eng.sem:
                return
            if deps.get(s, (None, 0))[1] < v:
                deps[s] = (s, v)

        for t in reads:
            add(t.w, True)
        for t in writes:
            add(t.w, True)
            for s, v in t.rd.items():
                add((s, v), True)
        for s, v in deps.values():
            if eng.seen.get(s, 0) >= v:
                continue
            eng.h.wait_ge(s, v)
            eng.seen[s] = v

    def _commit(self, me, reads, writes):
        s, v = me
        for t in reads:
            t.rd[s] = v
        for t in writes:
            t.w = me
            t.rd = {}

    def op(self, eng, fn, reads=(), writes=()):
        self._waits(eng, reads, writes)
        ins = fn()
        eng.n += 1
        ins.then_inc(eng.sem, 1)
        self._commit((eng.sem, eng.n), reads, writes)

    def dma(self, q, out_ap, in_ap, reads=(), writes=(), **kw):
        eng = self.sp if q == "sp" else self.pool
        self._waits(eng, reads, writes)
        lst = self.dsem[q]
        j = self.drr[q]
        self.drr[q] = (j + 1) % len(lst)
        ent = lst[j]
        if ent[1] > 0 and eng.seen.get(ent[0], 0) < ent[1]:
            eng.h.wait_ge(ent[0], ent[1])
            eng.seen[ent[0]] = ent[1]
        ins = eng.h.dma_start(out=out_ap, in_=in_ap, **kw)
        ent[1] += 16
        ins.then_inc(ent[0], 16)
        self._commit((ent[0], ent[1]), reads, writes)

    def dma_group(self, q, pairs, reads=(), writes=(), key="g0"):
        eng = self.sp if q == "sp" else self.pool
        self._waits(eng, reads, writes)
        if key not in self.gsem:
            self.gsem[key] = [self.es.enter_context(self.nc.semaphore("dg_" + key)), 0]
        ent = self.gsem[key]
        for (o, i_) in pairs:
            eng.h.dma_start(out=o, in_=i_).then_inc(ent[0], 16)
            ent[1] += 16
        self._commit((ent[0], ent[1]), reads, writes)

    def finish(self):
        for q in ("sp", "pool"):
            for s, v in self.dsem[q]:
                if v > 0:
                    self.nc.sync.wait_ge(s, v)


class Blk:
    def __init__(self, kind, xc, n, nseq, T, tok0=0, sh=0):
        self.kind = kind
        self.xc = xc
        self.n = n
        self.nseq = nseq
        self.T = T
        self.tok0 = tok0
        self.sh = sh
        self.first = (kind == "p" and tok0 == 0)
        self.last = (kind == "p" and tok0 + n == 2048)


def build_program():
    nc = bass.Bass("TRN2", target_bir_lowering=False)
    es = contextlib.ExitStack()

    def din(name, shape):
        return nc.dram_tensor(name, shape, F32, kind="ExternalInput").ap()

    def dout(name, shape):
        return nc.dram_tensor(name, shape, F32, kind="ExternalOutput").ap()

    xp = din("xp", [2048, 1024])
    xs = din("xs", [128, 1024])
    st_a = din("st_a", [2, 16, 2, 512])
    st_b = din("st_b", [2, 16, 15, 512])
    st_d = din("st_d", [2, 16, 30, 512])
    p_gpre = din("p_gpre", [128, 4, 8])
    p_gpost = din("p_gpost", [128, 4, 8])
    p_caw = din("p_caw", [128, 2, 3, 4])
    p_psc = din("p_psc", [128, 2, 4])
    p_cdb = din("p_cdb", [128, 2, 4])
    p_clg = din("p_clg", [128, 2, 4])
    p_clb = din("p_clb", [128, 2, 4])
    p_cdw4 = din("p_cdw4", [128, 2, 8, 16])
    p_wsf = din("p_wsf", [2, 64, 4, 64])
    p_bsrows = din("p_bsrows", [2, 4, 64])
    w_in = [din("w_in_even", [2, 1024, 3072]), din("w_in_odd", [2, 1024, 3072])]
    w_out = [din("w_out_even", [2, 1024, 1024]), din("w_out_odd", [2, 1024, 1024])]
    pool_w = din("pool_w", [2, 4, 128, 128])
    sgu_ln_g = din("sgu_ln_g", [2, 512])
    sgu_ln_b = din("sgu_ln_b", [2, 512])
    sgu_w = din("sgu_w", [2, 4, 128, 128])
    sgu_b = din("sgu_b", [2, 4, 128])
    c_ident = din("c_ident", [128, 128])
    c_tril = din("c_tril", [128, 128])
    c_bmask = din("c_bmask", [64, 64])
    c_invc = din("c_invc", [128, 16])
    c_idrep = din("c_idrep", [128, 32])
    c_sel = din("c_sel", [4, 4, 128])

    y_p = dout("y_p", [2048, 1024])
    y_s = dout("y_s", [128, 1024])
    ca_p = dout("ca_p", [2, 2, 512])
    ca_s = dout("ca_s", [2, 16, 2, 512])
    pb_p = dout("pb_p", [2, 15, 512])
    pb_s = dout("pb_s", [2, 16, 15, 512])
    cd_p = dout("cd_p", [2, 30, 512])
    cd_s = dout("cd_s", [2, 16, 30, 512])
    v_s = dout("v_s", [2, 16, 8, 512])

    S = Sched(nc, es)
    PE, ACT, DVE, POOL = S.pe, S.act, S.dve, S.pool

    def sb(name, shape, dt=F32):
        return es.enter_context(nc.sbuf_tensor(name, shape, dt))

    XW = 1152
    X = sb("X", [128, 8, XW])
    RING = sb("RING", [128, NSLOT, 8, 512], BF16)
    H = sb("H", [128, 2, 8, 256], BF16)
    SQ = sb("SQ", [128, 8, 256], BF16)
    YB = sb("YB", [128, 1, 8, 256], BF16)
    OG = sb("OG", [128, 8, 256])
    MS = sb("MS", [128, 2])
    RSM = sb("RSM", [128, 2])
    DG = sb("DG", [128, 2, 2, 128])
    ONESF2 = sb("ONESF2", [128, 128])
    LST = sb("LST", [128, 4])
    LT = sb("LT", [128, 2])
    LRS = sb("LRS", [128, 2])
    LNMR = sb("LNMR", [128, 2])
    BSHI = sb("BSHI", [4, 128], BF16)
    BSLO = sb("BSLO", [4, 128], BF16)
    BSHIS = sb("BSHIS", [4, 64], BF16)
    BSLOS = sb("BSLOS", [4, 64], BF16)
    BSTMP = sb("BSTMP", [4, 128])
    TB = [sb("TB%d" % j, [128, TBSZ[j]]) if j not in (3, 4) else None for j in range(NTB)]
    TB34 = sb("TB34", [128, TBSZ[3] + TBSZ[4]])
    TB[3] = TB34[:, 0:TBSZ[3]]
    TB[4] = TB34[:, TBSZ[3]:TBSZ[3] + TBSZ[4]]
    BFB = [sb("BFB%d" % j, [128, 1024], BF16) for j in range(3)]
    XEB = sb("XEB", [128, 1248], BF16)
    WP4 = sb("WP4", [128, 8, 16, 32], BF16)
    CDW4 = sb("CDW4", [128, 2, 8, 16])
    IDREP = sb("IDREP", [128, 32])
    IDB = sb("IDB", [128, 128], BF16)
    SEL = sb("SEL", [4, 4, 128], BF16)
    STG = sb("STG", [128, 2, 512])
    PP = es.enter_context(nc.psum_tensor("PP", [128, 2, 1024], F32))
    PMX = es.enter_context(nc.psum_tensor("PMX", [128, 1024], F32))
    PS = es.enter_context(nc.psum_tensor("PS", [128, 2, 512], F32))
    IDENT = sb("IDENT", [128, 128])
    TRIL = sb("TRIL", [128, 128])
    BMASK = sb("BMASK", [64, 64])
    INVC = sb("INVC", [128, 16])
    ONESB = sb("ONESB", [128, 128], BF16)
    MHALF = sb("MHALF", [128, 2])
    GPRE = sb("GPRE", [128, 4, 8])
    GPOST = sb("GPOST", [128, 4, 8])
    CAW = sb("CAW", [128, 2, 3, 4])
    PSC = sb("PSC", [128, 2, 4])
    CDB = sb("CDB", [128, 2, 4])
    CLG = sb("CLG", [128, 2, 4])
    CLB = sb("CLB", [128, 2, 4])
    PW = sb("PW", [128, 4, 128], BF16)
    GB = sb("GB", [128, 512])
    BB = sb("BB", [128, 512])
    WMT = sb("WMT", [128, 2, 4, 128], BF16)
    WMTS = sb("WMTS", [64, 2, 4, 64], BF16)
    WSF = sb("WSF", [64, 4, 64])
    BSROW = sb("BSROW", [4, 128])
    BSROWS = sb("BSROWS", [4, 64])
    SH = sb("SH", [128, 2, 4, 2])
    PH = sb("PH", [128, 2, 4, 15])
    GH = sb("GH", [128, 2, 4, 30])
    BST = sb("BST", [128, 2, 6])
    MV = sb("MV", [128, 2, 2])
    VE = sb("VE", [128, 2])
    RS = sb("RS", [128, 2])
    NMR = sb("NMR", [128, 2])
    FIX = sb("FIX", [128, 16])
    CMP = sb("CMP", [128, 4, 64])
    GST = CMP
    DGA = sb("DGA", [128, 2, 128])

    tk = {}

    def T(name):
        if name not in tk:
            tk[name] = Tk(name)
        return tk[name]

    XT = lambda xc: T("X%d" % xc)
    TBT = [T("TB%d" % j) for j in range(NTB)]
    BFT = [T("BFB%d" % j) for j in range(3)]
    PPT = [T("PP0"), T("PP1")]
    PST = [T("PS0"), T("PS1")]
    HT = [T("H0"), T("H1")]
    YT = [T("Y0"), T("Y1")]
    YTA, YTB = T("YA"), T("YB")
    STGT = [T("STG0"), T("STG1")]
    RINGT = [T("RING%d" % j) for j in range(NSLOT)]
    rr = {"pp": 0, "ps": 0, "h": 0, "y": 0, "stg": 0, "ms": 0}

    def nxt(k, m=2):
        v = rr[k]
        rr[k] = (v + 1) % m
        return v

    def act_copy(out, in_, reads, writes, scale=None, func=None, bias=None):
        def fn():
            kw = {}
            if scale is not None:
                kw["scale"] = scale
            if bias is not None:
                kw["bias"] = bias
            f = func if func is not None else (AF.Copy if (scale is None and bias is None) else AF.Identity)
            return nc.scalar.activation(out=out, in_=in_, func=f, **kw)
        S.op(ACT, fn, reads=reads, writes=writes)

    def dve_tt(out, a, b, op, reads, writes):
        S.op(DVE, lambda: nc.vector.tensor_tensor(out=out, in0=a, in1=b, op=op), reads=reads, writes=writes)

    def dve_stt(out, a, sc, b, op0, op1, reads, writes):
        S.op(DVE, lambda: nc.vector.scalar_tensor_tensor(out=out, in0=a, scalar=sc, in1=b, op0=op0, op1=op1),
             reads=reads, writes=writes)

    def dve_ts(out, a, s1, s2, op0, op1, reads, writes):
        if s2 is None:
            S.op(DVE, lambda: nc.vector.tensor_scalar(out=out, in0=a, scalar1=s1, scalar2=None, op0=op0),
                 reads=reads, writes=writes)
        else:
            S.op(DVE, lambda: nc.vector.tensor_scalar(out=out, in0=a, scalar1=s1, scalar2=s2, op0=op0, op1=op1),
                 reads=reads, writes=writes)

    def dve_copy(out, in_, reads, writes):
        S.op(DVE, lambda: nc.vector.tensor_copy(out=out, in_=in_), reads=reads, writes=writes)

    def pool_pow(out, in_, n, reads, writes):
        S.op(POOL, lambda: nc.gpsimd.tensor_tensor(out=out, in0=in_, in1=MHALF[0:in_.shape[0], 0:n], op=ALU.pow),
             reads=list(reads) + [T("MHALF")], writes=writes)

    def cload(dst_ap, src_ap, name, q="sp", **kw):
        S.dma(q, dst_ap, src_ap, writes=[T(name)], **kw)

    groups = {}
    gcount = [0]

    def group_dma(items, base, q="sp"):
        tiles = [T(base)]
        for k, (dst, src, kw) in enumerate(items):
            gcount[0] += 1
            u = T("%s_u%d" % (base, gcount[0]))
            if k == 0:
                S.dma(q, dst, src, writes=[T(base), u], **kw)
            else:
                S.dma(q, dst, src, reads=[T(base)], writes=[u], **kw)
            tiles.append(u)
        groups[base] = tiles
        return tiles

    NCD = dict(allow_slow_non_contiguous=True)
    cload(IDENT[:, :], c_ident, "IDENT")
    cload(GPRE[:, :, :], p_gpre, "GPRE")
    dve_copy(IDB[:, :], IDENT[:, :], [T("IDENT")], [T("IDB")])
    S.op(DVE, lambda: nc.vector.memset(XEB[:, :], 0.0), writes=[T("XEB")])

    def late_consts():
        cload(GPOST[:, :, :], p_gpost, "GPOST")
        cload(CAW[:, :, :, :], p_caw, "CAW")
        cload(PSC[:, :, :], p_psc, "PSC")
        cload(INVC[:, :], c_invc, "INVC")
        cload(TRIL[:, :], c_tril, "TRIL")
        cload(BMASK[:, :], c_bmask, "BMASK")
        cload(IDREP[:, :], c_idrep, "IDREP")
        cload(SEL[:, :, :], c_sel, "SEL", q="pool")
        cload(CDB[:, :, :], p_cdb, "CDB")
        cload(CLG[:, :, :], p_clg, "CLG")
        cload(CLB[:, :, :], p_clb, "CLB")
        cload(CDW4[:, :, :, :], p_cdw4, "CDW4")
        groups["CDW4"] = [T("CDW4")]

    def load_params(l):
        par, i = l % 2, l // 2
        if par == 0:
            cload(PW[:, :, :], pool_w[i].rearrange("g c d -> c g d"), "PW", q="pool")
        else:
            cload(GB[:, :], sgu_ln_g[i:i + 1, :].partition_broadcast(128)[:, 0, :], "GB")
            cload(BB[:, :], sgu_ln_b[i:i + 1, :].partition_broadcast(128)[:, 0, :], "BB")
            cload(BSROW[:, :], sgu_b[i], "BSROW")
            cload(BSROWS[:, :], p_bsrows[i], "BSROWS")
            bst = [T("BSROWS")]
            for (src, srct, hi, lo, w) in ((BSROW, [T("BSROW")], BSHI, BSLO, 128), (BSROWS, bst, BSHIS, BSLOS, 64)):
                dve_copy(hi[:, :], src[:, :], srct, [T("BSHL")])
                dve_tt(BSTMP[:, 0:w], src[:, :], hi[:, :], ALU.subtract, srct + [T("BSHL")], [T("BSTMP")])
                dve_copy(lo[:, :], BSTMP[:, 0:w], [T("BSTMP")], [T("BSHL")])
            for m_ in range(8):
                S.op(POOL, lambda m_=m_: nc.gpsimd.tensor_tensor(
                    out=WP4[:, m_, :, :], in0=IDREP[:, :].unsqueeze(1).broadcast_to([128, 16, 32]),
                    in1=CDW4[:, i, m_, :].unsqueeze(2).broadcast_to([128, 16, 32]), op=ALU.mult),
                    reads=[T("IDREP")] + groups["CDW4"], writes=[T("WP4")])

    S.op(DVE, lambda: nc.vector.memset(ONESB[:, :], 1.0), writes=[T("ONESB")])
    S.op(DVE, lambda: nc.vector.memset(ONESF2[:, :], 1.0), writes=[T("ONESF2")])
    S.op(DVE, lambda: nc.vector.memset(MHALF[:, :], -0.5), writes=[T("MHALF")])
    S.op(DVE, lambda: nc.vector.memset(SH[:, :, :, :], 0.0), writes=[T("SH0"), T("SH1")])
    S.op(DVE, lambda: nc.vector.memset(PH[:, :, :, :], 0.0), writes=[T("PH0"), T("PH1")])
    S.op(DVE, lambda: nc.vector.memset(GH[:, :, :, :], 0.0), writes=[T("GH0"), T("GH1")])

    chunks = []
    ORDER = {0: [4, 1, 2, 3, 0, 5], 1: [3, 4, 1, 2, 0, 5]}
    for ps in range(2):
        for l in range(4):
            par, i = l % 2, l // 2
            wi = w_in[par][i].rearrange("(k p) f -> p k f", p=128)
            wo = w_out[par][i].rearrange("(k p) f -> p k f", p=128)
            for part in ORDER[par]:
                chunks.append(wi[:, :, part * 512:(part + 1) * 512])
            for hf in range(2):
                chunks.append(wo[:, :, hf * 512:(hf + 1) * 512])
    nchunk_loaded = [0]
    rep_no = [0]

    def load_chunk():
        n = nchunk_loaded[0]
        if n >= len(chunks):
            return
        nchunk_loaded[0] += 1
        slot = n % NSLOT
        S.dma("pool", RING[:, slot, :, :], chunks[n], writes=[RINGT[slot]])

    for _ in range(NSLOT):
        load_chunk()

    def sgu_setup():
        for i in range(2):
            for g in range(4):
                tb = TB[g % 2]
                tbt = TBT[g % 2]
                S.dma("sp", tb[:, 0:128], sgu_w[i, g], writes=[tbt])
                dve_tt(tb[:, 0:128], tb[:, 0:128], TRIL[:, :], ALU.mult, [tbt, T("TRIL")], [tbt])
                pm = nxt("ps")
                S.op(PE, lambda tb=tb, pm=pm: nc.tensor.transpose(out=PS[:, pm, 0:128], in_=tb[:, 0:128], identity=IDENT[:, :]),
                     reads=[tbt, T("IDENT")], writes=[PST[pm]])
                act_copy(WMT[:, i, g, :], PS[:, pm, 0:128], [PST[pm]], [T("WMT")])
            cload(WSF[:, :, :], p_wsf[i], "WSF")
            wt = [T("WSF")]
            for g in range(4):
                dve_tt(WMTS[:, i, g, :], WSF[:, g, :], BMASK[:, :], ALU.mult, wt + [T("BMASK")], [T("WMTS")])

    def load_x_tile(src_rows, xc, j, xts):
        tb, tbt = TB[j % 8], TBT[j % 8]
        S.dma("sp", tb[:, 0:1024], src_rows, writes=[tbt])
        pp = nxt("pp")

        def fn():
            ins = None
            for k in range(8):
                ins = nc.tensor.transpose(out=PP[:, pp, k * 128:(k + 1) * 128], in_=tb[:, k * 128:(k + 1) * 128],
                                          identity=IDENT[:, :])
            return ins
        S.op(PE, fn, reads=[tbt, T("IDENT")], writes=[PPT[pp]])
        src = PP[:, pp, :].rearrange("p (k t) -> p k t", k=8)
        if j % 2 == 0:
            act_copy(X[:, :, xc:xc + 128], src, [PPT[pp]], list(xts))
        else:
            dve_copy(X[:, :, xc:xc + 128], src, [PPT[pp]], list(xts))

    def store_x_tile(dst_rows, xc, j, xts):
        pp = nxt("pp")

        def fn():
            ins = None
            for k in range(8):
                ins = nc.tensor.transpose(out=PP[:, pp, k * 128:(k + 1) * 128], in_=X[:, k, xc:xc + 128],
                                          identity=IDENT[:, :])
            return ins
        S.op(PE, fn, reads=list(xts) + [T("IDENT")], writes=[PPT[pp]])
        tb, tbt = TB[j % 8], TBT[j % 8]
        if j % 2 == 0:
            act_copy(tb[:, 0:1024], PP[:, pp, :], [PPT[pp]], [tbt])
        else:
            dve_copy(tb[:, 0:1024], PP[:, pp, :], [PPT[pp]], [tbt])
        S.dma("sp", dst_rows, tb[:, 0:1024], reads=[tbt])

    def rows_out(src_fn, nrows, dst_ap, reads, src4=None):
        if src4 is not None:
            act_copy(CMP[:, :, 0:nrows].rearrange("p g (s j) -> p g s j", s=src4.shape[2]), src4, list(reads), [T("CMP")])
            src_fn = lambda g: CMP[:, g, 0:nrows]
            reads = [T("CMP")]
        pm = nxt("ps")

        def fn():
            ins = None
            for g in range(4):
                ins = nc.tensor.transpose(out=PS[0:nrows, pm, g * 128:(g + 1) * 128], in_=src_fn(g), identity=IDENT[:, :])
            return ins
        S.op(PE, fn, reads=list(reads) + [T("IDENT")], writes=[PST[pm]])
        sg = nxt("stg")
        act_copy(STG[0:nrows, sg, :], PS[0:nrows, pm, 0:512], [PST[pm]], [STGT[sg]])
        S.dma("sp", dst_ap, STG[0:nrows, sg, :], reads=[STGT[sg]])

    def hist_in(src_ap, nrows, dst_view, nsq, J, wtile):
        sg = nxt("stg")
        S.dma("sp", STG[0:nrows, sg, :], src_ap, writes=[STGT[sg]])
        pm = nxt("ps")

        def fn():
            ins = None
            for g in range(4):
                ins = nc.tensor.transpose(out=PS[:, pm, g * nrows:(g + 1) * nrows], in_=STG[0:nrows, sg, g * 128:(g + 1) * 128],
                                          identity=IDENT[0:nrows, 0:nrows])
            return ins
        S.op(PE, fn, reads=[STGT[sg], T("IDENT")], writes=[PST[pm]])
        act_copy(dst_view, PS[:, pm, 0:4 * nrows].rearrange("p (g s j) -> p g s j", g=4, s=nsq),
                 [PST[pm]], [wtile])

    def bcast_dg(cols, m, ntile, reads):
        for j, col in enumerate(cols):
            for ti in range(ntile):
                S.op(DVE, lambda j=j, ti=ti, col=col: nc.vector.tensor_scalar(
                    out=DG[0:m, j, ti, 0:m], in0=IDENT[0:m, 0:m], scalar1=col[:, ti:ti + 1], scalar2=None, op0=ALU.mult),
                    reads=list(reads) + [T("IDENT")], writes=[T("DG")])

    def bcast_pe(ps, ncols, m, ntile):
        def fn():
            ins = None
            for j in range(ncols):
                for ti in range(ntile):
                    ins = nc.tensor.matmul(PS[:, ps, j * 256 + ti * m:j * 256 + (ti + 1) * m], lhsT=ONESF2[0:m, :],
                                           rhs=DG[0:m, j, ti, 0:m], start=True, stop=True)
            return ins
        S.op(PE, fn, reads=[T("DG"), T("ONESF2")], writes=[PST[ps]])

    def stats_rstd_a(sq_n, reads_sq, n, own=False):
        m = min(n, 128)
        ntile = n // m
        ps = nxt("ps")

        def fn():
            ins = None
            for ti in range(ntile):
                for k in range(8):
                    ins = nc.tensor.matmul(PS[0:m, ps, 508 + ti:509 + ti], lhsT=SQ[:, k, ti * m:(ti + 1) * m], rhs=ONESB[:, 0:1],
                                           start=(k == 0), stop=(k == 7))
            return ins
        S.op(PE, fn, reads=list(reads_sq) + [T("ONESB")], writes=[PST[ps]])
        act_copy(MS[0:m, 0:ntile], PS[0:m, ps, 508:508 + ntile], [PST[ps]], [T("MS")], scale=1.0 / sq_n, bias=EPS)
        S.op(POOL, lambda: nc.gpsimd.tensor_tensor(out=RSM[0:m, 0:ntile], in0=MS[0:m, 0:ntile], in1=MHALF[0:m, 0:ntile], op=ALU.pow),
             reads=[T("MS"), T("MHALF")], writes=[T("RSM")])
        dgv = DGA if own else DG[:, 0, :, :]
        dgt = T("DGA") if own else T("DG")
        for ti in range(ntile):
            S.op(DVE, lambda ti=ti: nc.vector.tensor_scalar(
                out=dgv[0:m, ti, 0:m], in0=IDENT[0:m, 0:m], scalar1=RSM[0:m, ti:ti + 1], scalar2=None, op0=ALU.mult),
                reads=[T("RSM"), T("IDENT")], writes=[dgt])
        return (ps, m, ntile, dgv, dgt)

    def stats_rstd_b(ctx):
        ps, m, ntile, dgv, dgt = ctx
        if dgt is T("DGA"):
            ps = nxt("ps")

        def fn():
            ins = None
            for ti in range(ntile):
                ins = nc.tensor.matmul(PS[:, ps, ti * m:(ti + 1) * m], lhsT=ONESF2[0:m, :], rhs=dgv[0:m, ti, 0:m], start=True, stop=True)
            return ins
        S.op(PE, fn, reads=[dgt, T("ONESF2")], writes=[PST[ps]])
        return ps

    class Step:
        pass

    hookq = []

    def run_hook():
        if hookq:
            hookq.pop(0)()

    def make_step(l, blk, wslots, last_in_layer):
        st = Step()
        par, i = l % 2, l // 2
        n, nseq, Tt = blk.n, blk.nseq, blk.T
        xc = blk.xc
        xt = XT(xc)
        hs_box = [None]
        Yv = YB[:, 0, :, 0:n]
        ys = 0

        abox = {}

        def phase_Aa():
            S.op(ACT, lambda: nc.scalar.activation(out=SQ[:, :, 0:n], in_=X[:, :, xc:xc + n], func=AF.Square),
                 reads=[xt], writes=[T("SQ")])
            abox["ctx"] = stats_rstd_a(1024.0, [T("SQ")], n, own=True)

        def phase_Ab():
            psr = stats_rstd_b(abox["ctx"])
            hs = nxt("h")
            hs_box[0] = hs
            for k in range(8):
                dve_stt(H[:, hs, k, 0:n], X[:, k, xc:xc + n], GPRE[:, l, k:k + 1], PS[:, psr, 0:n], ALU.mult, ALU.mult,
                        [xt, T("GPRE"), PST[psr]], [HT[hs]])

        def phase_A():
            phase_Aa()
            phase_Ab()

        def after_use():
            if last_in_layer:
                load_chunk()

        def inproj_fm(c):
            hs = hs_box[0]
            slot = wslots[c]
            pp = nxt("pp")
            view = PP[:, pp, 0:4 * n].rearrange("p (g t) -> p g t", g=4)

            def fn():
                ins = None
                for g in range(4):
                    for k in range(8):
                        ins = nc.tensor.matmul(view[:, g, :], lhsT=RING[:, slot, k, g * 128:(g + 1) * 128], rhs=H[:, hs, k, 0:n],
                                               start=(k == 0), stop=(k == 7))
                return ins
            S.op(PE, fn, reads=[RINGT[slot], HT[hs]], writes=[PPT[pp]])
            after_use()
            return pp, view

        def v4(ap):
            return ap.rearrange("p g (s t) -> p g s t", s=nseq)

        pmx4 = PMX[:, 0:4 * n].rearrange("p (g t) -> p g t", g=4)
        sl8 = slice(8 * blk.sh, 8 * blk.sh + 8)

        if par == 0:
            E = 15 + Tt
            E2 = 2 + Tt
            PX = TB[3][:, 0:4 * nseq * E].rearrange("p (g s e) -> p g s e", g=4, s=nseq)
            SX = TB[1][:, 0:4 * nseq * E2].rearrange("p (g s e) -> p g s e", g=4, s=nseq)
            Df = BFB[0][:, 0:4 * n].rearrange("p (g t) -> p g t", g=4)
            G2 = TB[6][:, 0:4 * n].rearrange("p (g t) -> p g t", g=4)

            def phase_B1():
                pp, view = inproj_fm(0)
                if blk.kind == "p":
                    dve_copy(PX[:, :, 0, 0:15], PH[:, i, :, :], [T("PH%d" % i)], [TBT[3]])
                else:
                    hist_in(st_b[i, sl8].rearrange("s j c -> (s j) c"), 120, PX[:, :, :, 0:15], 8, 15, TBT[3])
                    S.dma("sp", pb_s[i, sl8, 0:7, :], st_b[i, sl8, 8:15, :])
                act_copy(PX[:, :, :, 15:E], v4(view), [PPT[pp]], [TBT[3]])
                if blk.kind == "p" and not blk.last:
                    dve_copy(PH[:, i, :, :], PX[:, :, 0, Tt:Tt + 15], [TBT[3]], [T("PH%d" % i)])
                if blk.last:
                    rows_out(lambda g: PX[:, g, 0, Tt:Tt + 15], 15, pb_p[i], [TBT[3]])
                if blk.kind == "s":
                    rows_out(None, 64, pb_s[i, sl8, 7:15, :], [TBT[3]], src4=PX[:, :, :, 15:E])
                A1 = TB[4][:, 0:4 * nseq * E].rearrange("p (g s e) -> p g s e", g=4, s=nseq)
                A2 = TB[5][:, 0:4 * nseq * E].rearrange("p (g s e) -> p g s e", g=4, s=nseq)
                dve_tt(A1[:, :, :, 1:E], PX[:, :, :, 1:E], PX[:, :, :, 0:E - 1], ALU.add, [TBT[3]], [TBT[4]])
                dve_tt(A2[:, 1:4, :, 3:E], A1[:, 1:4, :, 3:E], A1[:, 1:4, :, 1:E - 2], ALU.add, [TBT[4]], [TBT[5]])
                dve_tt(A1[:, 2:4, :, 7:E], A2[:, 2:4, :, 7:E], A2[:, 2:4, :, 3:E - 4], ALU.add, [TBT[5]], [TBT[4]])
                dve_tt(A2[:, 3, :, 15:E], A1[:, 3, :, 15:E], A1[:, 3, :, 7:E - 8], ALU.add, [TBT[4]], [TBT[5]])
                WIN = [A1[:, 0], A2[:, 1], A1[:, 2], A2[:, 3]]
                wint = [TBT[4], TBT[5], TBT[4], TBT[5]]
                Dv = BFB[0][:, 0:4 * n].rearrange("p (g s t) -> p g s t", g=4, s=nseq)
                for g in range(4):
                    dve_stt(Dv[:, g], WIN[g][:, :, 15:E], 1.0 / POOL_W[g], PX[:, g, :, 15:E], ALU.mult, ALU.subtract,
                            [wint[g], TBT[3]], [BFT[0]])
                if blk.first:
                    for g in range(4):
                        w = POOL_W[g]
                        dve_tt(FIX[:, 0:w - 1], WIN[g][:, 0, 15:15 + w - 1], INVC[:, 0:w - 1], ALU.mult,
                               [wint[g], T("INVC")], [T("FIX")])
                        dve_tt(Dv[:, g, 0, 0:w - 1], FIX[:, 0:w - 1], PX[:, g, 0, 15:15 + w - 1], ALU.subtract,
                               [T("FIX"), TBT[3]], [BFT[0]])
                run_hook()
                pp, view = inproj_fm(1)
                ZC = TB[0][:, 0:4 * n].rearrange("p (g t) -> p g t", g=4)
                act_copy(ZC, view, [PPT[pp]], [TBT[0]])
                if blk.kind == "p":
                    dve_copy(SX[:, :, 0, 0:2], SH[:, i, :, :], [T("SH%d" % i)], [TBT[1]])
                else:
                    hist_in(st_a[i, sl8].rearrange("s j c -> (s j) c"), 16, SX[:, :, :, 0:2], 8, 2, TBT[1])
                pp, view = inproj_fm(2)
                dve_tt(SX[:, :, :, 2:E2], v4(ZC), v4(view), ALU.mult, [TBT[0], PPT[pp]], [TBT[1]])
                if blk.kind == "p" and not blk.last:
                    dve_copy(SH[:, i, :, :], SX[:, :, 0, Tt:Tt + 2], [TBT[1]], [T("SH%d" % i)])
                if blk.last:
                    rows_out(lambda g: SX[:, g, 0, Tt:Tt + 2], 2, ca_p[i], [TBT[1]])
                if blk.kind == "s":
                    rows_out(None, 16, ca_s[i, sl8].rearrange("s j c -> (s j) c"), [TBT[1]], src4=SX[:, :, :, Tt:Tt + 2])
                run_hook()

            def phase_B2():
                Y3 = TB[0][:, 0:4 * n].rearrange("p (g s t) -> p g s t", g=4, s=nseq)
                y3t = [T("Y3_%d" % g) for g in range(4)]
                for k in range(3):
                    for g in range(4):
                        if k == 0:
                            dve_ts(Y3[:, g], SX[:, g, :, 0:Tt], CAW[:, i, 0, g:g + 1], None, ALU.mult, None,
                                   [TBT[1], T("CAW")], [TBT[0], y3t[g]] if g == 0 else [y3t[g]])
                        else:
                            dve_stt(Y3[:, g], SX[:, g, :, k:k + Tt], CAW[:, i, k, g:g + 1], Y3[:, g], ALU.mult, ALU.add,
                                    [TBT[1], T("CAW"), y3t[g]], [y3t[g]])
                pp, view = inproj_fm(3)
                G = TB[2][:, 0:4 * n].rearrange("p (g t) -> p g t", g=4)
                act_copy(G, view, [PPT[pp]], [TBT[2]], func=AF.Silu)
                pp, view = inproj_fm(4)
                dve_tt(G, view, G, ALU.mult, [PPT[pp], TBT[2]], [TBT[2]])
                dve_tt(v4(Yv[:, 0:4, :]), Y3, v4(G), ALU.mult, [TBT[2]] + y3t, [YTA, TBT[0]])
                def fnm():
                    ins = None
                    for g in range(4):
                        ins = nc.tensor.matmul(pmx4[:, g, :], lhsT=PW[:, g, :], rhs=Df[:, g, :], start=True, stop=True)
                    return ins
                S.op(PE, fnm, reads=[BFT[0], T("PW")], writes=[T("PMX")])
                pp, view = inproj_fm(5)
                t6h = [T("T6a"), T("T6b")]
                for hh in range(2):
                    gs = slice(2 * hh, 2 * hh + 2)
                    act_copy(G2[:, gs, :], view[:, gs, :], [PPT[pp]], [TBT[6], t6h[hh]] if hh == 0 else [t6h[hh]], func=AF.Silu)
                for g in range(4):
                    last = (g == 3)
                    dve_stt(Yv[:, 4 + g, :], pmx4[:, g, :], PSC[:, i, g:g + 1], G2[:, g, :], ALU.mult, ALU.mult,
                            [T("PMX"), T("PSC"), t6h[g // 2]], [YTB, TBT[6]] if last else [T("YBh0")])
                run_hook()
        else:
            m = 128 if blk.kind == "p" else 64
            ntile = n // m
            EL = 31 + Tt
            RL = 28 + Tt
            XE = XEB[:, 0:4 * nseq * EL].rearrange("p (g s e) -> p g s e", g=4, s=nseq)
            xet = T("XEB")
            RV = TB34[:, :].bitcast(BF16).rearrange("p (q s e) -> p q s e", q=16, s=nseq)
            YN = TB[5][:, 0:4 * n].rearrange("p (g t) -> p g t", g=4)
            YBF = BFB[1][:, 0:4 * n].rearrange("p (g t) -> p g t", g=4)
            YSQ = BFB[2][:, 0:4 * n].rearrange("p (g t) -> p g t", g=4)
            VN = BFB[0][:, 0:1024].rearrange("p (t c) -> p t c", t=2)
            G = TB[2][:, 0:4 * n].rearrange("p (g t) -> p g t", g=4)
            mL = min(n, 128)
            ntL = n // mL
            box = {}

            def phase_B1():
                ppa, viewa = inproj_fm(0)
                ZA = TB[0][:, 0:4 * n].rearrange("p (g t) -> p g t", g=4)
                act_copy(ZA, viewa, [PPT[ppa]], [TBT[0]], scale=0.5)
                run_hook()
                ppb, viewb = inproj_fm(1)
                TH = TB[6][:, 0:4 * n].rearrange("p (g t) -> p g t", g=4)
                act_copy(TH, viewb, [PPT[ppb]], [TBT[6]], scale=0.5, func=AF.Tanh)
                if blk.kind == "p":
                    dve_copy(XE[:, :, 0, 0:30], GH[:, i, :, :], [T("GH%d" % i)], [xet])
                else:
                    for q in range(2):
                        sl = slice(8 * blk.sh + 4 * q, 8 * blk.sh + 4 * q + 4)
                        hist_in(st_d[i, sl].rearrange("s j c -> (s j) c"), 120, XE[:, :, 4 * q:4 * q + 4, 0:30], 4, 30, xet)
                    S.dma("sp", cd_s[i, sl8, 0:22, :], st_d[i, sl8, 8:30, :])
                dve_stt(XE[:, :, :, 30:30 + Tt], v4(TH), 1.0, v4(ZA), ALU.add, ALU.mult, [TBT[6], TBT[0]], [xet])
                if blk.kind == "p":
                    dve_stt(GH[:, i, :, :], TH[:, :, Tt - 30:Tt], 1.0, ZA[:, :, Tt - 30:Tt], ALU.add, ALU.mult,
                            [TBT[6], TBT[0]], [T("GH%d" % i)])
                    if blk.last:
                        rows_out(lambda g: GH[:, i, g, :], 30, cd_p[i], [T("GH%d" % i)])
                else:
                    dve_stt(GST[:, :, :], TH, 1.0, ZA, ALU.add, ALU.mult, [TBT[6], TBT[0]], [T("CMP")])
                    rows_out(lambda g: GST[:, g, :], 64, cd_s[i, sl8, 22:30, :], [T("CMP")])
                rts = [T("R%d" % z) for z in range(8)]
                for z in range(8):
                    if z % 3 == 2:
                        ptile, pview = T("PMX"), PMX[:, :]
                    else:
                        pp = nxt("pp")
                        ptile, pview = PPT[pp], PP[:, pp, :]

                    def fnrep(z=z, pview=pview):
                        ins = None
                        for dq in range(2):
                            q = 2 * z + dq
                            J, j = q // 4, q % 4
                            for s_ in range(4):
                                rhs = XE[:, J, 0, s_:s_ + RL] if nseq == 1 else XE[:, J, :, s_:s_ + RL]
                                ins = nc.tensor.matmul(pview[32 * s_:32 * s_ + 32, dq * 512:dq * 512 + nseq * RL],
                                                       lhsT=IDB[:, 32 * j:32 * j + 32], rhs=rhs, start=True, stop=True,
                                                       tile_position=(0, 32 * s_))
                        return ins
                    S.op(PE, fnrep, reads=[xet, T("IDB")], writes=[ptile])
                    src = pview.rearrange("p (a c) -> p a c", a=2)[:, :, 0:nseq * RL].rearrange("p a (s e) -> p a s e", s=nseq)
                    dst = RV[:, 2 * z:2 * z + 2, :, 0:RL]
                    wr = [TBT[3], TBT[4], rts[z]] if z == 0 else [rts[z]]
                    rdx = [ptile] if z == 0 else [ptile, TBT[3]]
                    if z % 3 != 1:
                        act_copy(dst, src, rdx, wr)
                    else:
                        dve_copy(dst, src, rdx, wr)

                def fnconv():
                    ins = None
                    for J in range(4):
                        for m_ in range(8):
                            for j in range(4):
                                if nseq == 1:
                                    rhs = RV[:, 4 * J + j, 0, 4 * m_:4 * m_ + Tt]
                                else:
                                    rhs = RV[:, 4 * J + j, :, 4 * m_:4 * m_ + Tt]
                                ins = nc.tensor.matmul(pmx4[32 * j:32 * j + 32, J, :], lhsT=WP4[:, m_, 4 * J + j, :], rhs=rhs,
                                                       start=(m_ == 0), stop=(m_ == 7), tile_position=(0, 32 * j))
                    return ins
                S.op(PE, fnconv, reads=rts + [T("WP4")], writes=[T("PMX"), TBT[3], TBT[4]])
                run_hook()
                for g in range(4):
                    act_copy(YN[:, g, :], pmx4[:, g, :], [T("PMX"), T("CDB")], [TBT[5]], bias=CDB[:, i, g:g + 1])
                act_copy(YBF, YN, [TBT[5]], [BFT[1]])
                act_copy(YSQ, YN, [TBT[5]], [BFT[2]], func=AF.Square)
                slot = wslots[2]
                hs = hs_box[0]
                pp = nxt("pp")

                def fnv():
                    ins = None
                    for ti in range(ntile):
                        for k in range(8):
                            ins = nc.tensor.matmul(PP[0:m, pp, ti * 512:(ti + 1) * 512], lhsT=H[:, hs, k, ti * m:(ti + 1) * m],
                                                   rhs=RING[:, slot, k, :], start=(k == 0), stop=(k == 7))
                    return ins
                S.op(PE, fnv, reads=[RINGT[slot], HT[hs]], writes=[PPT[pp]])
                after_use()
                for ti in range(ntile):
                    S.op(DVE, lambda ti=ti: nc.vector.bn_stats(out=BST[0:m, ti, :], in_=PP[0:m, pp, ti * 512:(ti + 1) * 512]),
                         reads=[PPT[pp]], writes=[T("BST%d" % ti)])
                    S.op(DVE, lambda ti=ti: nc.vector.bn_aggr(out=MV[0:m, ti, :], in_=BST[0:m, ti, :]),
                         reads=[T("BST%d" % ti)], writes=[T("MV%d" % ti)])
                mvt = [T("MV%d" % ti) for ti in range(ntile)]
                S.op(POOL, lambda: nc.gpsimd.tensor_scalar(out=VE[0:m, 0:ntile], in0=MV[0:m, 0:ntile, 1], scalar1=EPS, scalar2=None,
                                                            op0=ALU.add), reads=mvt, writes=[T("VE")])
                S.op(POOL, lambda: nc.gpsimd.tensor_tensor(out=RS[0:m, 0:ntile], in0=VE[0:m, 0:ntile], in1=MHALF[0:m, 0:ntile],
                                                            op=ALU.pow), reads=[T("VE"), T("MHALF")], writes=[T("RS")])
                dve_stt(NMR[0:m, 0:ntile], MV[0:m, 0:ntile, 0], -1.0, RS[0:m, 0:ntile], ALU.mult, ALU.mult,
                        mvt + [T("RS")], [T("NMR")])
                NT = TB[1][:, 0:1024].rearrange("p (t c) -> p t c", t=2)
                for ti in range(ntile):
                    act_copy(NT[0:m, ti, :], PP[0:m, pp, ti * 512:(ti + 1) * 512], [PPT[pp], T("RS"), T("NMR")], [TBT[1]],
                             scale=RS[0:m, ti:ti + 1], bias=NMR[0:m, ti:ti + 1])
                gbb = GB[0:m, :].unsqueeze(1).broadcast_to([m, ntile, 512])
                bbb = BB[0:m, :].unsqueeze(1).broadcast_to([m, ntile, 512])
                dve_tt(NT[0:m, 0:ntile, :], NT[0:m, 0:ntile, :], gbb, ALU.mult, [TBT[1], T("GB")], [TBT[1]])
                if blk.kind == "p":
                    dve_tt(VN[0:m, 0:ntile, :], NT[0:m, 0:ntile, :], bbb, ALU.add, [TBT[1], T("BB")], [BFT[0]])
                else:
                    dve_tt(NT[0:m, 0:1, :], NT[0:m, 0:1, :], bbb, ALU.add, [TBT[1], T("BB")], [TBT[1]])
                    S.dma("sp", v_s[i, sl8].rearrange("s t c -> (s t) c"), NT[0:m, 0, :], reads=[TBT[1]])
                    act_copy(VN[0:m, 0:1, :], NT[0:m, 0:1, :], [TBT[1]], [BFT[0]])
                psa = nxt("ps")
                box["psa"] = psa

                def fnln():
                    ins = None
                    for ti in range(ntL):
                        for j, src in enumerate((YBF, YSQ)):
                            for g in range(4):
                                ins = nc.tensor.matmul(PS[0:mL, psa, 2 * j + ti:2 * j + ti + 1], lhsT=src[:, g, ti * mL:(ti + 1) * mL],
                                                       rhs=ONESB[:, 0:1], start=(g == 0), stop=(g == 3))
                    return ins
                S.op(PE, fnln, reads=[BFT[1], BFT[2], T("ONESB")], writes=[PST[psa]])
                act_copy(LST[0:mL, 0:4], PS[0:mL, psa, 0:4], [PST[psa]], [T("LST")], scale=1.0 / 512)
                dve_tt(LT[0:mL, 0:ntL], LST[0:mL, 0:ntL], LST[0:mL, 0:ntL], ALU.mult, [T("LST")], [T("LT")])
                dve_tt(LT[0:mL, 0:ntL], LST[0:mL, 2:2 + ntL], LT[0:mL, 0:ntL], ALU.subtract, [T("LST"), T("LT")], [T("LT")])
                S.op(POOL, lambda: nc.gpsimd.tensor_scalar(out=LT[0:mL, 0:ntL], in0=LT[0:mL, 0:ntL], scalar1=EPS, scalar2=None, op0=ALU.add),
                     reads=[T("LT")], writes=[T("LT")])
                S.op(POOL, lambda: nc.gpsimd.tensor_tensor(out=LRS[0:mL, 0:ntL], in0=LT[0:mL, 0:ntL], in1=MHALF[0:mL, 0:ntL], op=ALU.pow),
                     reads=[T("LT"), T("MHALF")], writes=[T("LRS")])
                dve_stt(LNMR[0:mL, 0:ntL], LST[0:mL, 0:ntL], -1.0, LRS[0:mL, 0:ntL], ALU.mult, ALU.mult,
                        [T("LST"), T("LRS")], [T("LNMR")])
                bcast_dg([LRS[0:mL, 0:ntL], LNMR[0:mL, 0:ntL]], mL, ntL, [T("LRS"), T("LNMR")])

            def phase_B2():
                ppg, viewg = inproj_fm(3)
                act_copy(G, viewg, [PPT[ppg]], [TBT[2]], func=AF.Silu)
                ppu, viewu = inproj_fm(4)
                dve_tt(G, viewu, G, ALU.mult, [PPT[ppu], TBT[2]], [TBT[2]])
                ppd, viewd = inproj_fm(5)
                SGD = TB[7][:, 0:4 * n].rearrange("p (g t) -> p g t", g=4)
                act_copy(SGD, viewd, [PPT[ppd]], [TBT[7]], func=AF.Silu)
                psb = nxt("ps")
                bcast_pe(psb, 2, mL, ntL)
                def fnmix():
                    ins = None
                    for g in range(4):
                        for ti in range(ntile):
                            o = pmx4[:, g, ti * m:(ti + 1) * m]
                            if blk.kind == "p":
                                nc.tensor.matmul(o, lhsT=VN[0:m, ti, g * 128:(g + 1) * 128], rhs=WMT[:, i, g, :], start=True, stop=False)
                                nc.tensor.matmul(o, lhsT=SEL[:, g, :], rhs=BSHI[:, :], start=False, stop=False)
                                ins = nc.tensor.matmul(o, lhsT=SEL[:, g, :], rhs=BSLO[:, :], start=False, stop=True)
                            else:
                                nc.tensor.matmul(o, lhsT=VN[0:m, ti, g * 128:(g + 1) * 128], rhs=WMTS[:, i, g, :], start=True, stop=False)
                                nc.tensor.matmul(o, lhsT=SEL[:, g, :], rhs=BSHIS[:, :], start=False, stop=False)
                                ins = nc.tensor.matmul(o, lhsT=SEL[:, g, :], rhs=BSLOS[:, :], start=False, stop=True)
                    return ins
                S.op(PE, fnmix, reads=[BFT[0], T("WMT"), T("WMTS"), T("SEL"), T("BSHL")], writes=[T("PMX")])
                dve_tt(Yv[:, 0:4, :], pmx4, G, ALU.mult, [T("PMX"), TBT[2]], [YTA])
                rb = PS[:, psb, 0:n].unsqueeze(1).broadcast_to([128, 4, n])
                mb = PS[:, psb, 256:256 + n].unsqueeze(1).broadcast_to([128, 4, n])
                t5h = [T("T5a"), T("T5b")]
                rbh = PS[:, psb, 0:n].unsqueeze(1).broadcast_to([128, 2, n])
                mbh = PS[:, psb, 256:256 + n].unsqueeze(1).broadcast_to([128, 2, n])
                for hh in range(2):
                    gs = slice(2 * hh, 2 * hh + 2)
                    dve_tt(YN[:, gs, :], YN[:, gs, :], rbh, ALU.mult, [TBT[5], PST[psb]], [t5h[hh]])
                    dve_tt(YN[:, gs, :], YN[:, gs, :], mbh, ALU.add, [t5h[hh], PST[psb]], [t5h[hh]])
                    for g in (2 * hh, 2 * hh + 1):
                        act_copy(YN[:, g, :], YN[:, g, :], [t5h[hh], T("CLG"), T("CLB")], [t5h[hh]], func=AF.Silu,
                                 scale=CLG[:, i, g:g + 1], bias=CLB[:, i, g:g + 1])
                dve_tt(Yv[:, 4:6, :], YN[:, 0:2, :], SGD[:, 0:2, :], ALU.mult, [t5h[0], TBT[7]], [T("YBh0")])
                dve_tt(Yv[:, 6:8, :], YN[:, 2:4, :], SGD[:, 2:4, :], ALU.mult, [t5h[1], TBT[7]], [YTB, TBT[5]])
                run_hook()

        def phase_C1():
            slots, views = [], []
            for hf in range(2):
                pp = nxt("pp")
                slots.append(pp)
                views.append(PP[:, pp, 0:4 * n].rearrange("p (g t) -> p g t", g=4))
            for hf in range(2):
                slot, view, pp = wslots[6 + hf], views[hf], slots[hf]

                def fno1(slot=slot, view=view):
                    ins = None
                    for g in range(4):
                        for k in range(4):
                            first = (k == 0) and ((g * n) % 512 == 0)
                            ins = nc.tensor.matmul(view[:, g, :], lhsT=RING[:, slot, k, g * 128:(g + 1) * 128], rhs=YB[:, ys, k, 0:n],
                                                   start=first, stop=False, skip_group_check=True)
                    return ins
                S.op(PE, fno1, reads=[RINGT[slot], YTA], writes=[PPT[pp]])
            for hf in range(2):
                slot, view, pp = wslots[6 + hf], views[hf], slots[hf]

                def fno2(slot=slot, view=view):
                    ins = None
                    for g in range(4):
                        for k in range(4, 8):
                            ins = nc.tensor.matmul(view[:, g, :], lhsT=RING[:, slot, k, g * 128:(g + 1) * 128], rhs=YB[:, ys, k, 0:n],
                                                   start=False, stop=(k == 7), skip_group_check=True)
                    return ins
                S.op(PE, fno2, reads=[RINGT[slot], YTB, T("YBh0"), PPT[pp]], writes=[PPT[pp]])
                after_use()
                act_copy(SQ[:, hf * 4:hf * 4 + 4, 0:n], view, [PPT[pp]], [T("SQ")], func=AF.Square)
                for g in range(4):
                    kk = hf * 4 + g
                    act_copy(OG[:, kk, 0:n], view[:, g, :], [PPT[pp], T("GPOST")], [T("OG")], scale=GPOST[:, l, kk:kk + 1])

        cbox = {}

        def phase_C2a():
            cbox["ctx"] = stats_rstd_a(1024.0, [T("SQ")], n)

        def phase_C2b():
            psr2 = stats_rstd_b(cbox["ctx"])
            rb2 = PS[:, psr2, 0:n].unsqueeze(1).broadcast_to([128, 8, n])
            dve_tt(OG[:, :, 0:n], OG[:, :, 0:n], rb2, ALU.mult, [T("OG"), PST[psr2]], [T("OG")])
            dve_tt(X[:, :, xc:xc + n], X[:, :, xc:xc + n], OG[:, :, 0:n], ALU.add, [xt, T("OG")], [xt])

        def phase_C2():
            phase_C2a()
            phase_C2b()

        st.Aa, st.Ab, st.C2a, st.C2b = phase_Aa, phase_Ab, phase_C2a, phase_C2b
        st.A, st.B1, st.B2, st.C1, st.C2 = phase_A, phase_B1, phase_B2, phase_C1, phase_C2
        return st

    passes = [
        [Blk("s", 1024, 64, 8, 8, sh=0), Blk("s", 1088, 64, 8, 8, sh=1)] +
        [Blk("p", 256 * b, 256, 1, 256, tok0=256 * b) for b in range(4)],
        [Blk("p", 256 * b, 256, 1, 256, tok0=1024 + 256 * b) for b in range(4)],
    ]
    chunk_no = 0
    for ps_, blocks in enumerate(passes):
        j = 0
        if ps_ == 0:
            for tt in range(8):
                load_x_tile(xp[tt * 128:(tt + 1) * 128, :], tt * 128, j, [XT((tt // 2) * 256)])
                j += 1
            load_x_tile(xs[:, :], 1024, j, [XT(1024), XT(1088)])
            j += 1
            late_consts()
            load_params(0)
        else:
            for tt in range(8):
                load_x_tile(xp[1024 + tt * 128:1024 + (tt + 1) * 128, :], tt * 128, j, [XT((tt // 2) * 256)])
                j += 1
        steps = []
        for l in range(4):
            wslots = [(chunk_no + c) % NSLOT for c in range(8)]
            for bi, blk in enumerate(blocks):
                stp = make_step(l, blk, wslots, bi == len(blocks) - 1)
                stp.pre = (lambda l=l: load_params((l + 1) % 4)) if (bi == 0 and not (ps_ == 1 and l == 3)) else None
                if ps_ == 0 and l == 0:
                    stp.pre = (lambda: (sgu_setup(), load_params(1))) if bi == 2 else None
                steps.append(stp)
            chunk_no += 8
        if PIPELINE == 4:
            steps[0].A()
            for si, stp in enumerate(steps):
                if stp.pre is not None:
                    stp.pre()
                prev = steps[si - 1] if si > 0 else None
                nx = steps[si + 1] if si + 1 < len(steps) else None
                if prev is not None:
                    hookq.append(prev.C2a)
                    if nx is not None:
                        hookq.append(lambda prev=prev, nx=nx: (prev.C2b(), nx.Aa()))
                        hookq.append(nx.Ab)
                    else:
                        hookq.append(prev.C2b)
                elif nx is not None:
                    hookq.append(nx.Aa)
                    hookq.append(nx.Ab)
                stp.B1()
                stp.B2()
                while hookq:
                    hookq.pop(0)()
                stp.C1()
            steps[-1].C2()
        elif PIPELINE == 2:
            steps[0].A()
            for si, stp in enumerate(steps):
                if stp.pre is not None:
                    stp.pre()
                stp.B1()
                if si > 0:
                    steps[si - 1].C2()
                if si + 1 < len(steps):
                    steps[si + 1].A()
                stp.B2()
                stp.C1()
            steps[-1].C2()
        elif PIPELINE == 3:
            for si, stp in enumerate(steps):
                if stp.pre is not None:
                    stp.pre()
                stp.A()
                stp.B1()
                if si > 0:
                    steps[si - 1].C2()
                stp.B2()
                stp.C1()
            steps[-1].C2()
        elif PIPELINE == 1:
            steps[0].A()
            for si, stp in enumerate(steps):
                if stp.pre is not None:
                    stp.pre()
                stp.B1()
                if si + 1 < len(steps):
                    steps[si + 1].A()
                stp.B2()
                stp.C1()
                stp.C2()
        else:
            for si, stp in enumerate(steps):
                if stp.pre is not None:
                    stp.pre()
                stp.A()
                stp.B1()
                stp.B2()
                stp.C1()
                stp.C2()
        j = 0
        if ps_ == 0:
            for tt in range(8):
                store_x_tile(y_p[tt * 128:(tt + 1) * 128, :], tt * 128, j, [XT((tt // 2) * 256)])
                j += 1
            store_x_tile(y_s[:, :], 1024, j, [XT(1024), XT(1088)])
        else:
            for tt in range(8):
                store_x_tile(y_p[1024 + tt * 128:1024 + (tt + 1) * 128, :], tt * 128, j, [XT((tt // 2) * 256)])
                j += 1
    print('SBUF bytes remaining per partition:', nc.sbuf_bytes_remaining)
    S.finish()
    es.close()
    return nc


_CACHE = {}


def kernel(x_prompt, x_sample, state_conv_a, state_pool_b, state_conv_d,
           norm_pre, norm_post, w_in_even, w_out_even, conv_a_w, pool_w, pool_scale,
           w_in_odd, w_out_odd, sgu_ln_g, sgu_ln_b, sgu_w, sgu_b,
           conf_dw_w, conf_dw_b, conf_ln_g, conf_ln_b):
    f = lambda a: np.ascontiguousarray(np.asarray(a, dtype=np.float32))
    x_prompt, x_sample = f(x_prompt), f(x_sample)
    state_conv_a, state_pool_b, state_conv_d = f(state_conv_a), f(state_pool_b), f(state_conv_d)
    shared = dict(
        norm_pre=f(norm_pre), norm_post=f(norm_post), w_in_even=f(w_in_even), w_out_even=f(w_out_even),
        conv_a_w=f(conv_a_w), pool_w=f(pool_w), pool_scale=f(pool_scale), w_in_odd=f(w_in_odd),
        w_out_odd=f(w_out_odd), sgu_ln_g=f(sgu_ln_g), sgu_ln_b=f(sgu_ln_b), sgu_w=f(sgu_w), sgu_b=f(sgu_b),
        conf_dw_w=f(conf_dw_w), conf_dw_b=f(conf_dw_b), conf_ln_g=f(conf_ln_g), conf_ln_b=f(conf_ln_b),
    )
    shared["c_ident"] = np.eye(128, dtype=np.float32)
    shared["c_tril"] = np.tril(np.ones((128, 128), np.float32))
    jj = np.arange(64)
    shared["c_bmask"] = ((jj[:, None] // 8 == jj[None, :] // 8) & (jj[:, None] % 8 <= jj[None, :] % 8)).astype(np.float32)
    npre, npost = shared.pop("norm_pre"), shared.pop("norm_post")
    caw, psc = shared.pop("conv_a_w"), shared.pop("pool_scale")
    cdw, cdb = shared.pop("conf_dw_w"), shared.pop("conf_dw_b")
    clg, clb = shared.pop("conf_ln_g"), shared.pop("conf_ln_b")
    c_ = np.ascontiguousarray
    shared["p_gpre"] = c_(npre.reshape(4, 8, 128).transpose(2, 0, 1))
    shared["p_gpost"] = c_(npost.reshape(4, 8, 128).transpose(2, 0, 1))
    shared["p_caw"] = c_(caw.reshape(2, 3, 4, 128).transpose(3, 0, 1, 2))
    shared["p_psc"] = c_(psc.reshape(2, 4, 128).transpose(2, 0, 1))
    shared["p_cdb"] = c_(cdb.reshape(2, 4, 128).transpose(2, 0, 1))
    shared["p_clg"] = c_(clg.reshape(2, 4, 128).transpose(2, 0, 1))
    shared["p_clb"] = c_(clb.reshape(2, 4, 128).transpose(2, 0, 1))
    wpad = np.zeros((2, 32, 512), np.float32)
    wpad[:, 0:31, :] = cdw
    shared["p_cdw4"] = c_(wpad.reshape(2, 8, 4, 16, 32).transpose(2, 4, 0, 1, 3).reshape(128, 2, 8, 16))
    sw = shared["sgu_w"]
    wsf = np.zeros((2, 64, 4, 64), np.float32)
    for s_ in range(8):
        wsf[:, 8 * s_:8 * s_ + 8, :, 8 * s_:8 * s_ + 8] = sw[:, :, 0:8, 0:8].transpose(0, 3, 1, 2)
    shared["p_wsf"] = wsf
    shared["p_bsrows"] = c_(np.tile(shared["sgu_b"][:, :, 0:8], (1, 1, 8)))
    shared["c_idrep"] = np.tile(np.eye(32, dtype=np.float32), (4, 1))
    sel = np.zeros((4, 4, 128), np.float32)
    for g in range(4):
        sel[g, g, :] = 1.0
    shared["c_sel"] = sel
    shared["c_invc"] = np.tile((1.0 / np.arange(1, 17, dtype=np.float32))[None, :], (128, 1)).astype(np.float32)
    in_maps = []
    for c in range(NCORES):
        m = dict(shared)
        m["xp"] = x_prompt[c]
        m["xs"] = np.ascontiguousarray(x_sample[16 * c:16 * c + 16].reshape(128, 1024))
        m["st_a"] = np.ascontiguousarray(state_conv_a[:, 16 * c:16 * c + 16])
        m["st_b"] = np.ascontiguousarray(state_pool_b[:, 16 * c:16 * c + 16])
        m["st_d"] = np.ascontiguousarray(state_conv_d[:, 16 * c:16 * c + 16])
        in_maps.append(m)
    if "nc" not in _CACHE:
        _CACHE["nc"] = build_program()
    res = run_bass_kernel_spmd(_CACHE["nc"], in_maps, core_ids=list(range(NCORES)))
    R = res.results
    y_prompt = np.stack([R[c]["y_p"] for c in range(NCORES)], 0)
    y_sample = np.concatenate([R[c]["y_s"].reshape(16, 8, 1024) for c in range(NCORES)], 0)
    ca_p = np.stack([R[c]["ca_p"] for c in range(NCORES)], 1)
    ca_s = np.concatenate([R[c]["ca_s"] for c in range(NCORES)], 1)
    pb_p = np.stack([R[c]["pb_p"] for c in range(NCORES)], 1)
    pb_s = np.concatenate([R[c]["pb_s"] for c in range(NCORES)], 1)
    cd_p = np.stack([R[c]["cd_p"] for c in range(NCORES)], 1)
    cd_s = np.concatenate([R[c]["cd_s"] for c in range(NCORES)], 1)
    v_s = np.concatenate([R[c]["v_s"] for c in range(NCORES)], 1)
    return (y_prompt, y_sample, ca_p, ca_s, pb_p, pb_s, cd_p, cd_s, v_s)
```

```python
import contextlib
import numpy as np
import concourse.bass as bass
import concourse.mybir as mybir
from concourse.bass_utils import run_bass_kernel_spmd

F32 = mybir.dt.float32
BF16 = mybir.dt.bfloat16
AF = mybir.ActivationFunctionType
ALU = mybir.AluOpType

NCORES = 8
EPS = 1e-6
NSLOT = 9
TBSZ = (1024, 1040, 1024, 1216, 1088, 1088, 1024, 1024)
NTB = 8
POOL_W = (2, 4, 8, 16)
PIPELINE = 4


class Tk:
    __slots__ = ("name", "w", "rd")

    def __init__(self, name):
        self.name = name
        self.w = None
        self.rd = {}


class Eng:
    def __init__(self, h, sem, name):
        self.h = h
        self.sem = sem
        self.name = name
        self.n = 0
        self.seen = {}


class Sched:
    def __init__(self, nc, es):
        self.nc = nc
        self.es = es
        mk = lambda nm: es.enter_context(nc.semaphore(nm))
        self.pe = Eng(nc.tensor, mk("s_pe"), "pe")
        self.act = Eng(nc.scalar, mk("s_act"), "act")
        self.dve = Eng(nc.vector, mk("s_dve"), "dve")
        self.pool = Eng(nc.gpsimd, mk("s_pool"), "pool")
        self.sp = Eng(nc.sync, mk("s_sp"), "sp")
        self.dsem = {}
        for q, cnt in (("sp", 16), ("pool", 12)):
            self.dsem[q] = [[mk("d_%s%d" % (q, j)), 0] for j in range(cnt)]
        self.drr = {"sp": 0, "pool": 0}
        self.gsem = {}

    def _waits(self, eng, reads, writes):
        deps = {}

        def add(d, same_ok):
            if d is None:
                return
            s, v = d
            if (not same_ok) and s is eng.sem:
                return
            if deps.get(s, (None, 0))[1] < v:
                deps[s] = (s, v)

        for t in reads:
            add(t.w, True)
        for t in writes:
            add(t.w, True)
            for s, v in t.rd.items():
                add((s, v), True)
        for s, v in deps.values():
            if eng.seen.get(s, 0) >= v:
                continue
            eng.h.wait_ge(s, v)
            eng.seen[s] = v

    def _commit(self, me, reads, writes):
        s, v = me
        for t in reads:
            t.rd[s] = v
        for t in writes:
            t.w = me
            t.rd = {}

    def op(self, eng, fn, reads=(), writes=()):
        self._waits(eng, reads, writes)
        ins = fn()
        eng.n += 1
        ins.then_inc(eng.sem, 1)
        self._commit((eng.sem, eng.n), reads, writes)

    def dma(self, q, out_ap, in_ap, reads=(), writes=(), **kw):
        eng = self.sp if q == "sp" else self.pool
        self._waits(eng, reads, writes)
        lst = self.dsem[q]
        j = self.drr[q]
        self.drr[q] = (j + 1) % len(lst)
        ent = lst[j]
        if ent[1] > 0 and eng.seen.get(ent[0], 0) < ent[1]:
            eng.h.wait_ge(ent[0], ent[1])
            eng.seen[ent[0]] = ent[1]
        ins = eng.h.dma_start(out=out_ap, in_=in_ap, **kw)
        ent[1] += 16
        ins.then_inc(ent[0], 16)
        self._commit((ent[0], ent[1]), reads, writes)

    def dma_group(self, q, pairs, reads=(), writes=(), key="g0"):
        eng = self.sp if q == "sp" else self.pool
        self._waits(eng, reads, writes)
        if key not in self.gsem:
            self.gsem[key] = [self.es.enter_context(self.nc.semaphore("dg_" + key)), 0]
        ent = self.gsem[key]
        for (o, i_) in pairs:
            eng.h.dma_start(out=o, in_=i_).then_inc(ent[0], 16)
            ent[1] += 16
        self._commit((ent[0], ent[1]), reads, writes)

    def finish(self):
        for q in ("sp", "pool"):
            for s, v in self.dsem[q]:
                if v > 0:
                    self.nc.sync.wait_ge(s, v)


class Blk:
    def __init__(self, kind, xc, n, nseq, T, tok0=0, sh=0):
        self.kind = kind
        self.xc = xc
        self.n = n
        self.nseq = nseq
        self.T = T
        self.tok0 = tok0
        self.sh = sh
        self.first = (kind == "p" and tok0 == 0)
        self.last = (kind == "p" and tok0 + n == 2048)


def build_program():
    nc = bass.Bass("TRN2", target_bir_lowering=False)
    es = contextlib.ExitStack()

    def din(name, shape):
        return nc.dram_tensor(name, shape, F32, kind="ExternalInput").ap()

    def dout(name, shape):
        return nc.dram_tensor(name, shape, F32, kind="ExternalOutput").ap()

    xp = din("xp", [2048, 1024])
    xs = din("xs", [128, 1024])
    st_a = din("st_a", [2, 16, 2, 512])
    st_b = din("st_b", [2, 16, 15, 512])
    st_d = din("st_d", [2, 16, 30, 512])
    p_gpre = din("p_gpre", [128, 4, 8])
    p_gpost = din("p_gpost", [128, 4, 8])
    p_caw = din("p_caw", [128, 2, 3, 4])
    p_psc = din("p_psc", [128, 2, 4])
    p_cdb = din("p_cdb", [128, 2, 4])
    p_clg = din("p_clg", [128, 2, 4])
    p_clb = din("p_clb", [128, 2, 4])
    p_cdw4 = din("p_cdw4", [128, 2, 8, 16])
    p_wsf = din("p_wsf", [2, 64, 4, 64])
    p_bsrows = din("p_bsrows", [2, 4, 64])
    w_in = [din("w_in_even", [2, 1024, 3072]), din("w_in_odd", [2, 1024, 3072])]
    w_out = [din("w_out_even", [2, 1024, 1024]), din("w_out_odd", [2, 1024, 1024])]
    pool_w = din("pool_w", [2, 4, 128, 128])
    sgu_ln_g = din("sgu_ln_g", [2, 512])
    sgu_ln_b = din("sgu_ln_b", [2, 512])
    sgu_w = din("sgu_w", [2, 4, 128, 128])
    sgu_b = din("sgu_b", [2, 4, 128])
    c_ident = din("c_ident", [128, 128])
    c_tril = din("c_tril", [128, 128])
    c_bmask = din("c_bmask", [64, 64])
    c_invc = din("c_invc", [128, 16])
    c_idrep = din("c_idrep", [128, 32])
    c_sel = din("c_sel", [4, 4, 128])

    y_p = dout("y_p", [2048, 1024])
    y_s = dout("y_s", [128, 1024])
    ca_p = dout("ca_p", [2, 2, 512])
    ca_s = dout("ca_s", [2, 16, 2, 512])
    pb_p = dout("pb_p", [2, 15, 512])
    pb_s = dout("pb_s", [2, 16, 15, 512])
    cd_p = dout("cd_p", [2, 30, 512])
    cd_s = dout("cd_s", [2, 16, 30, 512])
    v_s = dout("v_s", [2, 16, 8, 512])

    S = Sched(nc, es)
    PE, ACT, DVE, POOL = S.pe, S.act, S.dve, S.pool

    def sb(name, shape, dt=F32):
        return es.enter_context(nc.sbuf_tensor(name, shape, dt))

    XW = 1152
    X = sb("X", [128, 8, XW])
    RING = sb("RING", [128, NSLOT, 8, 512], BF16)
    H = sb("H", [128, 2, 8, 256], BF16)
    SQ = sb("SQ", [128, 8, 256], BF16)
    YB = sb("YB", [128, 1, 8, 256], BF16)
    OG = sb("OG", [128, 8, 256])
    MS = sb("MS", [128, 2])
    RSM = sb("RSM", [128, 2])
    DG = sb("DG", [128, 2, 2, 128])
    ONESF2 = sb("ONESF2", [128, 128])
    LST = sb("LST", [128, 4])
    LT = sb("LT", [128, 2])
    LRS = sb("LRS", [128, 2])
    LNMR = sb("LNMR", [128, 2])
    BSHI = sb("BSHI", [4, 128], BF16)
    BSLO = sb("BSLO", [4, 128], BF16)
    BSHIS = sb("BSHIS", [4, 64], BF16)
    BSLOS = sb("BSLOS", [4, 64], BF16)
    BSTMP = sb("BSTMP", [4, 128])
    TB = [sb("TB%d" % j, [128, TBSZ[j]]) if j not in (3, 4) else None for j in range(NTB)]
    TB34 = sb("TB34", [128, TBSZ[3] + TBSZ[4]])
    TB[3] = TB34[:, 0:TBSZ[3]]
    TB[4] = TB34[:, TBSZ[3]:TBSZ[3] + TBSZ[4]]
    BFB = [sb("BFB%d" % j, [128, 1024], BF16) for j in range(3)]
    XEB = sb("XEB", [128, 1248], BF16)
    WP4 = sb("WP4", [128, 8, 16, 32], BF16)
    CDW4 = sb("CDW4", [128, 2, 8, 16])
    IDREP = sb("IDREP", [128, 32])
    IDB = sb("IDB", [128, 128], BF16)
    SEL = sb("SEL", [4, 4, 128], BF16)
    STG = sb("STG", [128, 2, 512])
    PP = es.enter_context(nc.psum_tensor("PP", [128, 2, 1024], F32))
    PMX = es.enter_context(nc.psum_tensor("PMX", [128, 1024], F32))
    PS = es.enter_context(nc.psum_tensor("PS", [128, 2, 512], F32))
    IDENT = sb("IDENT", [128, 128])
    TRIL = sb("TRIL", [128, 128])
    BMASK = sb("BMASK", [64, 64])
    INVC = sb("INVC", [128, 16])
    ONESB = sb("ONESB", [128, 128], BF16)
    MHALF = sb("MHALF", [128, 2])
    GPRE = sb("GPRE", [128, 4, 8])
    GPOST = sb("GPOST", [128, 4, 8])
    CAW = sb("CAW", [128, 2, 3, 4])
    PSC = sb("PSC", [128, 2, 4])
    CDB = sb("CDB", [128, 2, 4])
    CLG = sb("CLG", [128, 2, 4])
    CLB = sb("CLB", [128, 2, 4])
    PW = sb("PW", [128, 4, 128], BF16)
    GB = sb("GB", [128, 512])
    BB = sb("BB", [128, 512])
    WMT = sb("WMT", [128, 2, 4, 128], BF16)
    WMTS = sb("WMTS", [64, 2, 4, 64], BF16)
    WSF = sb("WSF", [64, 4, 64])
    BSROW = sb("BSROW", [4, 128])
    BSROWS = sb("BSROWS", [4, 64])
    SH = sb("SH", [128, 2, 4, 2])
    PH = sb("PH", [128, 2, 4, 15])
    GH = sb("GH", [128, 2, 4, 30])
    BST = sb("BST", [128, 2, 6])
    MV = sb("MV", [128, 2, 2])
    VE = sb("VE", [128, 2])
    RS = sb("RS", [128, 2])
    NMR = sb("NMR", [128, 2])
    FIX = sb("FIX", [128, 16])
    CMP = sb("CMP", [128, 4, 64])
    GST = CMP
    DGA = sb("DGA", [128, 2, 128])

    tk = {}

    def T(name):
        if name not in tk:
            tk[name] = Tk(name)
        return tk[name]

    XT = lambda xc: T("X%d" % xc)
    TBT = [T("TB%d" % j) for j in range(NTB)]
    BFT = [T("BFB%d" % j) for j in range(3)]
    PPT = [T("PP0"), T("PP1")]
    PST = [T("PS0"), T("PS1")]
    HT = [T("H0"), T("H1")]
    YT = [T("Y0"), T("Y1")]
    YTA, YTB = T("YA"), T("YB")
    STGT = [T("STG0"), T("STG1")]
    RINGT = [T("RING%d" % j) for j in range(NSLOT)]
    rr = {"pp": 0, "ps": 0, "h": 0, "y": 0, "stg": 0, "ms": 0}

    def nxt(k, m=2):
        v = rr[k]
        rr[k] = (v + 1) % m
        return v

    def act_copy(out, in_, reads, writes, scale=None, func=None, bias=None):
        def fn():
            kw = {}
            if scale is not None:
                kw["scale"] = scale
            if bias is not None:
                kw["bias"] = bias
            f = func if func is not None else (AF.Copy if (scale is None and bias is None) else AF.Identity)
            return nc.scalar.activation(out=out, in_=in_, func=f, **kw)
        S.op(ACT, fn, reads=reads, writes=writes)

    def dve_tt(out, a, b, op, reads, writes):
        S.op(DVE, lambda: nc.vector.tensor_tensor(out=out, in0=a, in1=b, op=op), reads=reads, writes=writes)

    def dve_stt(out, a, sc, b, op0, op1, reads, writes):
        S.op(DVE, lambda: nc.vector.scalar_tensor_tensor(out=out, in0=a, scalar=sc, in1=b, op0=op0, op1=op1),
             reads=reads, writes=writes)

    def dve_ts(out, a, s1, s2, op0, op1, reads, writes):
        if s2 is None:
            S.op(DVE, lambda: nc.vector.tensor_scalar(out=out, in0=a, scalar1=s1, scalar2=None, op0=op0),
                 reads=reads, writes=writes)
        else:
            S.op(DVE, lambda: nc.vector.tensor_scalar(out=out, in0=a, scalar1=s1, scalar2=s2, op0=op0, op1=op1),
                 reads=reads, writes=writes)

    def dve_copy(out, in_, reads, writes):
        S.op(DVE, lambda: nc.vector.tensor_copy(out=out, in_=in_), reads=reads, writes=writes)

    def pool_pow(out, in_, n, reads, writes):
        S.op(POOL, lambda: nc.gpsimd.tensor_tensor(out=out, in0=in_, in1=MHALF[0:in_.shape[0], 0:n], op=ALU.pow),
             reads=list(reads) + [T("MHALF")], writes=writes)

    def cload(dst_ap, src_ap, name, q="sp", **kw):
        S.dma(q, dst_ap, src_ap, writes=[T(name)], **kw)

    groups = {}
    gcount = [0]

    def group_dma(items, base, q="sp"):
        tiles = [T(base)]
        for k, (dst, src, kw) in enumerate(items):
            gcount[0] += 1
            u = T("%s_u%d" % (base, gcount[0]))
            if k == 0:
                S.dma(q, dst, src, writes=[T(base), u], **kw)
            else:
                S.dma(q, dst, src, reads=[T(base)], writes=[u], **kw)
            tiles.append(u)
        groups[base] = tiles
        return tiles

    NCD = dict(allow_slow_non_contiguous=True)
    cload(IDENT[:, :], c_ident, "IDENT")
    cload(GPRE[:, :, :], p_gpre, "GPRE")
    dve_copy(IDB[:, :], IDENT[:, :], [T("IDENT")], [T("IDB")])
    S.op(DVE, lambda: nc.vector.memset(XEB[:, :], 0.0), writes=[T("XEB")])

    def late_consts():
        cload(GPOST[:, :, :], p_gpost, "GPOST")
        cload(CAW[:, :, :, :], p_caw, "CAW")
        cload(PSC[:, :, :], p_psc, "PSC")
        cload(INVC[:, :], c_invc, "INVC")
        cload(TRIL[:, :], c_tril, "TRIL")
        cload(BMASK[:, :], c_bmask, "BMASK")
        cload(IDREP[:, :], c_idrep, "IDREP")
        cload(SEL[:, :, :], c_sel, "SEL", q="pool")
        cload(CDB[:, :, :], p_cdb, "CDB")
        cload(CLG[:, :, :], p_clg, "CLG")
        cload(CLB[:, :, :], p_clb, "CLB")
        cload(CDW4[:, :, :, :], p_cdw4, "CDW4")
        groups["CDW4"] = [T("CDW4")]

    def load_params(l):
        par, i = l % 2, l // 2
        if par == 0:
            cload(PW[:, :, :], pool_w[i].rearrange("g c d -> c g d"), "PW", q="pool")
        else:
            cload(GB[:, :], sgu_ln_g[i:i + 1, :].partition_broadcast(128)[:, 0, :], "GB")
            cload(BB[:, :], sgu_ln_b[i:i + 1, :].partition_broadcast(128)[:, 0, :], "BB")
            cload(BSROW[:, :], sgu_b[i], "BSROW")
            cload(BSROWS[:, :], p_bsrows[i], "BSROWS")
            bst = [T("BSROWS")]
            for (src, srct, hi, lo, w) in ((BSROW, [T("BSROW")], BSHI, BSLO, 128), (BSROWS, bst, BSHIS, BSLOS, 64)):
                dve_copy(hi[:, :], src[:, :], srct, [T("BSHL")])
                dve_tt(BSTMP[:, 0:w], src[:, :], hi[:, :], ALU.subtract, srct + [T("BSHL")], [T("BSTMP")])
                dve_copy(lo[:, :], BSTMP[:, 0:w], [T("BSTMP")], [T("BSHL")])
            for m_ in range(8):
                S.op(POOL, lambda m_=m_: nc.gpsimd.tensor_tensor(
                    out=WP4[:, m_, :, :], in0=IDREP[:, :].unsqueeze(1).broadcast_to([128, 16, 32]),
                    in1=CDW4[:, i, m_, :].unsqueeze(2).broadcast_to([128, 16, 32]), op=ALU.mult),
                    reads=[T("IDREP")] + groups["CDW4"], writes=[T("WP4")])

    S.op(DVE, lambda: nc.vector.memset(ONESB[:, :], 1.0), writes=[T("ONESB")])
    S.op(DVE, lambda: nc.vector.memset(ONESF2[:, :], 1.0), writes=[T("ONESF2")])
    S.op(DVE, lambda: nc.vector.memset(MHALF[:, :], -0.5), writes=[T("MHALF")])
    S.op(DVE, lambda: nc.vector.memset(SH[:, :, :, :], 0.0), writes=[T("SH0"), T("SH1")])
    S.op(DVE, lambda: nc.vector.memset(PH[:, :, :, :], 0.0), writes=[T("PH0"), T("PH1")])
    S.op(DVE, lambda: nc.vector.memset(GH[:, :, :, :], 0.0), writes=[T("GH0"), T("GH1")])

    chunks = []
    ORDER = {0: [4, 1, 2, 3, 0, 5], 1: [3, 4, 1, 2, 0, 5]}
    for ps in range(2):
        for l in range(4):
            par, i = l % 2, l // 2
            wi = w_in[par][i].rearrange("(k p) f -> p k f", p=128)
            wo = w_out[par][i].rearrange("(k p) f -> p k f", p=128)
            for part in ORDER[par]:
                chunks.append(wi[:, :, part * 512:(part + 1) * 512])
            for hf in range(2):
                chunks.append(wo[:, :, hf * 512:(hf + 1) * 512])
    nchunk_loaded = [0]
    rep_no = [0]

    def load_chunk():
        n = nchunk_loaded[0]
        if n >= len(chunks):
            return
        nchunk_loaded[0] += 1
        slot = n % NSLOT
        S.dma("pool", RING[:, slot, :, :], chunks[n], writes=[RINGT[slot]])

    for _ in range(NSLOT):
        load_chunk()

    def sgu_setup():
        for i in range(2):
            for g in range(4):
                tb = TB[g % 2]
                tbt = TBT[g % 2]
                S.dma("sp", tb[:, 0:128], sgu_w[i, g], writes=[tbt])
                dve_tt(tb[:, 0:128], tb[:, 0:128], TRIL[:, :], ALU.mult, [tbt, T("TRIL")], [tbt])
                pm = nxt("ps")
                S.op(PE, lambda tb=tb, pm=pm: nc.tensor.transpose(out=PS[:, pm, 0:128], in_=tb[:, 0:128], identity=IDENT[:, :]),
                     reads=[tbt, T("IDENT")], writes=[PST[pm]])
                act_copy(WMT[:, i, g, :], PS[:, pm, 0:128], [PST[pm]], [T("WMT")])
            cload(WSF[:, :, :], p_wsf[i], "WSF")
            wt = [T("WSF")]
            for g in range(4):
                dve_tt(WMTS[:, i, g, :], WSF[:, g, :], BMASK[:, :], ALU.mult, wt + [T("BMASK")], [T("WMTS")])

    def load_x_tile(src_rows, xc, j, xts):
        tb, tbt = TB[j % 8], TBT[j % 8]
        S.dma("sp", tb[:, 0:1024], src_rows, writes=[tbt])
        pp = nxt("pp")

        def fn():
            ins = None
            for k in range(8):
                ins = nc.tensor.transpose(out=PP[:, pp, k * 128:(k + 1) * 128], in_=tb[:, k * 128:(k + 1) * 128],
                                          identity=IDENT[:, :])
            return ins
        S.op(PE, fn, reads=[tbt, T("IDENT")], writes=[PPT[pp]])
        src = PP[:, pp, :].rearrange("p (k t) -> p k t", k=8)
        if j % 2 == 0:
            act_copy(X[:, :, xc:xc + 128], src, [PPT[pp]], list(xts))
        else:
            dve_copy(X[:, :, xc:xc + 128], src, [PPT[pp]], list(xts))

    def store_x_tile(dst_rows, xc, j, xts):
        pp = nxt("pp")

        def fn():
            ins = None
            for k in range(8):
                ins = nc.tensor.transpose(out=PP[:, pp, k * 128:(k + 1) * 128], in_=X[:, k, xc:xc + 128],
                                          identity=IDENT[:, :])
            return ins
        S.op(PE, fn, reads=list(xts) + [T("IDENT")], writes=[PPT[pp]])
        tb, tbt = TB[j % 8], TBT[j % 8]
        if j % 2 == 0:
            act_copy(tb[:, 0:1024], PP[:, pp, :], [PPT[pp]], [tbt])
        else:
            dve_copy(tb[:, 0:1024], PP[:, pp, :], [PPT[pp]], [tbt])
        S.dma("sp", dst_rows, tb[:, 0:1024], reads=[tbt])

    def rows_out(src_fn, nrows, dst_ap, reads, src4=None):
        if src4 is not None:
            act_copy(CMP[:, :, 0:nrows].rearrange("p g (s j) -> p g s j", s=src4.shape[2]), src4, list(reads), [T("CMP")])
            src_fn = lambda g: CMP[:, g, 0:nrows]
            reads = [T("CMP")]
        pm = nxt("ps")

        def fn():
            ins = None
            for g in range(4):
                ins = nc.tensor.transpose(out=PS[0:nrows, pm, g * 128:(g + 1) * 128], in_=src_fn(g), identity=IDENT[:, :])
            return ins
        S.op(PE, fn, reads=list(reads) + [T("IDENT")], writes=[PST[pm]])
        sg = nxt("stg")
        act_copy(STG[0:nrows, sg, :], PS[0:nrows, pm, 0:512], [PST[pm]], [STGT[sg]])
        S.dma("sp", dst_ap, STG[0:nrows, sg, :], reads=[STGT[sg]])

    def hist_in(src_ap, nrows, dst_view, nsq, J, wtile):
        sg = nxt("stg")
        S.dma("sp", STG[0:nrows, sg, :], src_ap, writes=[STGT[sg]])
        pm = nxt("ps")

        def fn():
            ins = None
            for g in range(4):
                ins = nc.tensor.transpose(out=PS[:, pm, g * nrows:(g + 1) * nrows], in_=STG[0:nrows, sg, g * 128:(g + 1) * 128],
                                          identity=IDENT[0:nrows, 0:nrows])
            return ins
        S.op(PE, fn, reads=[STGT[sg], T("IDENT")], writes=[PST[pm]])
        act_copy(dst_view, PS[:, pm, 0:4 * nrows].rearrange("p (g s j) -> p g s j", g=4, s=nsq),
                 [PST[pm]], [wtile])

    def bcast_dg(cols, m, ntile, reads):
        for j, col in enumerate(cols):
            for ti in range(ntile):
                S.op(DVE, lambda j=j, ti=ti, col=col: nc.vector.tensor_scalar(
                    out=DG[0:m, j, ti, 0:m], in0=IDENT[0:m, 0:m], scalar1=col[:, ti:ti + 1], scalar2=None, op0=ALU.mult),
                    reads=list(reads) + [T("IDENT")], writes=[T("DG")])

    def bcast_pe(ps, ncols, m, ntile):
        def fn():
            ins = None
            for j in range(ncols):
                for ti in range(ntile):
                    ins = nc.tensor.matmul(PS[:, ps, j * 256 + ti * m:j * 256 + (ti + 1) * m], lhsT=ONESF2[0:m, :],
                                           rhs=DG[0:m, j, ti, 0:m], start=True, stop=True)
            return ins
        S.op(PE, fn, reads=[T("DG"), T("ONESF2")], writes=[PST[ps]])

    def stats_rstd_a(sq_n, reads_sq, n, own=False):
        m = min(n, 128)
        ntile = n // m
        ps = nxt("ps")

        def fn():
            ins = None
            for ti in range(ntile):
                for k in range(8):
                    ins = nc.tensor.matmul(PS[0:m, ps, 508 + ti:509 + ti], lhsT=SQ[:, k, ti * m:(ti + 1) * m], rhs=ONESB[:, 0:1],
                                           start=(k == 0), stop=(k == 7))
            return ins
        S.op(PE, fn, reads=list(reads_sq) + [T("ONESB")], writes=[PST[ps]])
        act_copy(MS[0:m, 0:ntile], PS[0:m, ps, 508:508 + ntile], [PST[ps]], [T("MS")], scale=1.0 / sq_n, bias=EPS)
        S.op(POOL, lambda: nc.gpsimd.tensor_tensor(out=RSM[0:m, 0:ntile], in0=MS[0:m, 0:ntile], in1=MHALF[0:m, 0:ntile], op=ALU.pow),
             reads=[T("MS"), T("MHALF")], writes=[T("RSM")])
        dgv = DGA if own else DG[:, 0, :, :]
        dgt = T("DGA") if own else T("DG")
        for ti in range(ntile):
            S.op(DVE, lambda ti=ti: nc.vector.tensor_scalar(
                out=dgv[0:m, ti, 0:m], in0=IDENT[0:m, 0:m], scalar1=RSM[0:m, ti:ti + 1], scalar2=None, op0=ALU.mult),
                reads=[T("RSM"), T("IDENT")], writes=[dgt])
        return (ps, m, ntile, dgv, dgt)

    def stats_rstd_b(ctx):
        ps, m, ntile, dgv, dgt = ctx
        if dgt is T("DGA"):
            ps = nxt("ps")

        def fn():
            ins = None
            for ti in range(ntile):
                ins = nc.tensor.matmul(PS[:, ps, ti * m:(ti + 1) * m], lhsT=ONESF2[0:m, :], rhs=dgv[0:m, ti, 0:m], start=True, stop=True)
            return ins
        S.op(PE, fn, reads=[dgt, T("ONESF2")], writes=[PST[ps]])
        return ps

    class Step:
        pass

    hookq = []

    def run_hook():
        if hookq:
            hookq.pop(0)()

    def make_step(l, blk, wslots, last_in_layer):
        st = Step()
        par, i = l % 2, l // 2
        n, nseq, Tt = blk.n, blk.nseq, blk.T
        xc = blk.xc
        xt = XT(xc)
        hs_box = [None]
        Yv = YB[:, 0, :, 0:n]
        ys = 0

        abox = {}

        def phase_Aa():
            S.op(ACT, lambda: nc.scalar.activation(out=SQ[:, :, 0:n], in_=X[:, :, xc:xc + n], func=AF.Square),
                 reads=[xt], writes=[T("SQ")])
            abox["ctx"] = stats_rstd_a(1024.0, [T("SQ")], n, own=True)

        def phase_Ab():
            psr = stats_rstd_b(abox["ctx"])
            hs = nxt("h")
            hs_box[0] = hs
            for k in range(8):
                dve_stt(H[:, hs, k, 0:n], X[:, k, xc:xc + n], GPRE[:, l, k:k + 1], PS[:, psr, 0:n], ALU.mult, ALU.mult,
                        [xt, T("GPRE"), PST[psr]], [HT[hs]])

        def phase_A():
            phase_Aa()
            phase_Ab()

        def after_use():
            if last_in_layer:
                load_chunk()

        def inproj_fm(c):
            hs = hs_box[0]
            slot = wslots[c]
            pp = nxt("pp")
            view = PP[:, pp, 0:4 * n].rearrange("p (g t) -> p g t", g=4)

            def fn():
                ins = None
                for g in range(4):
                    for k in range(8):
                        ins = nc.tensor.matmul(view[:, g, :], lhsT=RING[:, slot, k, g * 128:(g + 1) * 128], rhs=H[:, hs, k, 0:n],
                                               start=(k == 0), stop=(k == 7))
                return ins
            S.op(PE, fn, reads=[RINGT[slot], HT[hs]], writes=[PPT[pp]])
            after_use()
            return pp, view

        def v4(ap):
            return ap.rearrange("p g (s t) -> p g s t", s=nseq)

        pmx4 = PMX[:, 0:4 * n].rearrange("p (g t) -> p g t", g=4)
        sl8 = slice(8 * blk.sh, 8 * blk.sh + 8)

        if par == 0:
            E = 15 + Tt
            E2 = 2 + Tt
            PX = TB[3][:, 0:4 * nseq * E].rearrange("p (g s e) -> p g s e", g=4, s=nseq)
            SX = TB[1][:, 0:4 * nseq * E2].rearrange("p (g s e) -> p g s e", g=4, s=nseq)
            Df = BFB[0][:, 0:4 * n].rearrange("p (g t) -> p g t", g=4)
            G2 = TB[6][:, 0:4 * n].rearrange("p (g t) -> p g t", g=4)

            def phase_B1():
                pp, view = inproj_fm(0)
                if blk.kind == "p":
                    dve_copy(PX[:, :, 0, 0:15], PH[:, i, :, :], [T("PH%d" % i)], [TBT[3]])
                else:
                    hist_in(st_b[i, sl8].rearrange("s j c -> (s j) c"), 120, PX[:, :, :, 0:15], 8, 15, TBT[3])
                    S.dma("sp", pb_s[i, sl8, 0:7, :], st_b[i, sl8, 8:15, :])
                act_copy(PX[:, :, :, 15:E], v4(view), [PPT[pp]], [TBT[3]])
                if blk.kind == "p" and not blk.last:
                    dve_copy(PH[:, i, :, :], PX[:, :, 0, Tt:Tt + 15], [TBT[3]], [T("PH%d" % i)])
                if blk.last:
                    rows_out(lambda g: PX[:, g, 0, Tt:Tt + 15], 15, pb_p[i], [TBT[3]])
                if blk.kind == "s":
                    rows_out(None, 64, pb_s[i, sl8, 7:15, :], [TBT[3]], src4=PX[:, :, :, 15:E])
                A1 = TB[4][:, 0:4 * nseq * E].rearrange("p (g s e) -> p g s e", g=4, s=nseq)
                A2 = TB[5][:, 0:4 * nseq * E].rearrange("p (g s e) -> p g s e", g=4, s=nseq)
                dve_tt(A1[:, :, :, 1:E], PX[:, :, :, 1:E], PX[:, :, :, 0:E - 1], ALU.add, [TBT[3]], [TBT[4]])
                dve_tt(A2[:, 1:4, :, 3:E], A1[:, 1:4, :, 3:E], A1[:, 1:4, :, 1:E - 2], ALU.add, [TBT[4]], [TBT[5]])
                dve_tt(A1[:, 2:4, :, 7:E], A2[:, 2:4, :, 7:E], A2[:, 2:4, :, 3:E - 4], ALU.add, [TBT[5]], [TBT[4]])
                dve_tt(A2[:, 3, :, 15:E], A1[:, 3, :, 15:E], A1[:, 3, :, 7:E - 8], ALU.add, [TBT[4]], [TBT[5]])
                WIN = [A1[:, 0], A2[:, 1], A1[:, 2], A2[:, 3]]
                wint = [TBT[4], TBT[5], TBT[4], TBT[5]]
                Dv = BFB[0][:, 0:4 * n].rearrange("p (g s t) -> p g s t", g=4, s=nseq)
                for g in range(4):
                    dve_stt(Dv[:, g], WIN[g][:, :, 15:E], 1.0 / POOL_W[g], PX[:, g, :, 15:E], ALU.mult, ALU.subtract,
                            [wint[g], TBT[3]], [BFT[0]])
                if blk.first:
                    for g in range(4):
                        w = POOL_W[g]
                        dve_tt(FIX[:, 0:w - 1], WIN[g][:, 0, 15:15 + w - 1], INVC[:, 0:w - 1], ALU.mult,
                               [wint[g], T("INVC")], [T("FIX")])
                        dve_tt(Dv[:, g, 0, 0:w - 1], FIX[:, 0:w - 1], PX[:, g, 0, 15:15 + w - 1], ALU.subtract,
                               [T("FIX"), TBT[3]], [BFT[0]])
                run_hook()
                pp, view = inproj_fm(1)
                ZC = TB[0][:, 0:4 * n].rearrange("p (g t) -> p g t", g=4)
                act_copy(ZC, view, [PPT[pp]], [TBT[0]])
                if blk.kind == "p":
                    dve_copy(SX[:, :, 0, 0:2], SH[:, i, :, :], [T("SH%d" % i)], [TBT[1]])
                else:
                    hist_in(st_a[i, sl8].rearrange("s j c -> (s j) c"), 16, SX[:, :, :, 0:2], 8, 2, TBT[1])
                pp, view = inproj_fm(2)
                dve_tt(SX[:, :, :, 2:E2], v4(ZC), v4(view), ALU.mult, [TBT[0], PPT[pp]], [TBT[1]])
                if blk.kind == "p" and not blk.last:
                    dve_copy(SH[:, i, :, :], SX[:, :, 0, Tt:Tt + 2], [TBT[1]], [T("SH%d" % i)])
                if blk.last:
                    rows_out(lambda g: SX[:, g, 0, Tt:Tt + 2], 2, ca_p[i], [TBT[1]])
                if blk.kind == "s":
                    rows_out(None, 16, ca_s[i, sl8].rearrange("s j c -> (s j) c"), [TBT[1]], src4=SX[:, :, :, Tt:Tt + 2])
                run_hook()

            def phase_B2():
                Y3 = TB[0][:, 0:4 * n].rearrange("p (g s t) -> p g s t", g=4, s=nseq)
                y3t = [T("Y3_%d" % g) for g in range(4)]
                for k in range(3):
                    for g in range(4):
                        if k == 0:
                            dve_ts(Y3[:, g], SX[:, g, :, 0:Tt], CAW[:, i, 0, g:g + 1], None, ALU.mult, None,
                                   [TBT[1], T("CAW")], [TBT[0], y3t[g]] if g == 0 else [y3t[g]])
                        else:
                            dve_stt(Y3[:, g], SX[:, g, :, k:k + Tt], CAW[:, i, k, g:g + 1], Y3[:, g], ALU.mult, ALU.add,
                                    [TBT[1], T("CAW"), y3t[g]], [y3t[g]])
                pp, view = inproj_fm(3)
                G = TB[2][:, 0:4 * n].rearrange("p (g t) -> p g t", g=4)
                act_copy(G, view, [PPT[pp]], [TBT[2]], func=AF.Silu)
                pp, view = inproj_fm(4)
                dve_tt(G, view, G, ALU.mult, [PPT[pp], TBT[2]], [TBT[2]])
                dve_tt(v4(Yv[:, 0:4, :]), Y3, v4(G), ALU.mult, [TBT[2]] + y3t, [YTA, TBT[0]])
                pp, view = inproj_fm(5)
                act_copy(G2, view, [PPT[pp]], [TBT[6]], func=AF.Silu)

                def fnm():
                    ins = None
                    for g in range(4):
                        ins = nc.tensor.matmul(pmx4[:, g, :], lhsT=PW[:, g, :], rhs=Df[:, g, :], start=True, stop=True)
                    return ins
                S.op(PE, fnm, reads=[BFT[0], T("PW")], writes=[T("PMX")])
                for g in range(4):
                    dve_stt(Yv[:, 4 + g, :], pmx4[:, g, :], PSC[:, i, g:g + 1], G2[:, g, :], ALU.mult, ALU.mult,
                            [T("PMX"), T("PSC"), TBT[6]], [YTB])
                run_hook()
        else:
            m = 128 if blk.kind == "p" else 64
            ntile = n // m
            EL = 31 + Tt
            RL = 28 + Tt
            XE = XEB[:, 0:4 * nseq * EL].rearrange("p (g s e) -> p g s e", g=4, s=nseq)
            xet = T("XEB")
            RV = TB34[:, :].bitcast(BF16).rearrange("p (q s e) -> p q s e", q=16, s=nseq)
            YN = TB[5][:, 0:4 * n].rearrange("p (g t) -> p g t", g=4)
            YBF = BFB[1][:, 0:4 * n].rearrange("p (g t) -> p g t", g=4)
            YSQ = BFB[2][:, 0:4 * n].rearrange("p (g t) -> p g t", g=4)
            VN = BFB[0][:, 0:1024].rearrange("p (t c) -> p t c", t=2)
            G = TB[2][:, 0:4 * n].rearrange("p (g t) -> p g t", g=4)
            mL = min(n, 128)
            ntL = n // mL
            box = {}

            def phase_B1():
                ppa, viewa = inproj_fm(0)
                ZA = TB[0][:, 0:4 * n].rearrange("p (g t) -> p g t", g=4)
                act_copy(ZA, viewa, [PPT[ppa]], [TBT[0]], scale=0.5)
                run_hook()
                ppb, viewb = inproj_fm(1)
                TH = TB[6][:, 0:4 * n].rearrange("p (g t) -> p g t", g=4)
                act_copy(TH, viewb, [PPT[ppb]], [TBT[6]], scale=0.5, func=AF.Tanh)
                if blk.kind == "p":
                    dve_copy(XE[:, :, 0, 0:30], GH[:, i, :, :], [T("GH%d" % i)], [xet])
                else:
                    for q in range(2):
                        sl = slice(8 * blk.sh + 4 * q, 8 * blk.sh + 4 * q + 4)
                        hist_in(st_d[i, sl].rearrange("s j c -> (s j) c"), 120, XE[:, :, 4 * q:4 * q + 4, 0:30], 4, 30, xet)
                    S.dma("sp", cd_s[i, sl8, 0:22, :], st_d[i, sl8, 8:30, :])
                dve_stt(XE[:, :, :, 30:30 + Tt], v4(TH), 1.0, v4(ZA), ALU.add, ALU.mult, [TBT[6], TBT[0]], [xet])
                if blk.kind == "p":
                    dve_stt(GH[:, i, :, :], TH[:, :, Tt - 30:Tt], 1.0, ZA[:, :, Tt - 30:Tt], ALU.add, ALU.mult,
                            [TBT[6], TBT[0]], [T("GH%d" % i)])
                    if blk.last:
                        rows_out(lambda g: GH[:, i, g, :], 30, cd_p[i], [T("GH%d" % i)])
                else:
                    dve_stt(GST[:, :, :], TH, 1.0, ZA, ALU.add, ALU.mult, [TBT[6], TBT[0]], [T("CMP")])
                    rows_out(lambda g: GST[:, g, :], 64, cd_s[i, sl8, 22:30, :], [T("CMP")])
                rts = [T("R%d" % z) for z in range(8)]
                for z in range(8):
                    if z % 3 == 2:
                        ptile, pview = T("PMX"), PMX[:, :]
                    else:
                        pp = nxt("pp")
                        ptile, pview = PPT[pp], PP[:, pp, :]

                    def fnrep(z=z, pview=pview):
                        ins = None
                        for dq in range(2):
                            q = 2 * z + dq
                            J, j = q // 4, q % 4
                            for s_ in range(4):
                                rhs = XE[:, J, 0, s_:s_ + RL] if nseq == 1 else XE[:, J, :, s_:s_ + RL]
                                ins = nc.tensor.matmul(pview[32 * s_:32 * s_ + 32, dq * 512:dq * 512 + nseq * RL],
                                                       lhsT=IDB[:, 32 * j:32 * j + 32], rhs=rhs, start=True, stop=True,
                                                       tile_position=(0, 32 * s_))
                        return ins
                    S.op(PE, fnrep, reads=[xet, T("IDB")], writes=[ptile])
                    src = pview.rearrange("p (a c) -> p a c", a=2)[:, :, 0:nseq * RL].rearrange("p a (s e) -> p a s e", s=nseq)
                    dst = RV[:, 2 * z:2 * z + 2, :, 0:RL]
                    wr = [TBT[3], TBT[4], rts[z]] if z == 0 else [rts[z]]
                    rdx = [ptile] if z == 0 else [ptile, TBT[3]]
                    if z % 3 != 1:
                        act_copy(dst, src, rdx, wr)
                    else:
                        dve_copy(dst, src, rdx, wr)

                def fnconv():
                    ins = None
                    for J in range(4):
                        for m_ in range(8):
                            for j in range(4):
                                if nseq == 1:
                                    rhs = RV[:, 4 * J + j, 0, 4 * m_:4 * m_ + Tt]
                                else:
                                    rhs = RV[:, 4 * J + j, :, 4 * m_:4 * m_ + Tt]
                                ins = nc.tensor.matmul(pmx4[32 * j:32 * j + 32, J, :], lhsT=WP4[:, m_, 4 * J + j, :], rhs=rhs,
                                                       start=(m_ == 0), stop=(m_ == 7), tile_position=(0, 32 * j))
                    return ins
                S.op(PE, fnconv, reads=rts + [T("WP4")], writes=[T("PMX"), TBT[3], TBT[4]])
                run_hook()
                for g in range(4):
                    act_copy(YN[:, g, :], pmx4[:, g, :], [T("PMX"), T("CDB")], [TBT[5]], bias=CDB[:, i, g:g + 1])
                act_copy(YBF, YN, [TBT[5]], [BFT[1]])
                act_copy(YSQ, YN, [TBT[5]], [BFT[2]], func=AF.Square)
                slot = wslots[2]
                hs = hs_box[0]
                pp = nxt("pp")

                def fnv():
                    ins = None
                    for ti in range(ntile):
                        for k in range(8):
                            ins = nc.tensor.matmul(PP[0:m, pp, ti * 512:(ti + 1) * 512], lhsT=H[:, hs, k, ti * m:(ti + 1) * m],
                                                   rhs=RING[:, slot, k, :], start=(k == 0), stop=(k == 7))
                    return ins
                S.op(PE, fnv, reads=[RINGT[slot], HT[hs]], writes=[PPT[pp]])
                after_use()
                for ti in range(ntile):
                    S.op(DVE, lambda ti=ti: nc.vector.bn_stats(out=BST[0:m, ti, :], in_=PP[0:m, pp, ti * 512:(ti + 1) * 512]),
                         reads=[PPT[pp]], writes=[T("BST%d" % ti)])
                    S.op(DVE, lambda ti=ti: nc.vector.bn_aggr(out=MV[0:m, ti, :], in_=BST[0:m, ti, :]),
                         reads=[T("BST%d" % ti)], writes=[T("MV%d" % ti)])
                mvt = [T("MV%d" % ti) for ti in range(ntile)]
                S.op(POOL, lambda: nc.gpsimd.tensor_scalar(out=VE[0:m, 0:ntile], in0=MV[0:m, 0:ntile, 1], scalar1=EPS, scalar2=None,
                                                            op0=ALU.add), reads=mvt, writes=[T("VE")])
                S.op(POOL, lambda: nc.gpsimd.tensor_tensor(out=RS[0:m, 0:ntile], in0=VE[0:m, 0:ntile], in1=MHALF[0:m, 0:ntile],
                                                            op=ALU.pow), reads=[T("VE"), T("MHALF")], writes=[T("RS")])
                dve_stt(NMR[0:m, 0:ntile], MV[0:m, 0:ntile, 0], -1.0, RS[0:m, 0:ntile], ALU.mult, ALU.mult,
                        mvt + [T("RS")], [T("NMR")])
                NT = TB[1][:, 0:1024].rearrange("p (t c) -> p t c", t=2)
                for ti in range(ntile):
                    act_copy(NT[0:m, ti, :], PP[0:m, pp, ti * 512:(ti + 1) * 512], [PPT[pp], T("RS"), T("NMR")], [TBT[1]],
                             scale=RS[0:m, ti:ti + 1], bias=NMR[0:m, ti:ti + 1])
                gbb = GB[0:m, :].unsqueeze(1).broadcast_to([m, ntile, 512])
                bbb = BB[0:m, :].unsqueeze(1).broadcast_to([m, ntile, 512])
                dve_tt(NT[0:m, 0:ntile, :], NT[0:m, 0:ntile, :], gbb, ALU.mult, [TBT[1], T("GB")], [TBT[1]])
                if blk.kind == "p":
                    dve_tt(VN[0:m, 0:ntile, :], NT[0:m, 0:ntile, :], bbb, ALU.add, [TBT[1], T("BB")], [BFT[0]])
                else:
                    dve_tt(NT[0:m, 0:1, :], NT[0:m, 0:1, :], bbb, ALU.add, [TBT[1], T("BB")], [TBT[1]])
                    S.dma("sp", v_s[i, sl8].rearrange("s t c -> (s t) c"), NT[0:m, 0, :], reads=[TBT[1]])
                    act_copy(VN[0:m, 0:1, :], NT[0:m, 0:1, :], [TBT[1]], [BFT[0]])
                psa = nxt("ps")
                box["psa"] = psa

                def fnln():
                    ins = None
                    for ti in range(ntL):
                        for j, src in enumerate((YBF, YSQ)):
                            for g in range(4):
                                ins = nc.tensor.matmul(PS[0:mL, psa, 2 * j + ti:2 * j + ti + 1], lhsT=src[:, g, ti * mL:(ti + 1) * mL],
                                                       rhs=ONESB[:, 0:1], start=(g == 0), stop=(g == 3))
                    return ins
                S.op(PE, fnln, reads=[BFT[1], BFT[2], T("ONESB")], writes=[PST[psa]])
                act_copy(LST[0:mL, 0:4], PS[0:mL, psa, 0:4], [PST[psa]], [T("LST")], scale=1.0 / 512)
                dve_tt(LT[0:mL, 0:ntL], LST[0:mL, 0:ntL], LST[0:mL, 0:ntL], ALU.mult, [T("LST")], [T("LT")])
                dve_tt(LT[0:mL, 0:ntL], LST[0:mL, 2:2 + ntL], LT[0:mL, 0:ntL], ALU.subtract, [T("LST"), T("LT")], [T("LT")])
                S.op(POOL, lambda: nc.gpsimd.tensor_scalar(out=LT[0:mL, 0:ntL], in0=LT[0:mL, 0:ntL], scalar1=EPS, scalar2=None, op0=ALU.add),
                     reads=[T("LT")], writes=[T("LT")])
                S.op(POOL, lambda: nc.gpsimd.tensor_tensor(out=LRS[0:mL, 0:ntL], in0=LT[0:mL, 0:ntL], in1=MHALF[0:mL, 0:ntL], op=ALU.pow),
                     reads=[T("LT"), T("MHALF")], writes=[T("LRS")])
                dve_stt(LNMR[0:mL, 0:ntL], LST[0:mL, 0:ntL], -1.0, LRS[0:mL, 0:ntL], ALU.mult, ALU.mult,
                        [T("LST"), T("LRS")], [T("LNMR")])
                bcast_dg([LRS[0:mL, 0:ntL], LNMR[0:mL, 0:ntL]], mL, ntL, [T("LRS"), T("LNMR")])

            def phase_B2():
                ppg, viewg = inproj_fm(3)
                act_copy(G, viewg, [PPT[ppg]], [TBT[2]], func=AF.Silu)
                ppu, viewu = inproj_fm(4)
                dve_tt(G, viewu, G, ALU.mult, [PPT[ppu], TBT[2]], [TBT[2]])
                def fnmix():
                    ins = None
                    for g in range(4):
                        for ti in range(ntile):
                            o = pmx4[:, g, ti * m:(ti + 1) * m]
                            if blk.kind == "p":
                                nc.tensor.matmul(o, lhsT=VN[0:m, ti, g * 128:(g + 1) * 128], rhs=WMT[:, i, g, :], start=True, stop=False)
                                nc.tensor.matmul(o, lhsT=SEL[:, g, :], rhs=BSHI[:, :], start=False, stop=False)
                                ins = nc.tensor.matmul(o, lhsT=SEL[:, g, :], rhs=BSLO[:, :], start=False, stop=True)
                            else:
                                nc.tensor.matmul(o, lhsT=VN[0:m, ti, g * 128:(g + 1) * 128], rhs=WMTS[:, i, g, :], start=True, stop=False)
                                nc.tensor.matmul(o, lhsT=SEL[:, g, :], rhs=BSHIS[:, :], start=False, stop=False)
                                ins = nc.tensor.matmul(o, lhsT=SEL[:, g, :], rhs=BSLOS[:, :], start=False, stop=True)
                    return ins
                S.op(PE, fnmix, reads=[BFT[0], T("WMT"), T("WMTS"), T("SEL"), T("BSHL")], writes=[T("PMX")])
                dve_tt(Yv[:, 0:4, :], pmx4, G, ALU.mult, [T("PMX"), TBT[2]], [YTA])
                ppd, viewd = inproj_fm(5)
                SGD = TB[7][:, 0:4 * n].rearrange("p (g t) -> p g t", g=4)
                act_copy(SGD, viewd, [PPT[ppd]], [TBT[7]], func=AF.Silu)
                psb = nxt("ps")
                bcast_pe(psb, 2, mL, ntL)
                t5h = [T("T5a"), T("T5b")]
                rbh = PS[:, psb, 0:n].unsqueeze(1).broadcast_to([128, 2, n])
                mbh = PS[:, psb, 256:256 + n].unsqueeze(1).broadcast_to([128, 2, n])
                for hh in range(2):
                    gs = slice(2 * hh, 2 * hh + 2)
                    dve_tt(YN[:, gs, :], YN[:, gs, :], rbh, ALU.mult, [TBT[5], PST[psb]], [t5h[hh]])
                    dve_tt(YN[:, gs, :], YN[:, gs, :], mbh, ALU.add, [t5h[hh], PST[psb]], [t5h[hh]])
                    for g in (2 * hh, 2 * hh + 1):
                        act_copy(YN[:, g, :], YN[:, g, :], [t5h[hh], T("CLG"), T("CLB")], [t5h[hh]], func=AF.Silu,
                                 scale=CLG[:, i, g:g + 1], bias=CLB[:, i, g:g + 1])
                dve_tt(Yv[:, 4:6, :], YN[:, 0:2, :], SGD[:, 0:2, :], ALU.mult, [t5h[0], TBT[7]], [T("YBh0")])
                dve_tt(Yv[:, 6:8, :], YN[:, 2:4, :], SGD[:, 2:4, :], ALU.mult, [t5h[1], TBT[7]], [YTB, TBT[5]])
                run_hook()

        def phase_C1():
            slots, views = [], []
            for hf in range(2):
                pp = nxt("pp")
                slots.append(pp)
                views.append(PP[:, pp, 0:4 * n].rearrange("p (g t) -> p g t", g=4))
            for hf in range(2):
                slot, view, pp = wslots[6 + hf], views[hf], slots[hf]

                def fno1(slot=slot, view=view):
                    ins = None
                    for g in range(4):
                        for k in range(4):
                            first = (k == 0) and ((g * n) % 512 == 0)
                            ins = nc.tensor.matmul(view[:, g, :], lhsT=RING[:, slot, k, g * 128:(g + 1) * 128], rhs=YB[:, ys, k, 0:n],
                                                   start=first, stop=False, skip_group_check=True)
                    return ins
                S.op(PE, fno1, reads=[RINGT[slot], YTA], writes=[PPT[pp]])
            for hf in range(2):
                slot, view, pp = wslots[6 + hf], views[hf], slots[hf]

                def fno2(slot=slot, view=view):
                    ins = None
                    for g in range(4):
                        for k in range(4, 8):
                            ins = nc.tensor.matmul(view[:, g, :], lhsT=RING[:, slot, k, g * 128:(g + 1) * 128], rhs=YB[:, ys, k, 0:n],
                                                   start=False, stop=(k == 7), skip_group_check=True)
                    return ins
                S.op(PE, fno2, reads=[RINGT[slot], YTB, T("YBh0"), PPT[pp]], writes=[PPT[pp]])
                after_use()
                act_copy(SQ[:, hf * 4:hf * 4 + 4, 0:n], view, [PPT[pp]], [T("SQ")], func=AF.Square)
                for g in range(4):
                    kk = hf * 4 + g
                    act_copy(OG[:, kk, 0:n], view[:, g, :], [PPT[pp], T("GPOST")], [T("OG")], scale=GPOST[:, l, kk:kk + 1])

        cbox = {}

        def phase_C2a():
            cbox["ctx"] = stats_rstd_a(1024.0, [T("SQ")], n)

        def phase_C2b():
            psr2 = stats_rstd_b(cbox["ctx"])
            rb2 = PS[:, psr2, 0:n].unsqueeze(1).broadcast_to([128, 8, n])
            dve_tt(OG[:, :, 0:n], OG[:, :, 0:n], rb2, ALU.mult, [T("OG"), PST[psr2]], [T("OG")])
            dve_tt(X[:, :, xc:xc + n], X[:, :, xc:xc + n], OG[:, :, 0:n], ALU.add, [xt, T("OG")], [xt])

        def phase_C2():
            phase_C2a()
            phase_C2b()

        st.Aa, st.Ab, st.C2a, st.C2b = phase_Aa, phase_Ab, phase_C2a, phase_C2b
        st.A, st.B1, st.B2, st.C1, st.C2 = phase_A, phase_B1, phase_B2, phase_C1, phase_C2
        return st

    passes = [
        [Blk("s", 1024, 64, 8, 8, sh=0), Blk("s", 1088, 64, 8, 8, sh=1)] +
        [Blk("p", 256 * b, 256, 1, 256, tok0=256 * b) for b in range(4)],
        [Blk("p", 256 * b, 256, 1, 256, tok0=1024 + 256 * b) for b in range(4)],
    ]
    chunk_no = 0
    for ps_, blocks in enumerate(passes):
        j = 0
        if ps_ == 0:
            for tt in range(8):
                load_x_tile(xp[tt * 128:(tt + 1) * 128, :], tt * 128, j, [XT((tt // 2) * 256)])
                j += 1
            load_x_tile(xs[:, :], 1024, j, [XT(1024), XT(1088)])
            j += 1
            late_consts()
            load_params(0)
        else:
            for tt in range(8):
                load_x_tile(xp[1024 + tt * 128:1024 + (tt + 1) * 128, :], tt * 128, j, [XT((tt // 2) * 256)])
                j += 1
        steps = []
        for l in range(4):
            wslots = [(chunk_no + c) % NSLOT for c in range(8)]
            for bi, blk in enumerate(blocks):
                stp = make_step(l, blk, wslots, bi == len(blocks) - 1)
                stp.pre = (lambda l=l: load_params((l + 1) % 4)) if (bi == 0 and not (ps_ == 1 and l == 3)) else None
                if ps_ == 0 and l == 0:
                    stp.pre = (lambda: (sgu_setup(), load_params(1))) if bi == 2 else None
                steps.append(stp)
            chunk_no += 8
        if PIPELINE == 4:
            steps[0].A()
            for si, stp in enumerate(steps):
                if stp.pre is not None:
                    stp.pre()
                prev = steps[si - 1] if si > 0 else None
                nx = steps[si + 1] if si + 1 < len(steps) else None
                if prev is not None:
                    hookq.append(prev.C2a)
                    if nx is not None:
                        hookq.append(lambda prev=prev, nx=nx: (prev.C2b(), nx.Aa()))
                        hookq.append(nx.Ab)
                    else:
                        hookq.append(prev.C2b)
                elif nx is not None:
                    hookq.append(nx.Aa)
                    hookq.append(nx.Ab)
                stp.B1()
                stp.B2()
                while hookq:
                    hookq.pop(0)()
                stp.C1()
            steps[-1].C2()
        elif PIPELINE == 2:
            steps[0].A()
            for si, stp in enumerate(steps):
                if stp.pre is not None:
                    stp.pre()
                stp.B1()
                if si > 0:
                    steps[si - 1].C2()
                if si + 1 < len(steps):
                    steps[si + 1].A()
                stp.B2()
                stp.C1()
            steps[-1].C2()
        elif PIPELINE == 3:
            for si, stp in enumerate(steps):
                if stp.pre is not None:
                    stp.pre()
                stp.A()
                stp.B1()
                if si > 0:
                    steps[si - 1].C2()
                stp.B2()
                stp.C1()
            steps[-1].C2()
        elif PIPELINE == 1:
            steps[0].A()
            for si, stp in enumerate(steps):
                if stp.pre is not None:
                    stp.pre()
                stp.B1()
                if si + 1 < len(steps):
                    steps[si + 1].A()
                stp.B2()
                stp.C1()
                stp.C2()
        else:
            for si, stp in enumerate(steps):
                if stp.pre is not None:
                    stp.pre()
                stp.A()
                stp.B1()
                stp.B2()
                stp.C1()
                stp.C2()
        j = 0
        if ps_ == 0:
            for tt in range(8):
                store_x_tile(y_p[tt * 128:(tt + 1) * 128, :], tt * 128, j, [XT((tt // 2) * 256)])
                j += 1
            store_x_tile(y_s[:, :], 1024, j, [XT(1024), XT(1088)])
        else:
            for tt in range(8):
                store_x_tile(y_p[1024 + tt * 128:1024 + (tt + 1) * 128, :], tt * 128, j, [XT((tt // 2) * 256)])
                j += 1
    print('SBUF bytes remaining per partition:', nc.sbuf_bytes_remaining)
    S.finish()
    es.close()
    return nc


_CACHE = {}


def kernel(x_prompt, x_sample, state_conv_a, state_pool_b, state_conv_d,
           norm_pre, norm_post, w_in_even, w_out_even, conv_a_w, pool_w, pool_scale,
           w_in_odd, w_out_odd, sgu_ln_g, sgu_ln_b, sgu_w, sgu_b,
           conf_dw_w, conf_dw_b, conf_ln_g, conf_ln_b):
    f = lambda a: np.ascontiguousarray(np.asarray(a, dtype=np.float32))
    x_prompt, x_sample = f(x_prompt), f(x_sample)
    state_conv_a, state_pool_b, state_conv_d = f(state_conv_a), f(state_pool_b), f(state_conv_d)
    shared = dict(
        norm_pre=f(norm_pre), norm_post=f(norm_post), w_in_even=f(w_in_even), w_out_even=f(w_out_even),
        conv_a_w=f(conv_a_w), pool_w=f(pool_w), pool_scale=f(pool_scale), w_in_odd=f(w_in_odd),
        w_out_odd=f(w_out_odd), sgu_ln_g=f(sgu_ln_g), sgu_ln_b=f(sgu_ln_b), sgu_w=f(sgu_w), sgu_b=f(sgu_b),
        conf_dw_w=f(conf_dw_w), conf_dw_b=f(conf_dw_b), conf_ln_g=f(conf_ln_g), conf_ln_b=f(conf_ln_b),
    )
    shared["c_ident"] = np.eye(128, dtype=np.float32)
    shared["c_tril"] = np.tril(np.ones((128, 128), np.float32))
    jj = np.arange(64)
    shared["c_bmask"] = ((jj[:, None] // 8 == jj[None, :] // 8) & (jj[:, None] % 8 <= jj[None, :] % 8)).astype(np.float32)
    npre, npost = shared.pop("norm_pre"), shared.pop("norm_post")
    caw, psc = shared.pop("conv_a_w"), shared.pop("pool_scale")
    cdw, cdb = shared.pop("conf_dw_w"), shared.pop("conf_dw_b")
    clg, clb = shared.pop("conf_ln_g"), shared.pop("conf_ln_b")
    c_ = np.ascontiguousarray
    shared["p_gpre"] = c_(npre.reshape(4, 8, 128).transpose(2, 0, 1))
    shared["p_gpost"] = c_(npost.reshape(4, 8, 128).transpose(2, 0, 1))
    shared["p_caw"] = c_(caw.reshape(2, 3, 4, 128).transpose(3, 0, 1, 2))
    shared["p_psc"] = c_(psc.reshape(2, 4, 128).transpose(2, 0, 1))
    shared["p_cdb"] = c_(cdb.reshape(2, 4, 128).transpose(2, 0, 1))
    shared["p_clg"] = c_(clg.reshape(2, 4, 128).transpose(2, 0, 1))
    shared["p_clb"] = c_(clb.reshape(2, 4, 128).transpose(2, 0, 1))
    wpad = np.zeros((2, 32, 512), np.float32)
    wpad[:, 0:31, :] = cdw
    shared["p_cdw4"] = c_(wpad.reshape(2, 8, 4, 16, 32).transpose(2, 4, 0, 1, 3).reshape(128, 2, 8, 16))
    sw = shared["sgu_w"]
    wsf = np.zeros((2, 64, 4, 64), np.float32)
    for s_ in range(8):
        wsf[:, 8 * s_:8 * s_ + 8, :, 8 * s_:8 * s_ + 8] = sw[:, :, 0:8, 0:8].transpose(0, 3, 1, 2)
    shared["p_wsf"] = wsf
    shared["p_bsrows"] = c_(np.tile(shared["sgu_b"][:, :, 0:8], (1, 1, 8)))
    shared["c_idrep"] = np.tile(np.eye(32, dtype=np.float32), (4, 1))
    sel = np.zeros((4, 4, 128), np.float32)
    for g in range(4):
        sel[g, g, :] = 1.0
    shared["c_sel"] = sel
    shared["c_invc"] = np.tile((1.0 / np.arange(1, 17, dtype=np.float32))[None, :], (128, 1)).astype(np.float32)
    in_maps = []
    for c in range(NCORES):
        m = dict(shared)
        m["xp"] = x_prompt[c]
        m["xs"] = np.ascontiguousarray(x_sample[16 * c:16 * c + 16].reshape(128, 1024))
        m["st_a"] = np.ascontiguousarray(state_conv_a[:, 16 * c:16 * c + 16])
        m["st_b"] = np.ascontiguousarray(state_pool_b[:, 16 * c:16 * c + 16])
        m["st_d"] = np.ascontiguousarray(state_conv_d[:, 16 * c:16 * c + 16])
        in_maps.append(m)
    if "nc" not in _CACHE:
        _CACHE["nc"] = build_program()
    res = run_bass_kernel_spmd(_CACHE["nc"], in_maps, core_ids=list(range(NCORES)))
    R = res.results
    y_prompt = np.stack([R[c]["y_p"] for c in range(NCORES)], 0)
    y_sample = np.concatenate([R[c]["y_s"].reshape(16, 8, 1024) for c in range(NCORES)], 0)
    ca_p = np.stack([R[c]["ca_p"] for c in range(NCORES)], 1)
    ca_s = np.concatenate([R[c]["ca_s"] for c in range(NCORES)], 1)
    pb_p = np.stack([R[c]["pb_p"] for c in range(NCORES)], 1)
    pb_s = np.concatenate([R[c]["pb_s"] for c in range(NCORES)], 1)
    cd_p = np.stack([R[c]["cd_p"] for c in range(NCORES)], 1)
    cd_s = np.concatenate([R[c]["cd_s"] for c in range(NCORES)], 1)
    v_s = np.concatenate([R[c]["v_s"] for c in range(NCORES)], 1)
    return (y_prompt, y_sample, ca_p, ca_s, pb_p, pb_s, cd_p, cd_s, v_s)
```
